# Optimizing a Trainium2 kernel written in Bass

```python
import jax, jax.numpy as jnp
from jax import lax
import numpy as np

D_MODEL = 1024
BATCH = 16
SEQ = 2048
DEPTH = 4

CTX_LEN = 256
GRID_W = 64
HEAD_DIM = 64
ATTN_Q_HEADS = D_MODEL // 128
ATTN_KV_HEADS = ATTN_Q_HEADS // 4
ATTN_GROUP = ATTN_Q_HEADS // ATTN_KV_HEADS
MLSTM_HEADS = D_MODEL // 256
MLSTM_DIM = 64
CMLP_GROUPS = D_MODEL // 256
CMLP_DIM = 64
CMLP_CHUNK = 128
Q_BLOCK = 128
MLSTM_CHUNK = 64
D_FF = 2816
CONV_W = 3
ROPE_THETA = 10000.0
EPS = 1e-6
ATTN_W = ATTN_Q_HEADS * HEAD_DIM
KV_W = ATTN_KV_HEADS * HEAD_DIM
MLSTM_W = MLSTM_HEADS * MLSTM_DIM
CMLP_W = CMLP_GROUPS * CMLP_DIM
D_MIX = ATTN_W + MLSTM_W + CMLP_W
N_GATES = 4 * MLSTM_HEADS
IN_SPLITS = (ATTN_W, KV_W, KV_W, MLSTM_W, MLSTM_W, MLSTM_W, MLSTM_W, N_GATES, CMLP_W, CMLP_W)
D_IN = ATTN_W + 2 * KV_W + 4 * MLSTM_W + N_GATES + 2 * CMLP_W
GATE_OFF = ATTN_W + 2 * KV_W + 4 * MLSTM_W

kernel_name = "hybrid_attn_mlstm_gmlp_dit_block"


def rmsnorm(x, g):
    xf = x.astype(jnp.float32)
    y = xf * lax.rsqrt(jnp.mean(xf * xf, axis=-1, keepdims=True) + EPS)
    return (y * g.astype(jnp.float32)).astype(x.dtype)


def axial_rope_tables(n):
    rows = n // GRID_W
    t = jnp.arange(n)
    row = jnp.repeat(jnp.arange(rows), GRID_W).astype(jnp.float32)
    col = (t % GRID_W).astype(jnp.float32)
    nf = HEAD_DIM // 4
    inv = ROPE_THETA ** (-jnp.arange(nf, dtype=jnp.float32) / nf)
    ang = jnp.concatenate([row[:, None] * inv[None], col[:, None] * inv[None]], axis=-1)
    return jnp.cos(ang), jnp.sin(ang)


def apply_rope(x, cos, sin):
    n = x.shape[1]
    nf = HEAD_DIM // 4
    xf = x.astype(jnp.float32)
    xr = xf.reshape(*xf.shape[:-1], 2, 2, nf)
    x1, x2 = xr[..., 0, :], xr[..., 1, :]
    c = cos.reshape(n, 2, nf)[None, :, None]
    s = sin.reshape(n, 2, nf)[None, :, None]
    out = jnp.stack([x1 * c - x2 * s, x2 * c + x1 * s], axis=-2)
    return out.reshape(xf.shape).astype(x.dtype)


def gqa_attend(q, k, v):
    s = jnp.einsum('bqkgd,bskd->bkgqs', q, k, preferred_element_type=jnp.float32) * (HEAD_DIM ** -0.5)
    p = jax.nn.softmax(s, axis=-1).astype(v.dtype)
    return jnp.einsum('bkgqs,bskd->bqkgd', p, v)


def latent_attention(q, k_all, v_all):
    B, S = q.shape[0], q.shape[1]
    nb = S // Q_BLOCK
    qb = jnp.moveaxis(q.reshape(B, nb, Q_BLOCK, *q.shape[2:]), 1, 0)
    ob = lax.map(lambda blk: gqa_attend(blk, k_all, v_all), qb)
    return jnp.moveaxis(ob, 0, 1).reshape(B, S, -1)


def mlstm_scan(q, k, v, ig, fg, state):
    B, H, N, d = q.shape
    L = MLSTM_CHUNK
    nc = N // L
    chunk = lambda t: jnp.moveaxis(t.reshape(B, H, nc, L, *t.shape[3:]), 2, 0)
    tril = jnp.tril(jnp.ones((L, L), dtype=bool))

    def body(carry, inp):
        C, n, m = carry
        qc, kc, vc, ic, fc = inp
        b = jnp.cumsum(jax.nn.log_sigmoid(fc), axis=-1)
        Dm = jnp.where(tril, b[..., :, None] - b[..., None, :] + ic[..., None, :], -jnp.inf)
        m_inter = b + m[..., None]
        m_t = jnp.maximum(m_inter, jnp.max(Dm, axis=-1))
        w_inter = jnp.exp(m_inter - m_t)
        A = jnp.exp(Dm - m_t[..., None]) * jnp.einsum('bhtd,bhsd->bhts', qc, kc)
        num = w_inter[..., None] * jnp.einsum('bhtd,bhde->bhte', qc, C) + jnp.einsum('bhts,bhse->bhte', A, vc)
        den = w_inter * jnp.einsum('bhtd,bhd->bht', qc, n) + jnp.sum(A, axis=-1)
        h = num / jnp.maximum(jnp.abs(den), jnp.exp(-m_t))[..., None]
        m_new = m_t[..., -1]
        g_s = jnp.exp(b[..., -1:] - b + ic - m_new[..., None])
        w_c = jnp.exp(b[..., -1] + m - m_new)
        C_new = w_c[..., None, None] * C + jnp.einsum('bhs,bhsd,bhse->bhde', g_s, kc, vc)
        n_new = w_c[..., None] * n + jnp.einsum('bhs,bhsd->bhd', g_s, kc)
        return (C_new, n_new, m_new), h

    state_out, hs = lax.scan(body, state, (chunk(q), chunk(k), chunk(v), chunk(ig), chunk(fg)))
    return jnp.moveaxis(hs, 0, 2).reshape(B, H, N, d), state_out


def mlstm_dir(q, k, v, ig, fg, state, reverse):
    if reverse:
        h, st = mlstm_scan(jnp.flip(q, 2), jnp.flip(k, 2), jnp.flip(v, 2),
                           jnp.flip(ig, -1), jnp.flip(fg, -1), state)
        return jnp.flip(h, 2), st
    return mlstm_scan(q, k, v, ig, fg, state)


def mlstm_merge(h_f, h_b, o, g_mh):
    B, H, N, d = h_f.shape
    hs = jnp.transpose(h_f + h_b, (0, 2, 1, 3))
    hs = rmsnorm(hs, g_mh.reshape(H, d)).reshape(B, N, H * d)
    return (jax.nn.sigmoid(o.astype(jnp.float32)) * hs).astype(o.dtype)


def chunk_mlp(u, v, g_v, w_sp, b_sp):
    B, N, _ = u.shape
    nc = N // CMLP_CHUNK
    u = jax.nn.gelu(u)
    v = rmsnorm(jax.nn.gelu(v).reshape(B, N, CMLP_GROUPS, CMLP_DIM), g_v.reshape(CMLP_GROUPS, CMLP_DIM))
    vb = v.reshape(B, nc, CMLP_CHUNK, CMLP_GROUPS, CMLP_DIM)
    z = jnp.einsum('gpq,bcqgd->bcpgd', w_sp, vb) + jnp.transpose(b_sp)[None, None, :, :, None]
    return u * z.reshape(B, N, CMLP_W)


def conv_ffn(h, w_up, conv_w, conv_b, w_down):
    a = h @ w_up
    n = a.shape[1]
    ap = jnp.pad(a, ((0, 0), (1, 1), (0, 0)))
    a = ap[:, 0:n] * conv_w[0] + ap[:, 1:n + 1] * conv_w[1] + ap[:, 2:n + 2] * conv_w[2] + conv_b
    gate, val = jnp.split(a, 2, axis=-1)
    return (jax.nn.silu(gate) * val) @ w_down


def token_mixers(h, hc, w_in, b_in, g_q, g_k, g_mh, g_v, w_sp, b_sp, need_ctx):
    B, S, _ = h.shape
    T = hc.shape[1]
    split_at = [int(s) for s in np.cumsum(IN_SPLITS)[:-1]]
    qa, ka, va, qm, km, vm, om, gt, uc, vc = jnp.split(h @ w_in + b_in, split_at, axis=-1)
    qa_c, ka_c, va_c, qm_c, km_c, vm_c, om_c, gt_c, uc_c, vc_c = jnp.split(hc @ w_in + b_in, split_at, axis=-1)

    cos, sin = axial_rope_tables(S)
    q_l = apply_rope(rmsnorm(qa.reshape(B, S, ATTN_Q_HEADS, HEAD_DIM), g_q), cos, sin)
    k_l = apply_rope(rmsnorm(ka.reshape(B, S, ATTN_KV_HEADS, HEAD_DIM), g_k), cos, sin)
    k_c = rmsnorm(ka_c.reshape(B, T, ATTN_KV_HEADS, HEAD_DIM), g_k)
    v_l = va.reshape(B, S, ATTN_KV_HEADS, HEAD_DIM)
    v_c = va_c.reshape(B, T, ATTN_KV_HEADS, HEAD_DIM)
    k_all = jnp.concatenate([k_c, k_l], axis=1)
    v_all = jnp.concatenate([v_c, v_l], axis=1)
    attn_l = latent_attention(q_l.reshape(B, S, ATTN_KV_HEADS, ATTN_GROUP, HEAD_DIM), k_all, v_all)

    def heads(t, n):
        return jnp.transpose(t.reshape(B, n, MLSTM_HEADS, MLSTM_DIM), (0, 2, 1, 3)).astype(jnp.float32)

    def gates(t, n):
        return jnp.transpose(t.reshape(B, n, 4, MLSTM_HEADS), (2, 0, 3, 1)).astype(jnp.float32)

    kscale = MLSTM_DIM ** -0.5
    q_mc, k_mc, v_mc, g_c = heads(qm_c, T), heads(km_c, T) * kscale, heads(vm_c, T), gates(gt_c, T)
    q_ml, k_ml, v_ml, g_l = heads(qm, S), heads(km, S) * kscale, heads(vm, S), gates(gt, S)
    zero = (jnp.zeros((B, MLSTM_HEADS, MLSTM_DIM, MLSTM_DIM), jnp.float32),
            jnp.zeros((B, MLSTM_HEADS, MLSTM_DIM), jnp.float32),
            jnp.zeros((B, MLSTM_HEADS), jnp.float32))
    hf_c, st_f = mlstm_dir(q_mc, k_mc, v_mc, g_c[0], g_c[1], zero, False)
    hb_c, st_b = mlstm_dir(q_mc, k_mc, v_mc, g_c[2], g_c[3], zero, True)
    hf_l, _ = mlstm_dir(q_ml, k_ml, v_ml, g_l[0], g_l[1], st_f, False)
    hb_l, _ = mlstm_dir(q_ml, k_ml, v_ml, g_l[2], g_l[3], st_b, True)
    mlstm_l = mlstm_merge(hf_l, hb_l, om, g_mh)

    cmlp_l = chunk_mlp(uc, vc, g_v, w_sp, b_sp)

    mix_l = jnp.concatenate([attn_l, mlstm_l, cmlp_l], axis=-1)
    if not need_ctx:
        return mix_l, None
    q_c = rmsnorm(qa_c.reshape(B, T, ATTN_Q_HEADS, HEAD_DIM), g_q)
    attn_c = gqa_attend(q_c.reshape(B, T, ATTN_KV_HEADS, ATTN_GROUP, HEAD_DIM), k_c, v_c).reshape(B, T, ATTN_W)
    mlstm_c = mlstm_merge(hf_c, hb_c, om_c, g_mh)
    cmlp_c = chunk_mlp(uc_c, vc_c, g_v, w_sp, b_sp)
    mix_c = jnp.concatenate([attn_c, mlstm_c, cmlp_c], axis=-1)
    return mix_l, mix_c


def setup_inputs(seed: int = 0) -> dict:
    key = jax.random.key(seed)
    ks = jax.random.split(key, 24)
    nrm = lambda k, shape, s: jax.random.normal(k, shape, jnp.float32) * s
    forget_off = jnp.zeros((D_IN,), jnp.float32)
    fbias = jnp.linspace(3.0, 6.0, MLSTM_HEADS)
    forget_off = forget_off.at[GATE_OFF + MLSTM_HEADS:GATE_OFF + 2 * MLSTM_HEADS].set(fbias)
    forget_off = forget_off.at[GATE_OFF + 3 * MLSTM_HEADS:GATE_OFF + 4 * MLSTM_HEADS].set(fbias)
    return {
        "x": nrm(ks[0], (BATCH, SEQ, D_MODEL), 1.0),
        "c": nrm(ks[1], (BATCH, D_MODEL), 1.0),
        "ctx": nrm(ks[2], (BATCH, CTX_LEN, D_MODEL), 1.0),
        "c_ctx": nrm(ks[3], (D_MODEL,), 1.0),
        "w_ada": nrm(ks[4], (DEPTH, D_MODEL, 6 * D_MODEL), 0.5 * D_MODEL ** -0.5),
        "b_ada": nrm(ks[5], (DEPTH, 6 * D_MODEL), 0.02),
        "g_norm1": 1.0 + nrm(ks[6], (DEPTH, D_MODEL), 0.02),
        "w_in": nrm(ks[7], (DEPTH, D_MODEL, D_IN), D_MODEL ** -0.5),
        "b_in": nrm(ks[8], (DEPTH, D_IN), 0.02) + forget_off[None],
        "g_q": 1.0 + nrm(ks[9], (DEPTH, HEAD_DIM), 0.02),
        "g_k": 1.0 + nrm(ks[10], (DEPTH, HEAD_DIM), 0.02),
        "g_mh": 1.0 + nrm(ks[11], (DEPTH, MLSTM_W), 0.02),
        "g_v": 1.0 + nrm(ks[12], (DEPTH, CMLP_W), 0.02),
        "w_sp": nrm(ks[13], (DEPTH, CMLP_GROUPS, CMLP_CHUNK, CMLP_CHUNK), 0.5 * CMLP_CHUNK ** -0.5),
        "b_sp": 1.0 + nrm(ks[14], (DEPTH, CMLP_GROUPS, CMLP_CHUNK), 0.02),
        "w_out": nrm(ks[15], (DEPTH, D_MIX, D_MODEL), D_MIX ** -0.5),
        "g_norm2": 1.0 + nrm(ks[16], (DEPTH, D_MODEL), 0.02),
        "w_up": nrm(ks[17], (DEPTH, D_MODEL, 2 * D_FF), D_MODEL ** -0.5),
        "conv_w": nrm(ks[18], (DEPTH, CONV_W, 2 * D_FF), CONV_W ** -0.5),
        "conv_b": nrm(ks[19], (DEPTH, 2 * D_FF), 0.02),
        "w_down": nrm(ks[20], (DEPTH, D_FF, D_MODEL), D_FF ** -0.5),
    }


def reference(x, c, ctx, c_ctx, w_ada, b_ada, g_norm1, w_in, b_in, g_q, g_k, g_mh, g_v,
              w_sp, b_sp, w_out, g_norm2, w_up, conv_w, conv_b, w_down):
    sc = jax.nn.silu(c)
    scc = jax.nn.silu(c_ctx)
    xc = ctx
    for l in range(DEPTH):
        last = l == DEPTH - 1
        mod = (sc @ w_ada[l] + b_ada[l])[:, None, :]
        mod_c = (scc @ w_ada[l] + b_ada[l])[None, None, :]
        sh1, s1, g1, sh2, s2, g2 = jnp.split(mod, 6, axis=-1)
        sh1c, s1c, g1c, sh2c, s2c, g2c = jnp.split(mod_c, 6, axis=-1)
        h = rmsnorm(x, g_norm1[l]) * (1 + s1) + sh1
        hc = rmsnorm(xc, g_norm1[l]) * (1 + s1c) + sh1c
        mix, mix_c = token_mixers(h, hc, w_in[l], b_in[l], g_q[l], g_k[l], g_mh[l], g_v[l],
                                  w_sp[l], b_sp[l], not last)
        x = x + g1 * (mix @ w_out[l])
        h2 = rmsnorm(x, g_norm2[l]) * (1 + s2) + sh2
        x = x + g2 * conv_ffn(h2, w_up[l], conv_w[l], conv_b[l], w_down[l])
        if not last:
            xc = xc + g1c * (mix_c @ w_out[l])
            h2c = rmsnorm(xc, g_norm2[l]) * (1 + s2c) + sh2c
            xc = xc + g2c * conv_ffn(h2c, w_up[l], conv_w[l], conv_b[l], w_down[l])
    return x
```

```python
import contextlib
import numpy as np
import concourse.bass as bass
import concourse.mybir as mybir
from concourse.bass_utils import run_bass_kernel_spmd

F32 = mybir.dt.float32
BF16 = mybir.dt.bfloat16
AF = mybir.ActivationFunctionType
ALU = mybir.AluOpType
AX = mybir.AxisListType

D = 1024
S_LAT = 2048
T_CTX = 256
NT = 2304
DEPTH = 4
DFF = 2816
HD = 64
EPS = 1e-6
NTM = 1424
EW = 66
HBW = 2308
TT = [(0, 256), (256, 512), (768, 512), (1280, 512), (1792, 512)]
LN8 = float(np.log(0.125))


def hcol(t):
    return t + 1 if t < 256 else t + 3


class Tok:
    __slots__ = ("eng", "seq", "clk", "needed", "sem", "val")

    def __init__(self, eng, seq, clk):
        self.eng = eng
        self.seq = seq
        self.clk = clk
        self.needed = False
        self.sem = None
        self.val = 0


class Res:
    __slots__ = ("w", "r", "excl")

    def __init__(self, excl=False):
        self.w = None
        self.r = {}
        self.excl = excl


class Sched:
    ENGS = ("pe", "act", "dve", "pool", "sp")

    def __init__(self, n_dma_sems=8):
        self.q = {e: [] for e in self.ENGS}
        self.clock = {e: {} for e in self.ENGS}
        self.seq = {e: 0 for e in self.ENGS}
        self.toks = {e: [] for e in self.ENGS}
        self.n_dma = n_dma_sems
        self.dma_rr = {e: 0 for e in self.ENGS}
        self.dma_last = {}

    def _merge(self, clk, t):
        for k, v in t.clk.items():
            if clk.get(k, 0) < v:
                clk[k] = v
        if clk.get(t.eng, 0) < t.seq:
            clk[t.eng] = t.seq

    def _deps(self, eng, reads, writes):
        clk = self.clock[eng]
        waits = []
        cand = []
        for r in reads:
            if r.w is not None:
                cand.append(r.w)
        for w in writes:
            if w.w is not None:
                cand.append(w.w)
            cand.extend(w.r.values())
        for t in cand:
            if eng == "pe" and t.eng == "pe":
                continue
            if clk.get(t.eng, 0) < t.seq:
                waits.append(t)
                t.needed = True
                self._merge(clk, t)
        return waits

    def _mark(self, tok, reads, writes):
        for r in reads:
            r.r[tok.eng] = tok
        for w in writes:
            w.w = tok
            w.r = {}

    def op(self, eng, fn, reads=(), writes=()):
        ex = [r for r in reads if r.excl]
        if ex:
            writes = list(writes) + ex
        waits = self._deps(eng, reads, writes)
        self.seq[eng] += 1
        tok = Tok(eng, self.seq[eng], dict(self.clock[eng]))
        self.toks[eng].append(tok)
        self.q[eng].append((waits, fn, tok, False))
        self._mark(tok, reads, writes)
        return tok

    def dma(self, eng, fn, reads=(), writes=()):
        waits = self._deps(eng, reads, writes)
        j = self.dma_rr[eng]
        self.dma_rr[eng] = (j + 1) % self.n_dma
        key = ("dma", eng, j)
        last = self.dma_last.get(key)
        clk = self.clock[eng]
        if last is not None and clk.get(key, 0) < last.seq:
            waits.append(last)
            self._merge(clk, last)
        seq = (last.seq if last is not None else 0) + 1
        tok = Tok(key, seq, dict(clk))
        tok.needed = True
        self.dma_last[key] = tok
        self.q[eng].append((waits, fn, tok, True))
        self._mark(tok, reads, writes)
        return tok

    def barrier(self):
        lasts = [self.toks[e][-1] for e in self.ENGS if self.toks[e]]
        lasts += list(self.dma_last.values())
        for e in self.ENGS:
            clk = self.clock[e]
            waits = []
            for t in lasts:
                if e == "pe" and t.eng == "pe":
                    continue
                if clk.get(t.eng, 0) < t.seq:
                    waits.append(t)
                    t.needed = True
                    self._merge(clk, t)
            if waits:
                self.q[e].append((waits, None, None, False))

    def emit(self, nc, stack):
        esem = {}
        for e in self.ENGS:
            esem[e] = stack.enter_context(nc.semaphore("s_" + e))
            c = 0
            for t in self.toks[e]:
                if t.needed:
                    c += 1
                t.val = c
                t.sem = esem[e]
        dsem = {}
        for key in self.dma_last:
            dsem[key] = stack.enter_context(nc.semaphore("d_%s_%d" % (key[1], key[2])))
        block = stack.enter_context(nc.Block())
        q = self.q

        def run(e, engobj):
            for (waits, fn, tok, is_dma) in q[e]:
                for t in waits:
                    if isinstance(t.eng, tuple):
                        engobj.wait_ge(dsem[t.eng], 16 * t.seq)
                    else:
                        engobj.wait_ge(t.sem, t.val)
                if fn is None:
                    continue
                ins = fn(engobj)
                if is_dma:
                    ins.then_inc(dsem[tok.eng], 16)
                elif tok.needed:
                    ins.then_inc(tok.sem, 1)

        @block.tensor
        def _(e):
            run("pe", e)

        @block.scalar
        def _(e):
            run("act", e)

        @block.vector
        def _(e):
            run("dve", e)

        @block.gpsimd
        def _(e):
            run("pool", e)

        @block.sync
        def _(e):
            run("sp", e)


def build(nlayers=DEPTH, nb=2, dbg=(), stop=99):
    nc = bass.Bass("TRN2", target_bir_lowering=False)
    S = Sched()
    st = contextlib.ExitStack()

    def din(name, shape, dt=F32):
        return nc.dram_tensor(name, list(shape), dt, kind="ExternalInput").ap()

    def dscr(name, shape, dt):
        return nc.dram_tensor(name, list(shape), dt, kind="ExternalOutput").ap()

    xin = din("xin", [2, 128, 8, NT])
    cvec = din("cvec", [128, 8, 3])
    wada = din("wada", [DEPTH, 12, 128, 8, 512])
    bada = din("bada", [128, DEPTH, 48])
    gn12 = din("gn12", [128, DEPTH, 2, 8])
    winfm = din("winfm", [DEPTH, 7, 128, 8, 256])
    binfm = din("binfm", [128, DEPTH, 14])
    wintm = din("wintm", [DEPTH, 128, 8, NTM])
    bintm = din("bintm", [1, DEPTH, NTM])
    gqk = din("gqk", [128, DEPTH, 4])
    rope = din("rope", [128, 2, NT])
    gmv = din("gmv", [DEPTH, 2, 256])
    wsp = din("wsp", [DEPTH, 128, 4, 128])
    bsp = din("bsp", [128, DEPTH, 4])
    wout = din("wout", [DEPTH, 128, 8, 1024])
    wup = din("wup", [DEPTH, 2, 128, 8, 2816])
    convp = din("convp", [128, DEPTH, 2, 22, 4])
    wdown = din("wdown", [DEPTH, 2, 128, 11, 1024])
    consts = din("consts", [128, 8, 128])
    yout = nc.dram_tensor("yout", [2, 128, 8, S_LAT], F32, kind="ExternalOutput").ap()
    if dbg:
        xd = nc.dram_tensor("xd", [2, 128, 8, NT], F32, kind="ExternalOutput").ap()
        xm_dbg = nc.dram_tensor("xm_dbg", [2, 128, 8, NT], F32, kind="ExternalOutput").ap()
        mix_dbg = nc.dram_tensor("mix_dbg", [2, 128, 8, HBW], BF16, kind="ExternalOutput").ap()
    else:
        xd = dscr("xd", [2, 128, 8, NT], F32)
    vmad = dscr("vmad", [18, 128, (4 * EW)], BF16)
    omsd = dscr("omsd", [18, 128, 256], BF16)
    cmd = dscr("cmd", [18, 128, 256], BF16)
    dbg_out = {}

    def sb(name, shape, dt=F32):
        return st.enter_context(nc.sbuf_tensor(name, list(shape), dt))

    def psb(name):
        return st.enter_context(nc.psum_tensor(name, [128, 512], F32))

    def mm(out, lhsT, rhs, start, stop, rd, wr):
        S.op("pe", lambda e, o=out, l=lhsT, r=rhs, a=start, b=stop: e.matmul(o, lhsT=l, rhs=r, start=a, stop=b), rd, wr)

    def act(out, in_, func, rd, wr, bias=None, scale=None):
        kw = {}
        if bias is not None:
            kw["bias"] = bias
        if scale is not None:
            kw["scale"] = scale
        S.op("act", lambda e, o=out, i=in_, f=func, k=kw: e.activation(out=o, in_=i, func=f, **k), rd, wr)

    def tt(eng, out, in0, in1, op, rd, wr):
        S.op(eng, lambda e, o=out, a=in0, b=in1, p=op: e.tensor_tensor(out=o, in0=a, in1=b, op=p), rd, wr)

    def ts(eng, out, in0, s1, op0, rd, wr, s2=None, op1=None):
        if op1 is None:
            S.op(eng, lambda e, o=out, a=in0, x=s1, p=op0: e.tensor_scalar(out=o, in0=a, scalar1=x, scalar2=None, op0=p), rd, wr)
        else:
            S.op(eng, lambda e, o=out, a=in0, x=s1, y=s2, p=op0, q=op1: e.tensor_scalar(out=o, in0=a, scalar1=x, scalar2=y, op0=p, op1=q), rd, wr)

    def stt(eng, out, in0, scalar, in1, op0, op1, rd, wr, tmp=None):
        if eng == "pool":
            t_ = out if tmp is None else tmp
            S.op(eng, lambda e, o=t_, a=in0, x=scalar, p=op0: e.tensor_scalar(out=o, in0=a, scalar1=x, scalar2=None, op0=p), rd, wr)
            S.op(eng, lambda e, o=out, a=t_, b=in1, p=op1: e.tensor_tensor(out=o, in0=a, in1=b, op=p), rd, wr)
            return
        S.op(eng, lambda e, o=out, a=in0, s=scalar, b=in1, p=op0, q=op1: e.scalar_tensor_tensor(out=o, in0=a, scalar=s, in1=b, op0=p, op1=q), rd, wr)

    def cp(eng, out, in_, rd, wr):
        if eng == "act":
            S.op(eng, lambda e, o=out, i=in_: e.activation(out=o, in_=i, func=AF.Identity), rd, wr)
        else:
            S.op(eng, lambda e, o=out, i=in_: e.tensor_copy(out=o, in_=i), rd, wr)

    def recip(out, in_, rd, wr):
        S.op("dve", lambda e, o=out, i=in_: e.reciprocal(out=o, in_=i), rd, wr)

    def mset(eng, ap, val, wr):
        S.op(eng, lambda e, a=ap, v=val: e.memset(a, v), (), wr)

    def dma(eng, out, in_, rd, wr):
        return S.dma(eng, lambda e, o=out, i=in_: e.dma_start(out=o, in_=i), rd, wr)

    CONST = sb("CONST", [128, 8, 128]); rCONST = Res()
    CONSTB = sb("CONSTB", [128, 2, 128], BF16); rCONSTB = Res()
    ONE1 = sb("ONE1", [1, 128]); rONE1 = Res()
    CV = sb("CV", [128, 8, 3]); SCb = sb("SCb", [128, 8, 3], BF16); rCV = Res(); rSCb = Res()
    MOD = sb("MOD", [128, DEPTH, 48, 3]); rMOD = Res()
    BADA = sb("BADA", [128, DEPTH, 48]); rBADA = Res()
    GN = sb("GN", [128, DEPTH, 2, 8]); rGN = Res()
    A12 = sb("A12", [128, DEPTH, 2, 8, 3]); rA12 = Res()
    BFM = sb("BFM", [128, DEPTH, 14]); rBFM = Res()
    BTM = sb("BTM", [1, NTM]); rBTM = Res()
    GQK = sb("GQK", [128, DEPTH, 4]); rGQK = Res()
    BSP = sb("BSP", [128, DEPTH, 4]); rBSP = Res()
    CONVP = sb("CONVP", [128, DEPTH, 2, 22, 4]); rCONVP = Res()
    ROPE = sb("ROPE", [128, 2, NT], BF16); rROPE = Res()
    GMV = sb("GMV", [128, 2, 256]); rGMV = Res()
    WSPb = sb("WSPb", [128, 4, 128], BF16); rWSP = Res()
    HB = sb("HB", [128, 8, HBW], BF16)
    rHB = [[Res() for _ in range(5)] for _ in range(8)]
    XT = [sb("XT0", [128, 8, 512])]; rXT = [Res()]
    rSQ = Res()
    SD = sb("SD", [128, 512]); rSD = Res()
    RS = sb("RS", [128, 512]); rRS = Res()
    WP = [sb("WP%d" % i, [128, 8, 512], BF16) for i in range(2)]; rWP = [Res(), Res()]
    ARENA = sb("ARENA", [128, 48700], BF16)
    PS = [psb("PS%d" % i) for i in range(8)]
    rPS = [Res(excl=True) for _ in range(8)]

    IDENT = CONST[:, 0, :]
    TRIF = CONST[:, 1, :]
    TRIB = CONST[:, 2, :]
    NEG1 = CONST[:, 3, :]
    MASKF = CONST[:, 4, :]
    MASKB = CONST[:, 5, :]
    ONESb = CONSTB[:, 0, :]
    BLKb = CONSTB[:, 1, :]

    class Arena:
        def __init__(self):
            self.off = 0

        def reset(self):
            self.off = 0

        def get(self, shape, dt):
            n = int(np.prod(shape[1:]))
            if dt == F32:
                n2 = 2 * n
            else:
                n2 = n
            o = self.off + (self.off % 2)
            self.off = o + n2 + (n2 % 2)
            assert self.off <= 48700, ("arena overflow", self.off)
            v = ARENA[:, o:o + n2]
            if dt == F32:
                v = v.bitcast(F32)
            v = v[0:shape[0]]
            if len(shape) == 3:
                v = v.rearrange("p (a b) -> p a b", a=shape[1])
            elif len(shape) == 4:
                v = v.rearrange("p (a b c) -> p a b c", a=shape[1], b=shape[2])
            return v

    AR = Arena()
    wp_i = [0]

    def load_w(src_ap, width, kc=8):
        i = wp_i[0] % 2
        wp_i[0] += 1
        dma("pool", WP[i][:, 0:kc, 0:width], src_ap, (), [rWP[i]])
        return WP[i], rWP[i]

    ps_i = [0]

    def psrot(n=4):
        i = ps_i[0] % n
        ps_i[0] += 1
        return PS[i], rPS[i]

    dma("sp", CONST[:], consts[:, :, :], (), [rCONST])
    cp("dve", CONSTB[:], CONST[:, 6:8, :], [rCONST], [rCONSTB])
    mset("dve", ONE1[:], 1.0, [rONE1])
    dma("sp", CV[:], cvec[:, :, :], (), [rCV])
    dma("sp", BADA[:], bada[:, :, :], (), [rBADA])
    dma("sp", GN[:], gn12[:, :, :, :], (), [rGN])
    dma("sp", BFM[:], binfm[:, :, :], (), [rBFM])
    dma("sp", GQK[:], gqk[:, :, :], (), [rGQK])
    dma("sp", BSP[:], bsp[:, :, :], (), [rBSP])
    dma("sp", CONVP[:], convp[:, :, :, :, :], (), [rCONVP])
    dma("pool", ROPE[:], rope[:, :, :], (), [rROPE])
    act(CV[:], CV[:], AF.Silu, [rCV], [rCV])
    cp("dve", SCb[:], CV[:], [rCV], [rSCb])
    for k in range(8):
        mset("pool", HB[:, k, 0:1], 0.0, [rHB[k][0]])
        mset("pool", HB[:, k, 257:259], 0.0, [rHB[k][0]])
        mset("pool", HB[:, k, 2307:2308], 0.0, [rHB[k][4]])

    for l in range(nlayers):
        pm = PS[0][:, 0:144]
        for pc in range(12):
            w, rw = load_w(wada[l, pc], 512)
            for j in range(4):
                jj = pc * 4 + j
                for k in range(8):
                    mm(pm[:, jj * 3:jj * 3 + 3], w[:, k, j * 128:(j + 1) * 128], SCb[:, k, :], k == 0, k == 7,
                       [rw, rSCb], [rPS[0]])
        tt("dve", MOD[:, l, :, :], pm.rearrange("p (a b) -> p a b", b=3),
           BADA[:, l, :].unsqueeze(2).to_broadcast([128, 48, 3]), ALU.add, [rPS[0], rBADA], [rMOD])
        for i in range(2):
            ts("dve", A12[:, l, i, :, :], MOD[:, l, 8 + 24 * i:16 + 24 * i, :], 1.0, ALU.add, [rMOD], [rA12])
            tt("dve", A12[:, l, i, :, :], A12[:, l, i, :, :],
               GN[:, l, i, :].unsqueeze(2).to_broadcast([128, 8, 3]), ALU.mult, [rA12, rGN], [rA12])

    def norm_tile(l, which, m, n, xt, rxt, SQ):
        t0, w = TT[n]
        c0 = hcol(t0)
        act(SQ[:, :, 0:w], xt[:, :, 0:w], AF.Square, [rxt], [rSQ])
        pss, rpss = PS[4], rPS[4]
        for k in range(8):
            mm(pss[:, 0:w], ONESb, SQ[:, k, 0:w], k == 0, k == 7, [rSQ, rCONSTB], [rpss])
        act(SD[:, 0:w], pss[:, 0:w], AF.Sqrt, [rpss], [rSD], bias=EPS, scale=1.0 / D)
        recip(RS[:, 0:w], SD[:, 0:w], [rSD], [rRS])
        tt("dve", xt[:, :, 0:w], xt[:, :, 0:w], RS[:, 0:w].unsqueeze(1).to_broadcast([128, 8, w]), ALU.mult,
           [rxt, rRS], [rxt])
        sh = 0 if which == 0 else 24
        for k in range(8):
            act(HB[:, k, c0:c0 + w], xt[:, k, 0:w], AF.Identity, [rxt, rA12, rMOD], [rHB[k][n]],
                bias=MOD[:, l, sh + k, m:m + 1], scale=A12[:, l, which, k, m:m + 1])

    xt_i = [0]

    def xt_next():
        return XT[0], rXT[0]

    rXD = [[Res() for _ in range(5)] for _ in range(2)]

    def layer(b, l):
        last = (l == DEPTH - 1)
        xsrc = xin if l == 0 else xd
        ctx_m = 2
        dma("sp", GMV[:, 0, :], gmv[l, 0:1, :].partition_broadcast(128), (), [rGMV])
        dma("sp", GMV[:, 1, :], gmv[l, 1:2, :].partition_broadcast(128), (), [rGMV])
        dma("pool", WSPb[:], wsp[l], (), [rWSP])
        dma("sp", BTM[:], bintm[0:1, l, :], (), [rBTM])
        S.barrier()
        AR.reset()
        QK = AR.get([128, 5, NT], BF16); rQK = [[Res() for _ in range(5)] for _ in range(5)]
        QM = AR.get([128, 4, NT], BF16); rQM = [[Res() for _ in range(5)] for _ in range(4)]
        VA = AR.get([128, 18, 256], BF16); rVA = [Res() for _ in range(18)]
        GT = AR.get([128, 18, 16], F32); rGT = [Res() for _ in range(18)]
        DC = AR.get([128, 36, (2 * EW)], BF16); rDC = [Res() for _ in range(36)]
        CBF = AR.get([128, 36, (2 * EW)], BF16); rCBF = [Res() for _ in range(36)]
        SCAL = AR.get([128, 18, 32], F32); rSCAL = [Res() for _ in range(18)]
        SM = AR.get([128, 64], F32); rSM = Res()
        CST = AR.get([128, 2, (2 * EW)], F32); rCST = Res()
        mark_tmp = AR.off
        SQ = AR.get([128, 8, 512], BF16)
        for n in range(5):
            t0, w = TT[n]
            m = ctx_m if n == 0 else b
            xt, rxt = xt_next()
            dma("sp", xt[:, :, 0:w], xsrc[b, :, :, t0:t0 + w], [rXD[b][n]], [rxt])
            norm_tile(l, 0, m, n, xt, rxt, SQ)
        S.barrier()
        AR.off = mark_tmp
        F1 = AR.get([128, 512], F32); rF1 = Res()
        F2 = AR.get([128, 512], F32); rF2 = Res()
        SQH = AR.get([128, 512], BF16); rSQH = Res()
        RSQ = AR.get([128, 512], F32); rRSQ = Res()
        T1 = AR.get([128, 512], F32); rT1 = Res()
        T2 = AR.get([128, 512], F32); rT2 = Res()
        UG = AR.get([128, 512], F32); rUG = Res()
        VCN = AR.get([128, 256], BF16); rVCN = Res()
        KH = AR.get([128, 2, 256], BF16); rKH = Res()
        VMAt2 = AR.get([128, (4 * EW)], BF16); VMAt = VMAt2.rearrange("p (h d) -> p h d", h=4); rVMAt = Res()
        OMSt = AR.get([128, 256], BF16); rOMSt = Res()
        CMt = AR.get([128, 256], BF16); rCMt = Res()

        if stop <= 1:
            return
        for i in range(18):
            mset("pool", VA[:, i, :].rearrange("p (a b) -> p a b", a=2)[:, :, 64:128], 1.0, [rVA[i]])

        def fm_matmul(w, rw, cc, n, pst, rpst):
            t0, wd = TT[n]
            c0 = hcol(t0)
            for k in range(8):
                mm(pst[:, 0:wd], w[:, k, cc * 128:(cc + 1) * 128], HB[:, k, c0:c0 + wd], k == 0, k == 7,
                   [rw, rHB[k][n]], [rpst])

        for pi in range(7):
            w, rw = load_w(winfm[l, pi], 256)
            if pi < 5:
                gcol = 0 if pi < 4 else 2
                for n in range(5):
                    t0, wd = TT[n]
                    pa, rpa = psrot()
                    pb, rpb = psrot()
                    fm_matmul(w, rw, 0, n, pa, rpa)
                    fm_matmul(w, rw, 1, n, pb, rpb)
                    ba = BFM[:, l, 2 * pi:2 * pi + 1]
                    bb = BFM[:, l, 2 * pi + 1:2 * pi + 2]
                    act(F1[:, 0:wd], pa[:, 0:wd], AF.Identity, [rpa, rBFM], [rF1], bias=ba)
                    act(SQH[:, 0:wd], pa[:, 0:wd], AF.Square, [rpa, rBFM], [rSQH], bias=ba)
                    act(F2[:, 0:wd], pb[:, 0:wd], AF.Identity, [rpb, rBFM], [rF2], bias=bb)
                    pss, rpss = PS[4], rPS[4]
                    mm(pss[:, 0:wd], BLKb, SQH[:, 0:wd], True, True, [rSQH, rCONSTB], [rpss])
                    act(RSQ[:, 0:wd], pss[:, 0:wd], AF.Sqrt, [rpss], [rRSQ], bias=EPS, scale=1.0 / HD)
                    recip(RSQ[:, 0:wd], RSQ[:, 0:wd], [rRSQ], [rRSQ])
                    stt("dve", T1[:, 0:wd], F1[:, 0:wd], GQK[:, l, gcol:gcol + 1], ROPE[:, 0, t0:t0 + wd],
                        ALU.mult, ALU.mult, [rF1, rGQK, rROPE], [rT1])
                    stt("dve", T2[:, 0:wd], F2[:, 0:wd], GQK[:, l, gcol + 1:gcol + 2], ROPE[:, 1, t0:t0 + wd],
                        ALU.mult, ALU.mult, [rF2, rGQK, rROPE], [rT2])
                    tt("dve", T1[:, 0:wd], T1[:, 0:wd], T2[:, 0:wd], ALU.add, [rT1, rT2], [rT1])
                    tt("dve", QK[:, pi, t0:t0 + wd], T1[:, 0:wd], RSQ[:, 0:wd], ALU.mult, [rT1, rRSQ], [rQK[pi][n]])
            else:
                for cc in range(2):
                    ci = (pi - 5) * 2 + cc
                    for n in range(5):
                        t0, wd = TT[n]
                        pa, rpa = psrot()
                        fm_matmul(w, rw, cc, n, pa, rpa)
                        act(QM[:, ci, t0:t0 + wd], pa[:, 0:wd], AF.Identity, [rpa, rBFM], [rQM[ci][n]],
                            bias=BFM[:, l, 10 + ci:11 + ci])

        if stop <= 2:
            return
        def tm_matmul(w, rw, off, wd, i, pst, rpst):
            t0 = i * 128
            c0 = hcol(t0)
            n = 0 if i < 2 else 1 + (i - 2) // 4
            for k in range(8):
                mm(pst[:, 0:wd], HB[:, k, c0:c0 + 128], w[:, k, 0:wd], k == 0, False, [rw, rHB[k][n]], [rpst])
            mm(pst[:, 0:wd], ONE1[0:1, :], BTM[0:1, off:off + wd], False, True, [rONE1, rBTM], [rpst])

        w, rw = load_w(wintm[l, :, :, 0:16], 16)
        for i in range(18):
            pa, rpa = psrot()
            tm_matmul(w, rw, 0, 16, i, pa, rpa)
            cp("dve", GT[:, i, :], pa[:, 0:16], [rpa], [rGT[i]])
            gv = GT[:, i, :].rearrange("p (d t h) -> p d t h", d=2, t=2)
            fpre = gv[:, :, 1, :]
            ipre = gv[:, :, 0, :]
            spl = SM[:, 0:8].rearrange("p (d h) -> p d h", d=2)
            act(spl, fpre, AF.Exp, [rGT[i]], [rSM], scale=-1.0)
            act(spl, spl, AF.Ln, [rSM], [rSM], bias=1.0)
            pg, rpg = PS[5], rPS[5]
            mm(pg[:, 0:4], TRIF, SM[:, 0:4], True, True, [rSM, rCONST], [rpg])
            mm(pg[:, 4:8], TRIB, SM[:, 4:8], True, True, [rSM, rCONST], [rpg])
            mm(pg[:, 8:16], NEG1, SM[:, 0:8], True, True, [rSM, rCONST], [rpg])
            a_ = SM[:, 8:16]
            tt("dve", a_.rearrange("p (d h) -> p d h", d=2), ipre, pg[:, 0:8].rearrange("p (d h) -> p d h", d=2),
               ALU.subtract, [rGT[i], rpg], [rSM])
            act(SCAL[:, i, 0:8], a_, AF.Exp, [rSM], [rSCAL[i]], bias=LN8)
            tt("dve", SM[:, 16:24], a_, pg[:, 8:16], ALU.add, [rSM, rpg], [rSM])
            act(SCAL[:, i, 8:16], SM[:, 16:24], AF.Exp, [rSM], [rSCAL[i]], bias=LN8)
            act(SCAL[:, i, 16:24], pg[:, 0:8], AF.Exp, [rpg], [rSCAL[i]], scale=-1.0)
            act(SCAL[:, i, 24:32], pg[:, 8:16], AF.Exp, [rpg], [rSCAL[i]])

        if stop <= 3:
            return
        w, rw = load_w(wintm[l, :, :, 16:528], 512)
        for i in range(18):
            pa, rpa = psrot()
            tm_matmul(w, rw, 16, 512, i, pa, rpa)
            if stop <= 3.1:
                continue
            mset("pool", VMAt2, 0.0, [rVMAt])
            mset("pool", VMAt[:, :, 64:65], 1.0, [rVMAt])
            cp("act", VMAt[:, :, 0:64], pa[:, 0:256].rearrange("p (h d) -> p h d", h=4), [rpa], [rVMAt])
            dma("sp", vmad[i], VMAt2, [rVMAt], [])
            if stop <= 3.2:
                continue
            for d_ in range(2):
                tt("dve", KH[:, d_, :].rearrange("p (h d) -> p h d", h=4),
                   pa[:, 256:512].rearrange("p (h d) -> p h d", h=4),
                   SCAL[:, i, 8 + 4 * d_:12 + 4 * d_].unsqueeze(2).to_broadcast([128, 4, 64]), ALU.mult,
                   [rpa, rSCAL[i], rVMAt], [rKH])
            if stop <= 3.3:
                continue
            for d_ in range(2):
                for pr in range(2):
                    mm(PS[6 + d_][:, pr * (2 * EW):(pr + 1) * (2 * EW)], KH[:, d_, pr * 128:(pr + 1) * 128],
                       VMAt[:, 2 * pr:2 * pr + 2, :], True, True, [rKH, rVMAt], [rPS[6 + d_]])
            if stop <= 3.4:
                continue
            for d_ in range(2):
                for pr in range(2):
                    for hh in range(2):
                        cp("act", DC[64 * hh:64 * hh + 64, d_ * 18 + i, pr * EW:(pr + 1) * EW],
                           PS[6 + d_][64 * hh:64 * hh + 64, pr * (2 * EW) + hh * EW:pr * (2 * EW) + hh * EW + EW],
                           [rPS[6 + d_]], [rDC[d_ * 18 + i]])
        if stop <= 3.5:
            return

        mset("dve", CST[:], 0.0, [rCST])
        orders = [list(range(18)), [1, 0] + list(range(17, 1, -1))]
        for d_ in range(2):
            eng = "dve"
            for i in orders[d_]:
                cp(eng, CBF[:, d_ * 18 + i, :], CST[:, d_, :], [rCST], [rCBF[d_ * 18 + i]])
                for pr in range(2):
                    for hh in range(2):
                        col = 24 + d_ * 4 + 2 * pr + hh
                        stt(eng, CST[64 * hh:64 * hh + 64, d_, pr * EW:(pr + 1) * EW],
                            CST[64 * hh:64 * hh + 64, d_, pr * EW:(pr + 1) * EW],
                            SCAL[64 * hh:64 * hh + 64, i, col:col + 1],
                            DC[64 * hh:64 * hh + 64, d_ * 18 + i, pr * EW:(pr + 1) * EW],
                            ALU.mult, ALU.add, [rCST, rSCAL[i], rDC[d_ * 18 + i]], [rCST])

        if stop <= 4:
            return
        w, rw = load_w(wintm[l, :, :, 528:912], 384)
        for i in range(18):
            pa, rpa = psrot()
            tm_matmul(w, rw, 528, 384, i, pa, rpa)
            cp("act", VA[:, i, :].rearrange("p (a b) -> p a b", a=2)[:, :, 0:64],
               pa[:, 0:128].rearrange("p (a b) -> p a b", a=2), [rpa], [rVA[i]])
            act(OMSt[:], pa[:, 128:384], AF.Sigmoid, [rpa], [rOMSt])
            dma("sp", omsd[i], OMSt[:], [rOMSt], [])

        w, rw = load_w(wintm[l, :, :, 912:1424], 512)
        for i in range(18):
            pa, rpa = psrot()
            tm_matmul(w, rw, 912, 512, i, pa, rpa)
            act(UG[:], pa[:, 0:512], AF.Gelu_apprx_tanh, [rpa], [rUG])
            ugv = UG[:, 256:512].rearrange("p (g d) -> p g d", g=4)
            tt("dve", T1[:, 0:256], UG[:, 256:512], UG[:, 256:512], ALU.mult, [rUG], [rT1])
            S.op("dve", lambda e, o=SM[:, 32:36], a=T1[:, 0:256].rearrange("p (g d) -> p g d", g=4):
                 e.tensor_reduce(out=o, in_=a, axis=AX.X, op=ALU.add), [rT1], [rSM])
            act(SM[:, 32:36], SM[:, 32:36], AF.Sqrt, [rSM], [rSM], bias=EPS, scale=1.0 / 64)
            recip(SM[:, 36:40], SM[:, 32:36], [rSM], [rSM])
            tt("dve", T1[:, 0:256].rearrange("p (g d) -> p g d", g=4), ugv,
               SM[:, 36:40].unsqueeze(2).to_broadcast([128, 4, 64]), ALU.mult, [rUG, rSM], [rT1])
            tt("dve", VCN[:], T1[:, 0:256], GMV[:, 1, :], ALU.mult, [rT1, rGMV], [rVCN])
            pz, rpz = PS[5], rPS[5]
            for g in range(4):
                mm(pz[:, g * 64:(g + 1) * 64], WSPb[:, g, :], VCN[:, g * 64:(g + 1) * 64], True, True,
                   [rWSP, rVCN], [rpz])
            tt("dve", T2[:, 0:256].rearrange("p (g d) -> p g d", g=4), pz[:, 0:256].rearrange("p (g d) -> p g d", g=4),
               BSP[:, l, :].unsqueeze(2).to_broadcast([128, 4, 64]), ALU.add, [rpz, rBSP], [rT2])
            tt("dve", CMt[:], T2[:, 0:256], UG[:, 0:256], ALU.mult, [rT2, rUG], [rCMt])
            dma("sp", cmd[i], CMt[:], [rCMt], [])

        if stop <= 5:
            return
        S.barrier()
        AR.off = mark_tmp
        T1 = AR.get([128, 512], F32); rT1 = Res()
        T2 = AR.get([128, 512], F32); rT2 = Res()
        AT = AR.get([128, 2, 4, 128], BF16); rAT = Res()
        PT = [AR.get([128, 512], BF16) for _ in range(2)]; rPT = [Res(), Res()]
        OTM = AR.get([128, 512], F32); rOTM = Res()
        VMAt2 = AR.get([128, (4 * EW)], BF16); VMAt = VMAt2.rearrange("p (h d) -> p h d", h=4); rVMAt = Res()
        OMSt = AR.get([128, 256], BF16); rOMSt = Res()
        CMt = AR.get([128, 256], BF16); rCMt = Res()
        RD = AR.get([128, 512], F32); rRD = Res()
        HS = AR.get([128, 256], F32); rHS = Res()
        rMIX = [[Res() for _ in range(18)] for _ in range(8)]

        def mixcols(i):
            return hcol(i * 128)

        qblocks = list(range(2, 18)) + ([] if last else [0, 1])
        def attn_block(qb):
            ktiles = list(range(18)) if qb >= 2 else [0, 1]
            q0 = qb * 128
            nq = 0 if qb < 2 else 1 + (qb - 2) // 4
            for j in range(2):
                po, rpo = PS[4 + j], rPS[4 + j]
                for kk, kt in enumerate(ktiles):
                    nk = 0 if kt < 2 else 1 + (kt - 2) // 4
                    pscore, rpscore = psrot()
                    mm(pscore[:, 0:512], QK[64 * j:64 * j + 64, 4, kt * 128:(kt + 1) * 128],
                       QK[64 * j:64 * j + 64, 0:4, q0:q0 + 128], True, True,
                       [rQK[4][nk]] + [rQK[c][nq] for c in range(4)], [rpscore])
                    pi_ = kk % 2
                    act(PT[pi_][:], pscore[:, 0:512], AF.Exp, [rpscore], [rPT[pi_]], scale=0.125)
                    mm(po[:, 0:512], VA[:, kt, j * 128:(j + 1) * 128], PT[pi_][:], kk == 0, kk == len(ktiles) - 1,
                       [rVA[kt], rPT[pi_]], [rpo])
                recip(RD[64:128, :], po[64:128, :], [rpo], [rRD])
                c0 = mixcols(qb)
                tt("dve", HB[64 * j:64 * j + 64, 0:4, c0:c0 + 128],
                   po[0:64, 0:512].rearrange("p (c t) -> p c t", c=4),
                   RD[64:128, :].rearrange("p (c t) -> p c t", c=4), ALU.mult, [rpo, rRD],
                   [rMIX[c][qb] for c in range(4)])


        tiles2 = list(range(18)) if not last else list(range(2, 18))
        def pass2_tile(i):
            t0 = i * 128
            n = 0 if i < 2 else 1 + (i - 2) // 4
            if True:
                dma("sp", VMAt2, vmad[i], [], [rVMAt])
                dma("sp", OMSt[:], omsd[i], [], [rOMSt])
                dma("sp", CMt[:], cmd[i], [], [rCMt])
            pscs = [psrot(), psrot()]
            for h in (0, 2, 1, 3):
                ci, hh = h // 2, h % 2
                mm(pscs[hh][0][:, ci * 128:(ci + 1) * 128], QM[64 * hh:64 * hh + 64, 2 + ci, t0:t0 + 128],
                   QM[64 * hh:64 * hh + 64, ci, t0:t0 + 128], True, True, [rQM[2 + ci][n], rQM[ci][n]], [pscs[hh][1]])
            for d_ in range(2):
                msk = MASKF if d_ == 0 else MASKB
                for h in range(4):
                    ci, hh = h // 2, h % 2
                    stt("dve", AT[:, d_, h, :], pscs[hh][0][:, ci * 128:(ci + 1) * 128],
                        SCAL[:, i, d_ * 4 + h:d_ * 4 + h + 1], msk, ALU.mult, ALU.mult,
                        [pscs[hh][1], rSCAL[i], rCONST], [rAT])
            if stop <= 6.1:
                return
            for d_ in range(2):
                pout, rpout = PS[6 + d_], rPS[6 + d_]
                for h in range(4):
                    ci, hh = h // 2, h % 2
                    mm(pout[:, h * EW:(h + 1) * EW], AT[:, d_, h, :], VMAt[:, h, :], True, False,
                       [rAT, rVMAt], [rpout])
                    mm(pout[:, h * EW:(h + 1) * EW], QM[64 * hh:64 * hh + 64, ci, t0:t0 + 128],
                       CBF[64 * hh:64 * hh + 64, d_ * 18 + i, ci * EW:(ci + 1) * EW], False, True,
                       [rQM[ci][n], rCBF[d_ * 18 + i]], [rpout])
            if stop <= 6.2:
                return
            for d_ in range(2):
                pout, rpout = PS[6 + d_], rPS[6 + d_]
                pv = pout[:, 0:(4 * EW)].rearrange("p (h e) -> p h e", h=4)
                den = SM[:, 40 + 4 * d_:44 + 4 * d_]
                act(den, pv[:, :, 64], AF.Abs, [rpout], [rSM])
                tt("dve", den, den, SCAL[:, i, 16 + 4 * d_:20 + 4 * d_], ALU.max, [rSM, rSCAL[i]], [rSM])
                recip(den, den, [rSM], [rSM])
                dst = HS if d_ == 0 else T1[:, 0:256]
                tt("dve", dst.rearrange("p (h e) -> p h e", h=4), pv[:, :, 0:64],
                   den.unsqueeze(2).to_broadcast([128, 4, 64]), ALU.mult, [rpout, rSM], [rHS if d_ == 0 else rT1])
            tt("dve", HS[:], HS[:], T1[:, 0:256], ALU.add, [rHS, rT1], [rHS])
            if stop <= 6.3:
                return
            tt("dve", T2[:, 0:256], HS[:], HS[:], ALU.mult, [rHS], [rT2])
            S.op("dve", lambda e, o=SM[:, 48:52], a=T2[:, 0:256].rearrange("p (g d) -> p g d", g=4):
                 e.tensor_reduce(out=o, in_=a, axis=AX.X, op=ALU.add), [rT2], [rSM])
            act(SM[:, 48:52], SM[:, 48:52], AF.Sqrt, [rSM], [rSM], bias=EPS, scale=1.0 / 64)
            recip(SM[:, 52:56], SM[:, 48:52], [rSM], [rSM])
            tt("dve", HS[:].rearrange("p (h e) -> p h e", h=4), HS[:].rearrange("p (h e) -> p h e", h=4),
               SM[:, 52:56].unsqueeze(2).to_broadcast([128, 4, 64]), ALU.mult, [rHS, rSM], [rHS])
            tt("dve", HS[:], HS[:], GMV[:, 0, :], ALU.mult, [rHS, rGMV], [rHS])
            tt("dve", OTM[:, 0:256], HS[:], OMSt[:], ALU.mult, [rHS, rOMSt], [rOTM])
            cp("pool", OTM[:, 256:512], CMt[:], [rCMt], [rOTM])
            if stop <= 6.4:
                return
            ptr, rptr = psrot()
            for c in range(4):
                S.op("pe", lambda e, o=ptr[:, c * 128:(c + 1) * 128], a=OTM[:, c * 128:(c + 1) * 128]:
                     e.transpose(out=o, in_=a, identity=IDENT), [rOTM, rCONST], [rptr])
            c0 = mixcols(i)
            cp("act", HB[:, 4:8, c0:c0 + 128], ptr[:, 0:512].rearrange("p (c t) -> p c t", c=4), [rptr],
               [rMIX[c][i] for c in range(4, 8)])


        for idx in range(max(len(qblocks), len(tiles2))):
            if idx < len(qblocks):
                attn_block(qblocks[idx])
            if idx < len(tiles2) and stop > 6:
                pass2_tile(tiles2[idx])
        if stop <= 7:
            return
        S.barrier()
        if dbg:
            dma("sp", mix_dbg[b], HB[:], [], [])
            S.barrier()
        AR.reset()
        WOUT = AR.get([128, 8, 1024], BF16); rWOUT = [Res(), Res()]
        SQ = AR.get([128, 8, 512], BF16)
        for hf in range(2):
            dma("pool", WOUT[:, :, hf * 512:(hf + 1) * 512], wout[l, :, :, hf * 512:(hf + 1) * 512], (), [rWOUT[hf]])
        tiles3 = list(range(5)) if not last else list(range(1, 5))
        for n in tiles3:
            t0, w = TT[n]
            c0 = hcol(t0)
            m = ctx_m if n == 0 else b
            xt, rxt = xt_next()
            dma("sp", xt[:, :, 0:w], xsrc[b, :, :, t0:t0 + w], [rXD[b][n]], [rxt])
            for mo in range(8):
                pa, rpa = psrot()
                for k in range(8):
                    mm(pa[:, 0:w], WOUT[:, k, mo * 128:(mo + 1) * 128], HB[:, k, c0:c0 + w], k == 0, k == 7,
                       [rWOUT[mo // 4], rHB[k][n]], [rpa])
                stt("dve", xt[:, mo, 0:w], pa[:, 0:w], MOD[:, l, 16 + mo, m:m + 1], xt[:, mo, 0:w],
                    ALU.mult, ALU.add, [rpa, rMOD, rxt], [rxt])
            dma("sp", xd[b, :, :, t0:t0 + w], xt[:, :, 0:w], [rxt], [rXD[b][n]])
            if dbg:
                dma("sp", xm_dbg[b, :, :, t0:t0 + w], xt[:, :, 0:w], [rxt], [])
            norm_tile(l, 1, m, n, xt, rxt, SQ)

        if stop <= 8:
            return
        S.barrier()
        AR.reset()
        WUP = AR.get([128, 8, 2816], BF16); rWUP = [Res() for _ in range(6)]
        WDN = AR.get([128, 11, 1024], BF16); rWDN = [Res(), Res()]
        U = AR.get([128, 11, 512], BF16); rU = [Res() for _ in range(11)]
        AV = [AR.get([128, 512], F32) for _ in range(2)]; rAV = [Res(), Res()]
        TG = [AR.get([128, 512], F32) for _ in range(2)]; rTG = [Res(), Res()]
        TV = [AR.get([128, 512], F32) for _ in range(2)]; rTV = [Res(), Res()]
        SG = [AR.get([128, 512], F32) for _ in range(2)]; rSG = [Res(), Res()]
        TX = AR.get([128, 512], F32); rTX = Res()
        wins = []
        if not last:
            wins.append((0, 258, 0, 0))
        for i in range(4):
            wins.append((258 + 510 * i, 512, 256 + 510 * i, None))
        wins.append((258 + 2040, 10, 256 + 2040, None))
        rXW = [Res() for _ in range(len(wins))]
        gi = [0]
        for hf in range(2):
            for pc in range(6):
                wd = 512 if pc < 5 else 256
                dma("pool", WUP[:, :, pc * 512:pc * 512 + wd], wup[l, hf, :, :, pc * 512:pc * 512 + wd], (), [rWUP[pc]])
            for pc in range(2):
                dma("pool", WDN[:, :, pc * 512:(pc + 1) * 512], wdown[l, hf, :, :, pc * 512:(pc + 1) * 512], (), [rWDN[pc]])
            for wi, (c0, w, tok0, _) in enumerate(wins):
                wo = w - 2
                m = ctx_m if tok0 < 256 else b
                hbres = [rHB[k][nn] for k in range(8) for nn in range(5)]
                for g in range(11):
                    pg_, rpg_ = psrot()
                    pv_, rpv_ = psrot()
                    for k in range(8):
                        mm(pg_[:, 0:w], WUP[:, k, g * 128:(g + 1) * 128], HB[:, k, c0:c0 + w], k == 0, k == 7,
                           [rWUP[(g * 128) // 512]] + (hbres if k == 0 else []), [rpg_])
                    for k in range(8):
                        mm(pv_[:, 0:w], WUP[:, k, 1408 + g * 128:1408 + (g + 1) * 128], HB[:, k, c0:c0 + w], k == 0, k == 7,
                           [rWUP[(1408 + g * 128) // 512]], [rpv_])
                    x_ = gi[0] % 2
                    gi[0] += 1
                    cg = CONVP[:, l, hf, g, :]
                    cv_ = CONVP[:, l, hf, 11 + g, :]
                    act(TG[x_][:, 0:wo], pg_[:, 1:1 + wo], AF.Identity, [rpg_, rCONVP], [rTG[x_]],
                        bias=cg[:, 3:4], scale=cg[:, 1:2])
                    stt("dve", TG[x_][:, 0:wo], pg_[:, 0:wo], cg[:, 0:1], TG[x_][:, 0:wo], ALU.mult, ALU.add,
                        [rpg_, rTG[x_]], [rTG[x_]])
                    stt("dve", TG[x_][:, 0:wo], pg_[:, 2:2 + wo], cg[:, 2:3], TG[x_][:, 0:wo], ALU.mult, ALU.add,
                        [rpg_, rTG[x_]], [rTG[x_]])
                    act(TV[x_][:, 0:wo], pv_[:, 1:1 + wo], AF.Identity, [rpv_, rCONVP], [rTV[x_]],
                        bias=cv_[:, 3:4], scale=cv_[:, 1:2])
                    stt("dve", TV[x_][:, 0:wo], pv_[:, 0:wo], cv_[:, 0:1], TV[x_][:, 0:wo], ALU.mult, ALU.add,
                        [rpv_, rTV[x_]], [rTV[x_]])
                    stt("dve", TV[x_][:, 0:wo], pv_[:, 2:2 + wo], cv_[:, 2:3], TV[x_][:, 0:wo], ALU.mult, ALU.add,
                        [rpv_, rTV[x_]], [rTV[x_]])
                    act(SG[x_][:, 0:wo], TG[x_][:, 0:wo], AF.Silu, [rTG[x_]], [rSG[x_]])
                    tt("dve", U[:, g, 0:wo], SG[x_][:, 0:wo], TV[x_][:, 0:wo], ALU.mult, [rSG[x_], rTV[x_]], [rU[g]])
                xt, rxt = xt_next()
                dma("sp", xt[:, :, 0:wo], xd[b, :, :, tok0:tok0 + wo], [rXW[wi]], [rxt])
                for mo in range(8):
                    pa, rpa = psrot()
                    for kc in range(11):
                        mm(pa[:, 0:wo], WDN[:, kc, mo * 128:(mo + 1) * 128], U[:, kc, 0:wo], kc == 0, kc == 10,
                           [rWDN[mo // 4], rU[kc]], [rpa])
                    stt("dve", xt[:, mo, 0:wo], pa[:, 0:wo], MOD[:, l, 40 + mo, m:m + 1], xt[:, mo, 0:wo],
                        ALU.mult, ALU.add, [rpa, rMOD, rxt], [rxt])
                if last and hf == 1:
                    tk = dma("sp", yout[b, :, :, tok0 - 256:tok0 - 256 + wo], xt[:, :, 0:wo], [rxt], [rXW[wi]])
                    out_toks.append(tk)
                else:
                    dma("sp", xd[b, :, :, tok0:tok0 + wo], xt[:, :, 0:wo], [rxt], [rXW[wi]])
        S.barrier()

    out_toks = []
    for b in range(nb):
        for l in range(nlayers):
            if stop > 0:
                layer(b, l)
    S.barrier()
    S.emit(nc, st)
    st.close()
    return nc


def _rope_tables():
    n = S_LAT
    grid_w = 64
    nf = HD // 4
    t = np.arange(n)
    row = (t // grid_w).astype(np.float32)
    col = (t % grid_w).astype(np.float32)
    inv = (10000.0 ** (-np.arange(nf, dtype=np.float32) / nf)).astype(np.float32)
    ang = np.concatenate([row[:, None] * inv[None], col[:, None] * inv[None]], axis=-1)
    cos = np.cos(ang).astype(np.float32).reshape(n, 2, nf)
    sin = np.sin(ang).astype(np.float32).reshape(n, 2, nf)
    C = np.ones((HD, NT), np.float32)
    Sg = np.zeros((HD, NT), np.float32)
    for ax in range(2):
        for half in range(2):
            rows = slice(ax * 32 + half * 16, ax * 32 + half * 16 + 16)
            C[rows, T_CTX:] = cos[:, ax, :].T
            Sg[rows, T_CTX:] = (-1.0 if half == 0 else 1.0) * sin[:, ax, :].T
    tab = np.stack([np.concatenate([C, C], 0), np.concatenate([Sg, Sg], 0)], axis=1)
    return np.ascontiguousarray(tab)


def _consts():
    c = np.zeros((128, 8, 128), np.float32)
    r = np.arange(128)[:, None]
    s = np.arange(128)[None, :]
    c[:, 0, :] = (r == s)
    c[:, 1, :] = -1.0 * (r <= s)
    c[:, 2, :] = -1.0 * (r >= s)
    c[:, 3, :] = -1.0
    c[:, 4, :] = (r <= s)
    c[:, 5, :] = (r >= s)
    c[:, 6, :] = 1.0
    c[:, 7, :] = ((r // 64) == (s // 64))
    return c


def _swap_idx():
    d = np.arange(HD)
    ax, half, f = d // 32, (d // 16) % 2, d % 16
    return ax * 32 + (1 - half) * 16 + f


def _prep_shared(inp):
    f = np.float32
    w_ada = inp["w_ada"]; w_in = inp["w_in"]; b_in = inp["b_in"]
    sh = {}
    sh["wada"] = np.ascontiguousarray(w_ada.reshape(DEPTH, 8, 128, 12, 512).transpose(0, 3, 2, 1, 4))
    sh["bada"] = np.ascontiguousarray(inp["b_ada"].reshape(DEPTH, 48, 128).transpose(2, 0, 1))
    gn = np.stack([inp["g_norm1"], inp["g_norm2"]], axis=1)
    sh["gn12"] = np.ascontiguousarray(gn.reshape(DEPTH, 2, 8, 128).transpose(3, 0, 1, 2))
    sw = _swap_idx()
    QA, KA, VAo, QMo, KMo, VMo, OMo, GTo, UCo, VCo = 0, 512, 640, 768, 1024, 1280, 1536, 1792, 1808, 2064
    fm_cols = []
    for c in range(4):
        h0, h1 = c, 4 + c
        q = np.concatenate([QA + h0 * 64 + np.arange(64), QA + h1 * 64 + np.arange(64)])
        qs = np.concatenate([QA + h0 * 64 + sw, QA + h1 * 64 + sw])
        fm_cols += [q, qs]
    k = np.concatenate([KA + np.arange(64), KA + 64 + np.arange(64)])
    ks = np.concatenate([KA + sw, KA + 64 + sw])
    fm_cols += [k, ks]
    fm_cols += [QMo + np.arange(128), QMo + 128 + np.arange(128), KMo + np.arange(128), KMo + 128 + np.arange(128)]
    fm_idx = np.concatenate(fm_cols)
    wfm = w_in[:, :, fm_idx]
    sh["winfm"] = np.ascontiguousarray(wfm.reshape(DEPTH, 8, 128, 7, 256).transpose(0, 3, 2, 1, 4))
    sh["binfm"] = np.ascontiguousarray(b_in[:, fm_idx].reshape(DEPTH, 14, 128).transpose(2, 0, 1))
    tm_idx = np.concatenate([GTo + np.arange(16), VMo + np.arange(256), KMo + np.arange(256), VAo + np.arange(128),
                             OMo + np.arange(256), UCo + np.arange(256), VCo + np.arange(256)])
    wtm = w_in[:, :, tm_idx]
    sh["wintm"] = np.ascontiguousarray(wtm.reshape(DEPTH, 8, 128, NTM).transpose(0, 2, 1, 3))
    sh["bintm"] = np.ascontiguousarray(b_in[:, tm_idx][None])
    gq = inp["g_q"]; gk = inp["g_k"]
    g4 = np.stack([np.tile(gq, (1, 2)), np.tile(gq[:, sw], (1, 2)), np.tile(gk, (1, 2)), np.tile(gk[:, sw], (1, 2))], axis=2)
    sh["gqk"] = np.ascontiguousarray(g4.transpose(1, 0, 2))
    sh["rope"] = _rope_tables()
    sh["gmv"] = np.ascontiguousarray(np.stack([inp["g_mh"], inp["g_v"]], axis=1))
    sh["wsp"] = np.ascontiguousarray(inp["w_sp"].transpose(0, 3, 1, 2))
    sh["bsp"] = np.ascontiguousarray(inp["b_sp"].transpose(2, 0, 1))
    rows = []
    for c in range(4):
        rows += [c * 64 + np.arange(64), (4 + c) * 64 + np.arange(64)]
    rows += [512 + np.arange(512)]
    ridx = np.concatenate(rows)
    wo = inp["w_out"][:, ridx, :]
    sh["wout"] = np.ascontiguousarray(wo.reshape(DEPTH, 8, 128, 1024).transpose(0, 2, 1, 3))
    w_up = inp["w_up"]
    ucols = np.stack([np.concatenate([hf * 1408 + np.arange(1408), DFF + hf * 1408 + np.arange(1408)]) for hf in range(2)])
    wu = w_up[:, :, ucols]
    sh["wup"] = np.ascontiguousarray(wu.reshape(DEPTH, 8, 128, 2, 2816).transpose(0, 3, 2, 1, 4))
    cw = inp["conv_w"]; cb = inp["conv_b"]
    cpar = np.concatenate([cw, cb[:, None, :]], axis=1)
    cpar = cpar[:, :, ucols]
    sh["convp"] = np.ascontiguousarray(cpar.reshape(DEPTH, 4, 2, 22, 128).transpose(4, 0, 2, 3, 1))
    sh["wdown"] = np.ascontiguousarray(inp["w_down"].reshape(DEPTH, 2, 11, 128, 1024).transpose(0, 1, 3, 2, 4))
    sh["consts"] = _consts()
    return {k: np.ascontiguousarray(v, dtype=f) for k, v in sh.items()}


def _prep_core(inp, core):
    b0 = 2 * core
    xs = []
    for b in (b0, b0 + 1):
        xcat = np.concatenate([inp["ctx"][b], inp["x"][b]], axis=0)
        xs.append(xcat.T.reshape(8, 128, NT).transpose(1, 0, 2))
    xin = np.ascontiguousarray(np.stack(xs), dtype=np.float32)
    cv = np.stack([inp["c"][b0], inp["c"][b0 + 1], inp["c_ctx"]], axis=1)
    cvec = np.ascontiguousarray(cv.reshape(8, 128, 3).transpose(1, 0, 2), dtype=np.float32)
    return {"xin": xin, "cvec": cvec}


_NC_CACHE = {}


def kernel(**inputs):
    inp = {k: np.asarray(v) for k, v in inputs.items()}
    shared = _prep_shared(inp)
    if "nc" not in _NC_CACHE:
        _NC_CACHE["nc"] = build()
    nc = _NC_CACHE["nc"]
    in_maps = []
    for core in range(8):
        m = dict(shared)
        m.update(_prep_core(inp, core))
        in_maps.append(m)
    res = run_bass_kernel_spmd(nc, in_maps, core_ids=list(range(8)))
    out = np.empty((16, S_LAT, D), np.float32)
    for core in range(8):
        y = np.asarray(res.results[core]["yout"])
        for i in range(2):
            out[2 * core + i] = y[i].transpose(1, 0, 2).reshape(D, S_LAT).T
    return out
```

```python
import contextlib
import numpy as np
import concourse.bass as bass
import concourse.mybir as mybir
from concourse.bass_utils import run_bass_kernel_spmd

F32 = mybir.dt.float32
BF16 = mybir.dt.bfloat16
AF = mybir.ActivationFunctionType
ALU = mybir.AluOpType
AX = mybir.AxisListType

D = 1024
S_LAT = 2048
T_CTX = 256
NT = 2304
DEPTH = 4
DFF = 2816
HD = 64
EPS = 1e-6
NTM = 1424
EW = 66
HBW = 2308
TT = [(0, 256), (256, 512), (768, 512), (1280, 512), (1792, 512)]
LN8 = float(np.log(0.125))


def hcol(t):
    return t + 1 if t < 256 else t + 3


class Tok:
    __slots__ = ("eng", "seq", "clk", "needed", "sem", "val")

    def __init__(self, eng, seq, clk):
        self.eng = eng
        self.seq = seq
        self.clk = clk
        self.needed = False
        self.sem = None
        self.val = 0


class Res:
    __slots__ = ("w", "r", "excl")

    def __init__(self, excl=False):
        self.w = None
        self.r = {}
        self.excl = excl


class Sched:
    ENGS = ("pe", "act", "dve", "pool", "sp")

    def __init__(self, n_dma_sems=8):
        self.q = {e: [] for e in self.ENGS}
        self.clock = {e: {} for e in self.ENGS}
        self.seq = {e: 0 for e in self.ENGS}
        self.toks = {e: [] for e in self.ENGS}
        self.n_dma = n_dma_sems
        self.dma_rr = {e: 0 for e in self.ENGS}
        self.dma_last = {}

    def _merge(self, clk, t):
        for k, v in t.clk.items():
            if clk.get(k, 0) < v:
                clk[k] = v
        if clk.get(t.eng, 0) < t.seq:
            clk[t.eng] = t.seq

    def _deps(self, eng, reads, writes):
        clk = self.clock[eng]
        waits = []
        cand = []
        for r in reads:
            if r.w is not None:
                cand.append(r.w)
        for w in writes:
            if w.w is not None:
                cand.append(w.w)
            cand.extend(w.r.values())
        for t in cand:
            if eng == "pe" and t.eng == "pe":
                continue
            if clk.get(t.eng, 0) < t.seq:
                waits.append(t)
                t.needed = True
                self._merge(clk, t)
        return waits

    def _mark(self, tok, reads, writes):
        for r in reads:
            r.r[tok.eng] = tok
        for w in writes:
            w.w = tok
            w.r = {}

    def op(self, eng, fn, reads=(), writes=()):
        ex = [r for r in reads if r.excl]
        if ex:
            writes = list(writes) + ex
        waits = self._deps(eng, reads, writes)
        self.seq[eng] += 1
        tok = Tok(eng, self.seq[eng], dict(self.clock[eng]))
        self.toks[eng].append(tok)
        self.q[eng].append((waits, fn, tok, False))
        self._mark(tok, reads, writes)
        return tok

    def dma(self, eng, fn, reads=(), writes=()):
        waits = self._deps(eng, reads, writes)
        j = self.dma_rr[eng]
        self.dma_rr[eng] = (j + 1) % self.n_dma
        key = ("dma", eng, j)
        last = self.dma_last.get(key)
        clk = self.clock[eng]
        if last is not None and clk.get(key, 0) < last.seq:
            waits.append(last)
            self._merge(clk, last)
        seq = (last.seq if last is not None else 0) + 1
        tok = Tok(key, seq, dict(clk))
        tok.needed = True
        self.dma_last[key] = tok
        self.q[eng].append((waits, fn, tok, True))
        self._mark(tok, reads, writes)
        return tok

    def barrier(self):
        lasts = [self.toks[e][-1] for e in self.ENGS if self.toks[e]]
        lasts += list(self.dma_last.values())
        for e in self.ENGS:
            clk = self.clock[e]
            waits = []
            for t in lasts:
                if e == "pe" and t.eng == "pe":
                    continue
                if clk.get(t.eng, 0) < t.seq:
                    waits.append(t)
                    t.needed = True
                    self._merge(clk, t)
            if waits:
                self.q[e].append((waits, None, None, False))

    def emit(self, nc, stack):
        esem = {}
        for e in self.ENGS:
            esem[e] = stack.enter_context(nc.semaphore("s_" + e))
            c = 0
            for t in self.toks[e]:
                if t.needed:
                    c += 1
                t.val = c
                t.sem = esem[e]
        dsem = {}
        for key in self.dma_last:
            dsem[key] = stack.enter_context(nc.semaphore("d_%s_%d" % (key[1], key[2])))
        block = stack.enter_context(nc.Block())
        q = self.q

        def run(e, engobj):
            for (waits, fn, tok, is_dma) in q[e]:
                for t in waits:
                    if isinstance(t.eng, tuple):
                        engobj.wait_ge(dsem[t.eng], 16 * t.seq)
                    else:
                        engobj.wait_ge(t.sem, t.val)
                if fn is None:
                    continue
                ins = fn(engobj)
                if is_dma:
                    ins.then_inc(dsem[tok.eng], 16)
                elif tok.needed:
                    ins.then_inc(tok.sem, 1)

        @block.tensor
        def _(e):
            run("pe", e)

        @block.scalar
        def _(e):
            run("act", e)

        @block.vector
        def _(e):
            run("dve", e)

        @block.gpsimd
        def _(e):
            run("pool", e)

        @block.sync
        def _(e):
            run("sp", e)


def build(nlayers=DEPTH, nb=2, dbg=(), stop=99):
    nc = bass.Bass("TRN2", target_bir_lowering=False)
    S = Sched()
    st = contextlib.ExitStack()

    def din(name, shape, dt=F32):
        return nc.dram_tensor(name, list(shape), dt, kind="ExternalInput").ap()

    def dscr(name, shape, dt):
        return nc.dram_tensor(name, list(shape), dt, kind="ExternalOutput").ap()

    xin = din("xin", [2, 128, 8, NT])
    cvec = din("cvec", [128, 8, 3])
    wada = din("wada", [DEPTH, 12, 128, 8, 512])
    bada = din("bada", [128, DEPTH, 48])
    gn12 = din("gn12", [128, DEPTH, 2, 8])
    winfm = din("winfm", [DEPTH, 7, 128, 8, 256])
    binfm = din("binfm", [128, DEPTH, 14])
    wintm = din("wintm", [DEPTH, 128, 8, NTM])
    bintm = din("bintm", [1, DEPTH, NTM])
    gqk = din("gqk", [128, DEPTH, 4])
    rope = din("rope", [128, 2, NT])
    gmv = din("gmv", [DEPTH, 2, 256])
    wsp = din("wsp", [DEPTH, 128, 4, 128])
    bsp = din("bsp", [128, DEPTH, 4])
    wout = din("wout", [DEPTH, 128, 8, 1024])
    wup = din("wup", [DEPTH, 2, 128, 8, 2816])
    convp = din("convp", [128, DEPTH, 2, 22, 4])
    wdown = din("wdown", [DEPTH, 2, 128, 11, 1024])
    consts = din("consts", [128, 8, 128])
    yout = nc.dram_tensor("yout", [2, 128, 8, S_LAT], F32, kind="ExternalOutput").ap()
    if dbg:
        xd = nc.dram_tensor("xd", [2, 128, 8, NT], F32, kind="ExternalOutput").ap()
        xm_dbg = nc.dram_tensor("xm_dbg", [2, 128, 8, NT], F32, kind="ExternalOutput").ap()
        mix_dbg = nc.dram_tensor("mix_dbg", [2, 128, 8, HBW], BF16, kind="ExternalOutput").ap()
    else:
        xd = dscr("xd", [2, 128, 8, NT], F32)
    vmad = dscr("vmad", [18, 128, (4 * EW)], BF16)
    omsd = dscr("omsd", [18, 128, 256], BF16)
    cmd = dscr("cmd", [18, 128, 256], BF16)
    dbg_out = {}

    def sb(name, shape, dt=F32):
        return st.enter_context(nc.sbuf_tensor(name, list(shape), dt))

    def psb(name):
        return st.enter_context(nc.psum_tensor(name, [128, 512], F32))

    def mm(out, lhsT, rhs, start, stop, rd, wr):
        S.op("pe", lambda e, o=out, l=lhsT, r=rhs, a=start, b=stop: e.matmul(o, lhsT=l, rhs=r, start=a, stop=b), rd, wr)

    def act(out, in_, func, rd, wr, bias=None, scale=None):
        kw = {}
        if bias is not None:
            kw["bias"] = bias
        if scale is not None:
            kw["scale"] = scale
        S.op("act", lambda e, o=out, i=in_, f=func, k=kw: e.activation(out=o, in_=i, func=f, **k), rd, wr)

    def tt(eng, out, in0, in1, op, rd, wr):
        S.op(eng, lambda e, o=out, a=in0, b=in1, p=op: e.tensor_tensor(out=o, in0=a, in1=b, op=p), rd, wr)

    def ts(eng, out, in0, s1, op0, rd, wr, s2=None, op1=None):
        if op1 is None:
            S.op(eng, lambda e, o=out, a=in0, x=s1, p=op0: e.tensor_scalar(out=o, in0=a, scalar1=x, scalar2=None, op0=p), rd, wr)
        else:
            S.op(eng, lambda e, o=out, a=in0, x=s1, y=s2, p=op0, q=op1: e.tensor_scalar(out=o, in0=a, scalar1=x, scalar2=y, op0=p, op1=q), rd, wr)

    def stt(eng, out, in0, scalar, in1, op0, op1, rd, wr, tmp=None):
        if eng == "pool":
            t_ = out if tmp is None else tmp
            S.op(eng, lambda e, o=t_, a=in0, x=scalar, p=op0: e.tensor_scalar(out=o, in0=a, scalar1=x, scalar2=None, op0=p), rd, wr)
            S.op(eng, lambda e, o=out, a=t_, b=in1, p=op1: e.tensor_tensor(out=o, in0=a, in1=b, op=p), rd, wr)
            return
        S.op(eng, lambda e, o=out, a=in0, s=scalar, b=in1, p=op0, q=op1: e.scalar_tensor_tensor(out=o, in0=a, scalar=s, in1=b, op0=p, op1=q), rd, wr)

    def cp(eng, out, in_, rd, wr):
        if eng == "act":
            S.op(eng, lambda e, o=out, i=in_: e.activation(out=o, in_=i, func=AF.Identity), rd, wr)
        else:
            S.op(eng, lambda e, o=out, i=in_: e.tensor_copy(out=o, in_=i), rd, wr)

    def recip(out, in_, rd, wr):
        S.op("dve", lambda e, o=out, i=in_: e.reciprocal(out=o, in_=i), rd, wr)

    def mset(eng, ap, val, wr):
        S.op(eng, lambda e, a=ap, v=val: e.memset(a, v), (), wr)

    def dma(eng, out, in_, rd, wr):
        return S.dma(eng, lambda e, o=out, i=in_: e.dma_start(out=o, in_=i), rd, wr)

    CONST = sb("CONST", [128, 8, 128]); rCONST = Res()
    CONSTB = sb("CONSTB", [128, 2, 128], BF16); rCONSTB = Res()
    ONE1 = sb("ONE1", [1, 128]); rONE1 = Res()
    CV = sb("CV", [128, 8, 3]); SCb = sb("SCb", [128, 8, 3], BF16); rCV = Res(); rSCb = Res()
    MOD = sb("MOD", [128, DEPTH, 48, 3]); rMOD = Res()
    BADA = sb("BADA", [128, DEPTH, 48]); rBADA = Res()
    GN = sb("GN", [128, DEPTH, 2, 8]); rGN = Res()
    A12 = sb("A12", [128, DEPTH, 2, 8, 3]); rA12 = Res()
    BFM = sb("BFM", [128, DEPTH, 14]); rBFM = Res()
    BTM = sb("BTM", [1, NTM]); rBTM = Res()
    GQK = sb("GQK", [128, DEPTH, 4]); rGQK = Res()
    BSP = sb("BSP", [128, DEPTH, 4]); rBSP = Res()
    CONVP = sb("CONVP", [128, DEPTH, 2, 22, 4]); rCONVP = Res()
    ROPE = sb("ROPE", [128, 2, NT], BF16); rROPE = Res()
    GMV = sb("GMV", [128, 2, 256]); rGMV = Res()
    WSPb = sb("WSPb", [128, 4, 128], BF16); rWSP = Res()
    HB = sb("HB", [128, 8, HBW], BF16)
    rHB = [[Res() for _ in range(5)] for _ in range(8)]
    XT = [sb("XT0", [128, 8, 512])]; rXT = [Res()]
    rSQ = Res()
    SD = sb("SD", [128, 512]); rSD = Res()
    RS = sb("RS", [128, 512]); rRS = Res()
    WP = [sb("WP%d" % i, [128, 8, 512], BF16) for i in range(2)]; rWP = [Res(), Res()]
    ARENA = sb("ARENA", [128, 48700], BF16)
    PS = [psb("PS%d" % i) for i in range(8)]
    rPS = [Res(excl=True) for _ in range(8)]

    IDENT = CONST[:, 0, :]
    TRIF = CONST[:, 1, :]
    TRIB = CONST[:, 2, :]
    NEG1 = CONST[:, 3, :]
    MASKF = CONST[:, 4, :]
    MASKB = CONST[:, 5, :]
    ONESb = CONSTB[:, 0, :]
    BLKb = CONSTB[:, 1, :]

    class Arena:
        def __init__(self):
            self.off = 0

        def reset(self):
            self.off = 0

        def get(self, shape, dt):
            n = int(np.prod(shape[1:]))
            if dt == F32:
                n2 = 2 * n
            else:
                n2 = n
            o = self.off + (self.off % 2)
            self.off = o + n2 + (n2 % 2)
            assert self.off <= 48700, ("arena overflow", self.off)
            v = ARENA[:, o:o + n2]
            if dt == F32:
                v = v.bitcast(F32)
            v = v[0:shape[0]]
            if len(shape) == 3:
                v = v.rearrange("p (a b) -> p a b", a=shape[1])
            elif len(shape) == 4:
                v = v.rearrange("p (a b c) -> p a b c", a=shape[1], b=shape[2])
            return v

    AR = Arena()
    wp_i = [0]

    def load_w(src_ap, width, kc=8):
        i = wp_i[0] % 2
        wp_i[0] += 1
        dma("pool", WP[i][:, 0:kc, 0:width], src_ap, (), [rWP[i]])
        return WP[i], rWP[i]

    ps_i = [0]

    def psrot(n=4):
        i = ps_i[0] % n
        ps_i[0] += 1
        return PS[i], rPS[i]

    dma("sp", CONST[:], consts[:, :, :], (), [rCONST])
    cp("dve", CONSTB[:], CONST[:, 6:8, :], [rCONST], [rCONSTB])
    mset("dve", ONE1[:], 1.0, [rONE1])
    dma("sp", CV[:], cvec[:, :, :], (), [rCV])
    dma("sp", BADA[:], bada[:, :, :], (), [rBADA])
    dma("sp", GN[:], gn12[:, :, :, :], (), [rGN])
    dma("sp", BFM[:], binfm[:, :, :], (), [rBFM])
    dma("sp", GQK[:], gqk[:, :, :], (), [rGQK])
    dma("sp", BSP[:], bsp[:, :, :], (), [rBSP])
    dma("sp", CONVP[:], convp[:, :, :, :, :], (), [rCONVP])
    dma("pool", ROPE[:], rope[:, :, :], (), [rROPE])
    act(CV[:], CV[:], AF.Silu, [rCV], [rCV])
    cp("dve", SCb[:], CV[:], [rCV], [rSCb])
    for k in range(8):
        mset("pool", HB[:, k, 0:1], 0.0, [rHB[k][0]])
        mset("pool", HB[:, k, 257:259], 0.0, [rHB[k][0]])
        mset("pool", HB[:, k, 2307:2308], 0.0, [rHB[k][4]])

    for l in range(nlayers):
        pm = PS[0][:, 0:144]
        for pc in range(12):
            w, rw = load_w(wada[l, pc], 512)
            for j in range(4):
                jj = pc * 4 + j
                for k in range(8):
                    mm(pm[:, jj * 3:jj * 3 + 3], w[:, k, j * 128:(j + 1) * 128], SCb[:, k, :], k == 0, k == 7,
                       [rw, rSCb], [rPS[0]])
        tt("dve", MOD[:, l, :, :], pm.rearrange("p (a b) -> p a b", b=3),
           BADA[:, l, :].unsqueeze(2).to_broadcast([128, 48, 3]), ALU.add, [rPS[0], rBADA], [rMOD])
        for i in range(2):
            ts("dve", A12[:, l, i, :, :], MOD[:, l, 8 + 24 * i:16 + 24 * i, :], 1.0, ALU.add, [rMOD], [rA12])
            tt("dve", A12[:, l, i, :, :], A12[:, l, i, :, :],
               GN[:, l, i, :].unsqueeze(2).to_broadcast([128, 8, 3]), ALU.mult, [rA12, rGN], [rA12])

    def norm_tile(l, which, m, n, xt, rxt, SQ):
        t0, w = TT[n]
        c0 = hcol(t0)
        act(SQ[:, :, 0:w], xt[:, :, 0:w], AF.Square, [rxt], [rSQ])
        pss, rpss = PS[4], rPS[4]
        for k in range(8):
            mm(pss[:, 0:w], ONESb, SQ[:, k, 0:w], k == 0, k == 7, [rSQ, rCONSTB], [rpss])
        act(SD[:, 0:w], pss[:, 0:w], AF.Sqrt, [rpss], [rSD], bias=EPS, scale=1.0 / D)
        recip(RS[:, 0:w], SD[:, 0:w], [rSD], [rRS])
        tt("dve", xt[:, :, 0:w], xt[:, :, 0:w], RS[:, 0:w].unsqueeze(1).to_broadcast([128, 8, w]), ALU.mult,
           [rxt, rRS], [rxt])
        sh = 0 if which == 0 else 24
        for k in range(8):
            act(HB[:, k, c0:c0 + w], xt[:, k, 0:w], AF.Identity, [rxt, rA12, rMOD], [rHB[k][n]],
                bias=MOD[:, l, sh + k, m:m + 1], scale=A12[:, l, which, k, m:m + 1])

    xt_i = [0]

    def xt_next():
        return XT[0], rXT[0]

    rXD = [[Res() for _ in range(5)] for _ in range(2)]

    def layer(b, l):
        last = (l == DEPTH - 1)
        xsrc = xin if l == 0 else xd
        ctx_m = 2
        dma("sp", GMV[:, 0, :], gmv[l, 0:1, :].partition_broadcast(128), (), [rGMV])
        dma("sp", GMV[:, 1, :], gmv[l, 1:2, :].partition_broadcast(128), (), [rGMV])
        dma("pool", WSPb[:], wsp[l], (), [rWSP])
        dma("sp", BTM[:], bintm[0:1, l, :], (), [rBTM])
        S.barrier()
        AR.reset()
        QK = AR.get([128, 5, NT], BF16); rQK = [[Res() for _ in range(5)] for _ in range(5)]
        QM = AR.get([128, 4, NT], BF16); rQM = [[Res() for _ in range(5)] for _ in range(4)]
        VA = AR.get([128, 18, 256], BF16); rVA = [Res() for _ in range(18)]
        GT = AR.get([128, 18, 16], F32); rGT = [Res() for _ in range(18)]
        DC = AR.get([128, 36, (2 * EW)], BF16); rDC = [Res() for _ in range(36)]
        CBF = AR.get([128, 36, (2 * EW)], BF16); rCBF = [Res() for _ in range(36)]
        SCAL = AR.get([128, 18, 32], F32); rSCAL = [Res() for _ in range(18)]
        SM = AR.get([128, 64], F32); rSM = Res()
        CST = AR.get([128, 2, (2 * EW)], F32); rCST = Res()
        mark_tmp = AR.off
        SQ = AR.get([128, 8, 512], BF16)
        for n in range(5):
            t0, w = TT[n]
            m = ctx_m if n == 0 else b
            xt, rxt = xt_next()
            dma("sp", xt[:, :, 0:w], xsrc[b, :, :, t0:t0 + w], [rXD[b][n]], [rxt])
            norm_tile(l, 0, m, n, xt, rxt, SQ)
        S.barrier()
        AR.off = mark_tmp
        F1 = AR.get([128, 512], F32); rF1 = Res()
        F2 = AR.get([128, 512], F32); rF2 = Res()
        SQH = AR.get([128, 512], BF16); rSQH = Res()
        RSQ = AR.get([128, 512], F32); rRSQ = Res()
        T1 = AR.get([128, 512], F32); rT1 = Res()
        T2 = AR.get([128, 512], F32); rT2 = Res()
        UG = AR.get([128, 512], F32); rUG = Res()
        VCN = AR.get([128, 256], BF16); rVCN = Res()
        KH = AR.get([128, 2, 256], BF16); rKH = Res()
        VMAt2 = AR.get([128, (4 * EW)], BF16); VMAt = VMAt2.rearrange("p (h d) -> p h d", h=4); rVMAt = Res()
        OMSt = AR.get([128, 256], BF16); rOMSt = Res()
        CMt = AR.get([128, 256], BF16); rCMt = Res()

        if stop <= 1:
            return
        for i in range(18):
            mset("pool", VA[:, i, :].rearrange("p (a b) -> p a b", a=2)[:, :, 64:128], 1.0, [rVA[i]])

        def fm_matmul(w, rw, cc, n, pst, rpst):
            t0, wd = TT[n]
            c0 = hcol(t0)
            for k in range(8):
                mm(pst[:, 0:wd], w[:, k, cc * 128:(cc + 1) * 128], HB[:, k, c0:c0 + wd], k == 0, k == 7,
                   [rw, rHB[k][n]], [rpst])

        for pi in range(7):
            w, rw = load_w(winfm[l, pi], 256)
            if pi < 5:
                gcol = 0 if pi < 4 else 2
                for n in range(5):
                    t0, wd = TT[n]
                    pa, rpa = psrot()
                    pb, rpb = psrot()
                    fm_matmul(w, rw, 0, n, pa, rpa)
                    fm_matmul(w, rw, 1, n, pb, rpb)
                    ba = BFM[:, l, 2 * pi:2 * pi + 1]
                    bb = BFM[:, l, 2 * pi + 1:2 * pi + 2]
                    act(F1[:, 0:wd], pa[:, 0:wd], AF.Identity, [rpa, rBFM], [rF1], bias=ba)
                    act(SQH[:, 0:wd], pa[:, 0:wd], AF.Square, [rpa, rBFM], [rSQH], bias=ba)
                    act(F2[:, 0:wd], pb[:, 0:wd], AF.Identity, [rpb, rBFM], [rF2], bias=bb)
                    pss, rpss = PS[4], rPS[4]
                    mm(pss[:, 0:wd], BLKb, SQH[:, 0:wd], True, True, [rSQH, rCONSTB], [rpss])
                    act(RSQ[:, 0:wd], pss[:, 0:wd], AF.Sqrt, [rpss], [rRSQ], bias=EPS, scale=1.0 / HD)
                    recip(RSQ[:, 0:wd], RSQ[:, 0:wd], [rRSQ], [rRSQ])
                    stt("dve", T1[:, 0:wd], F1[:, 0:wd], GQK[:, l, gcol:gcol + 1], ROPE[:, 0, t0:t0 + wd],
                        ALU.mult, ALU.mult, [rF1, rGQK, rROPE], [rT1])
                    stt("dve", T2[:, 0:wd], F2[:, 0:wd], GQK[:, l, gcol + 1:gcol + 2], ROPE[:, 1, t0:t0 + wd],
                        ALU.mult, ALU.mult, [rF2, rGQK, rROPE], [rT2])
                    tt("dve", T1[:, 0:wd], T1[:, 0:wd], T2[:, 0:wd], ALU.add, [rT1, rT2], [rT1])
                    tt("dve", QK[:, pi, t0:t0 + wd], T1[:, 0:wd], RSQ[:, 0:wd], ALU.mult, [rT1, rRSQ], [rQK[pi][n]])
            else:
                for cc in range(2):
                    ci = (pi - 5) * 2 + cc
                    for n in range(5):
                        t0, wd = TT[n]
                        pa, rpa = psrot()
                        fm_matmul(w, rw, cc, n, pa, rpa)
                        act(QM[:, ci, t0:t0 + wd], pa[:, 0:wd], AF.Identity, [rpa, rBFM], [rQM[ci][n]],
                            bias=BFM[:, l, 10 + ci:11 + ci])

        if stop <= 2:
            return
        def tm_matmul(w, rw, off, wd, i, pst, rpst):
            t0 = i * 128
            c0 = hcol(t0)
            n = 0 if i < 2 else 1 + (i - 2) // 4
            for k in range(8):
                mm(pst[:, 0:wd], HB[:, k, c0:c0 + 128], w[:, k, 0:wd], k == 0, False, [rw, rHB[k][n]], [rpst])
            mm(pst[:, 0:wd], ONE1[0:1, :], BTM[0:1, off:off + wd], False, True, [rONE1, rBTM], [rpst])

        w, rw = load_w(wintm[l, :, :, 0:16], 16)
        for i in range(18):
            pa, rpa = psrot()
            tm_matmul(w, rw, 0, 16, i, pa, rpa)
            cp("dve", GT[:, i, :], pa[:, 0:16], [rpa], [rGT[i]])
            gv = GT[:, i, :].rearrange("p (d t h) -> p d t h", d=2, t=2)
            fpre = gv[:, :, 1, :]
            ipre = gv[:, :, 0, :]
            spl = SM[:, 0:8].rearrange("p (d h) -> p d h", d=2)
            act(spl, fpre, AF.Exp, [rGT[i]], [rSM], scale=-1.0)
            act(spl, spl, AF.Ln, [rSM], [rSM], bias=1.0)
            pg, rpg = PS[5], rPS[5]
            mm(pg[:, 0:4], TRIF, SM[:, 0:4], True, True, [rSM, rCONST], [rpg])
            mm(pg[:, 4:8], TRIB, SM[:, 4:8], True, True, [rSM, rCONST], [rpg])
            mm(pg[:, 8:16], NEG1, SM[:, 0:8], True, True, [rSM, rCONST], [rpg])
            a_ = SM[:, 8:16]
            tt("dve", a_.rearrange("p (d h) -> p d h", d=2), ipre, pg[:, 0:8].rearrange("p (d h) -> p d h", d=2),
               ALU.subtract, [rGT[i], rpg], [rSM])
            act(SCAL[:, i, 0:8], a_, AF.Exp, [rSM], [rSCAL[i]], bias=LN8)
            tt("dve", SM[:, 16:24], a_, pg[:, 8:16], ALU.add, [rSM, rpg], [rSM])
            act(SCAL[:, i, 8:16], SM[:, 16:24], AF.Exp, [rSM], [rSCAL[i]], bias=LN8)
            act(SCAL[:, i, 16:24], pg[:, 0:8], AF.Exp, [rpg], [rSCAL[i]], scale=-1.0)
            act(SCAL[:, i, 24:32], pg[:, 8:16], AF.Exp, [rpg], [rSCAL[i]])

        if stop <= 3:
            return
        w, rw = load_w(wintm[l, :, :, 16:528], 512)
        for i in range(18):
            pa, rpa = psrot()
            tm_matmul(w, rw, 16, 512, i, pa, rpa)
            if stop <= 3.1:
                continue
            mset("pool", VMAt2, 0.0, [rVMAt])
            mset("pool", VMAt[:, :, 64:65], 1.0, [rVMAt])
            cp("act", VMAt[:, :, 0:64], pa[:, 0:256].rearrange("p (h d) -> p h d", h=4), [rpa], [rVMAt])
            dma("sp", vmad[i], VMAt2, [rVMAt], [])
            if stop <= 3.2:
                continue
            for d_ in range(2):
                tt("dve", KH[:, d_, :].rearrange("p (h d) -> p h d", h=4),
                   pa[:, 256:512].rearrange("p (h d) -> p h d", h=4),
                   SCAL[:, i, 8 + 4 * d_:12 + 4 * d_].unsqueeze(2).to_broadcast([128, 4, 64]), ALU.mult,
                   [rpa, rSCAL[i], rVMAt], [rKH])
            if stop <= 3.3:
                continue
            for d_ in range(2):
                for pr in range(2):
                    mm(PS[6 + d_][:, pr * (2 * EW):(pr + 1) * (2 * EW)], KH[:, d_, pr * 128:(pr + 1) * 128],
                       VMAt[:, 2 * pr:2 * pr + 2, :], True, True, [rKH, rVMAt], [rPS[6 + d_]])
            if stop <= 3.4:
                continue
            for d_ in range(2):
                for pr in range(2):
                    for hh in range(2):
                        cp("act", DC[64 * hh:64 * hh + 64, d_ * 18 + i, pr * EW:(pr + 1) * EW],
                           PS[6 + d_][64 * hh:64 * hh + 64, pr * (2 * EW) + hh * EW:pr * (2 * EW) + hh * EW + EW],
                           [rPS[6 + d_]], [rDC[d_ * 18 + i]])
        if stop <= 3.5:
            return

        mset("dve", CST[:], 0.0, [rCST])
        orders = [list(range(18)), [1, 0] + list(range(17, 1, -1))]
        for d_ in range(2):
            eng = "dve"
            for i in orders[d_]:
                cp(eng, CBF[:, d_ * 18 + i, :], CST[:, d_, :], [rCST], [rCBF[d_ * 18 + i]])
                for pr in range(2):
                    for hh in range(2):
                        col = 24 + d_ * 4 + 2 * pr + hh
                        stt(eng, CST[64 * hh:64 * hh + 64, d_, pr * EW:(pr + 1) * EW],
                            CST[64 * hh:64 * hh + 64, d_, pr * EW:(pr + 1) * EW],
                            SCAL[64 * hh:64 * hh + 64, i, col:col + 1],
                            DC[64 * hh:64 * hh + 64, d_ * 18 + i, pr * EW:(pr + 1) * EW],
                            ALU.mult, ALU.add, [rCST, rSCAL[i], rDC[d_ * 18 + i]], [rCST])

        if stop <= 4:
            return
        w, rw = load_w(wintm[l, :, :, 528:912], 384)
        for i in range(18):
            pa, rpa = psrot()
            tm_matmul(w, rw, 528, 384, i, pa, rpa)
            cp("act", VA[:, i, :].rearrange("p (a b) -> p a b", a=2)[:, :, 0:64],
               pa[:, 0:128].rearrange("p (a b) -> p a b", a=2), [rpa], [rVA[i]])
            act(OMSt[:], pa[:, 128:384], AF.Sigmoid, [rpa], [rOMSt])
            dma("sp", omsd[i], OMSt[:], [rOMSt], [])

        w, rw = load_w(wintm[l, :, :, 912:1424], 512)
        for i in range(18):
            pa, rpa = psrot()
            tm_matmul(w, rw, 912, 512, i, pa, rpa)
            act(UG[:], pa[:, 0:512], AF.Gelu_apprx_tanh, [rpa], [rUG])
            ugv = UG[:, 256:512].rearrange("p (g d) -> p g d", g=4)
            tt("dve", T1[:, 0:256], UG[:, 256:512], UG[:, 256:512], ALU.mult, [rUG], [rT1])
            S.op("dve", lambda e, o=SM[:, 32:36], a=T1[:, 0:256].rearrange("p (g d) -> p g d", g=4):
                 e.tensor_reduce(out=o, in_=a, axis=AX.X, op=ALU.add), [rT1], [rSM])
            act(SM[:, 32:36], SM[:, 32:36], AF.Sqrt, [rSM], [rSM], bias=EPS, scale=1.0 / 64)
            recip(SM[:, 36:40], SM[:, 32:36], [rSM], [rSM])
            tt("dve", T1[:, 0:256].rearrange("p (g d) -> p g d", g=4), ugv,
               SM[:, 36:40].unsqueeze(2).to_broadcast([128, 4, 64]), ALU.mult, [rUG, rSM], [rT1])
            tt("dve", VCN[:], T1[:, 0:256], GMV[:, 1, :], ALU.mult, [rT1, rGMV], [rVCN])
            pz, rpz = PS[5], rPS[5]
            for g in range(4):
                mm(pz[:, g * 64:(g + 1) * 64], WSPb[:, g, :], VCN[:, g * 64:(g + 1) * 64], True, True,
                   [rWSP, rVCN], [rpz])
            tt("dve", T2[:, 0:256].rearrange("p (g d) -> p g d", g=4), pz[:, 0:256].rearrange("p (g d) -> p g d", g=4),
               BSP[:, l, :].unsqueeze(2).to_broadcast([128, 4, 64]), ALU.add, [rpz, rBSP], [rT2])
            tt("dve", CMt[:], T2[:, 0:256], UG[:, 0:256], ALU.mult, [rT2, rUG], [rCMt])
            dma("sp", cmd[i], CMt[:], [rCMt], [])

        if stop <= 5:
            return
        S.barrier()
        AR.off = mark_tmp
        T1 = AR.get([128, 512], F32); rT1 = Res()
        T2 = AR.get([128, 512], F32); rT2 = Res()
        AT = AR.get([128, 2, 4, 128], BF16); rAT = Res()
        PT = [AR.get([128, 512], BF16) for _ in range(2)]; rPT = [Res(), Res()]
        OTM = AR.get([128, 512], F32); rOTM = Res()
        VMAt2 = AR.get([128, (4 * EW)], BF16); VMAt = VMAt2.rearrange("p (h d) -> p h d", h=4); rVMAt = Res()
        OMSt = AR.get([128, 256], BF16); rOMSt = Res()
        CMt = AR.get([128, 256], BF16); rCMt = Res()
        RD = AR.get([128, 512], F32); rRD = Res()
        HS = AR.get([128, 256], F32); rHS = Res()
        rMIX = [[Res() for _ in range(18)] for _ in range(8)]

        def mixcols(i):
            return hcol(i * 128)

        qblocks = list(range(2, 18)) + ([] if last else [0, 1])
        its = []
        for qb in qblocks:
            ktiles = list(range(18)) if qb >= 2 else [0, 1]
            for j in range(2):
                for kk, kt in enumerate(ktiles):
                    its.append((qb, j, kk, kt, len(ktiles)))

        def a_score(n_):
            qb, j, kk, kt, nkt = its[n_]
            q0 = qb * 128
            nq = 0 if qb < 2 else 1 + (qb - 2) // 4
            nk = 0 if kt < 2 else 1 + (kt - 2) // 4
            pscore, rpscore = psrot()
            mm(pscore[:, 0:512], QK[64 * j:64 * j + 64, 4, kt * 128:(kt + 1) * 128],
               QK[64 * j:64 * j + 64, 0:4, q0:q0 + 128], True, True,
               [rQK[4][nk]] + [rQK[c][nq] for c in range(4)], [rpscore])
            pi_ = n_ % 2
            act(PT[pi_][:], pscore[:, 0:512], AF.Exp, [rpscore], [rPT[pi_]], scale=0.125)

        def a_pv(n_):
            qb, j, kk, kt, nkt = its[n_]
            po, rpo = PS[4 + j], rPS[4 + j]
            pi_ = n_ % 2
            mm(po[:, 0:512], VA[:, kt, j * 128:(j + 1) * 128], PT[pi_][:], kk == 0, kk == nkt - 1,
               [rVA[kt], rPT[pi_]], [rpo])
            if kk == nkt - 1:
                recip(RD[64:128, :], po[64:128, :], [rpo], [rRD])
                c0 = mixcols(qb)
                tt("dve", HB[64 * j:64 * j + 64, 0:4, c0:c0 + 128],
                   po[0:64, 0:512].rearrange("p (c t) -> p c t", c=4),
                   RD[64:128, :].rearrange("p (c t) -> p c t", c=4), ALU.mult, [rpo, rRD],
                   [rMIX[c][qb] for c in range(4)])

        for n_ in range(len(its)):
            a_score(n_)
            if n_ >= 1:
                a_pv(n_ - 1)
        a_pv(len(its) - 1)

        if stop <= 6:
            return
        tiles2 = list(range(18)) if not last else list(range(2, 18))
        for i in tiles2:
            t0 = i * 128
            n = 0 if i < 2 else 1 + (i - 2) // 4
            if True:
                dma("sp", VMAt2, vmad[i], [], [rVMAt])
                dma("sp", OMSt[:], omsd[i], [], [rOMSt])
                dma("sp", CMt[:], cmd[i], [], [rCMt])
            pscs = [psrot(), psrot()]
            for h in (0, 2, 1, 3):
                ci, hh = h // 2, h % 2
                mm(pscs[hh][0][:, ci * 128:(ci + 1) * 128], QM[64 * hh:64 * hh + 64, 2 + ci, t0:t0 + 128],
                   QM[64 * hh:64 * hh + 64, ci, t0:t0 + 128], True, True, [rQM[2 + ci][n], rQM[ci][n]], [pscs[hh][1]])
            for d_ in range(2):
                msk = MASKF if d_ == 0 else MASKB
                for h in range(4):
                    ci, hh = h // 2, h % 2
                    stt("dve", AT[:, d_, h, :], pscs[hh][0][:, ci * 128:(ci + 1) * 128],
                        SCAL[:, i, d_ * 4 + h:d_ * 4 + h + 1], msk, ALU.mult, ALU.mult,
                        [pscs[hh][1], rSCAL[i], rCONST], [rAT])
            if stop <= 6.1:
                continue
            for d_ in range(2):
                pout, rpout = PS[6 + d_], rPS[6 + d_]
                for h in range(4):
                    ci, hh = h // 2, h % 2
                    mm(pout[:, h * EW:(h + 1) * EW], AT[:, d_, h, :], VMAt[:, h, :], True, False,
                       [rAT, rVMAt], [rpout])
                    mm(pout[:, h * EW:(h + 1) * EW], QM[64 * hh:64 * hh + 64, ci, t0:t0 + 128],
                       CBF[64 * hh:64 * hh + 64, d_ * 18 + i, ci * EW:(ci + 1) * EW], False, True,
                       [rQM[ci][n], rCBF[d_ * 18 + i]], [rpout])
            if stop <= 6.2:
                continue
            for d_ in range(2):
                pout, rpout = PS[6 + d_], rPS[6 + d_]
                pv = pout[:, 0:(4 * EW)].rearrange("p (h e) -> p h e", h=4)
                den = SM[:, 40 + 4 * d_:44 + 4 * d_]
                act(den, pv[:, :, 64], AF.Abs, [rpout], [rSM])
                tt("dve", den, den, SCAL[:, i, 16 + 4 * d_:20 + 4 * d_], ALU.max, [rSM, rSCAL[i]], [rSM])
                recip(den, den, [rSM], [rSM])
                dst = HS if d_ == 0 else T1[:, 0:256]
                tt("dve", dst.rearrange("p (h e) -> p h e", h=4), pv[:, :, 0:64],
                   den.unsqueeze(2).to_broadcast([128, 4, 64]), ALU.mult, [rpout, rSM], [rHS if d_ == 0 else rT1])
            tt("dve", HS[:], HS[:], T1[:, 0:256], ALU.add, [rHS, rT1], [rHS])
            if stop <= 6.3:
                continue
            tt("dve", T2[:, 0:256], HS[:], HS[:], ALU.mult, [rHS], [rT2])
            S.op("dve", lambda e, o=SM[:, 48:52], a=T2[:, 0:256].rearrange("p (g d) -> p g d", g=4):
                 e.tensor_reduce(out=o, in_=a, axis=AX.X, op=ALU.add), [rT2], [rSM])
            act(SM[:, 48:52], SM[:, 48:52], AF.Sqrt, [rSM], [rSM], bias=EPS, scale=1.0 / 64)
            recip(SM[:, 52:56], SM[:, 48:52], [rSM], [rSM])
            tt("dve", HS[:].rearrange("p (h e) -> p h e", h=4), HS[:].rearrange("p (h e) -> p h e", h=4),
               SM[:, 52:56].unsqueeze(2).to_broadcast([128, 4, 64]), ALU.mult, [rHS, rSM], [rHS])
            tt("dve", HS[:], HS[:], GMV[:, 0, :], ALU.mult, [rHS, rGMV], [rHS])
            tt("dve", OTM[:, 0:256], HS[:], OMSt[:], ALU.mult, [rHS, rOMSt], [rOTM])
            cp("pool", OTM[:, 256:512], CMt[:], [rCMt], [rOTM])
            if stop <= 6.4:
                continue
            ptr, rptr = psrot()
            for c in range(4):
                S.op("pe", lambda e, o=ptr[:, c * 128:(c + 1) * 128], a=OTM[:, c * 128:(c + 1) * 128]:
                     e.transpose(out=o, in_=a, identity=IDENT), [rOTM, rCONST], [rptr])
            c0 = mixcols(i)
            cp("act", HB[:, 4:8, c0:c0 + 128], ptr[:, 0:512].rearrange("p (c t) -> p c t", c=4), [rptr],
               [rMIX[c][i] for c in range(4, 8)])

        if stop <= 7:
            return
        S.barrier()
        if dbg:
            dma("sp", mix_dbg[b], HB[:], [], [])
            S.barrier()
        AR.reset()
        WOUT = AR.get([128, 8, 1024], BF16); rWOUT = [Res(), Res()]
        SQ = AR.get([128, 8, 512], BF16)
        for hf in range(2):
            dma("pool", WOUT[:, :, hf * 512:(hf + 1) * 512], wout[l, :, :, hf * 512:(hf + 1) * 512], (), [rWOUT[hf]])
        tiles3 = list(range(5)) if not last else list(range(1, 5))
        for n in tiles3:
            t0, w = TT[n]
            c0 = hcol(t0)
            m = ctx_m if n == 0 else b
            xt, rxt = xt_next()
            dma("sp", xt[:, :, 0:w], xsrc[b, :, :, t0:t0 + w], [rXD[b][n]], [rxt])
            for mo in range(8):
                pa, rpa = psrot()
                for k in range(8):
                    mm(pa[:, 0:w], WOUT[:, k, mo * 128:(mo + 1) * 128], HB[:, k, c0:c0 + w], k == 0, k == 7,
                       [rWOUT[mo // 4], rHB[k][n]], [rpa])
                stt("dve", xt[:, mo, 0:w], pa[:, 0:w], MOD[:, l, 16 + mo, m:m + 1], xt[:, mo, 0:w],
                    ALU.mult, ALU.add, [rpa, rMOD, rxt], [rxt])
            dma("sp", xd[b, :, :, t0:t0 + w], xt[:, :, 0:w], [rxt], [rXD[b][n]])
            if dbg:
                dma("sp", xm_dbg[b, :, :, t0:t0 + w], xt[:, :, 0:w], [rxt], [])
            norm_tile(l, 1, m, n, xt, rxt, SQ)

        if stop <= 8:
            return
        S.barrier()
        AR.reset()
        WUP = AR.get([128, 8, 2816], BF16); rWUP = [Res() for _ in range(6)]
        WDN = AR.get([128, 11, 1024], BF16); rWDN = [Res(), Res()]
        U = AR.get([128, 11, 512], BF16); rU = [Res() for _ in range(11)]
        AV = [AR.get([128, 512], F32) for _ in range(2)]; rAV = [Res(), Res()]
        TG = [AR.get([128, 512], F32) for _ in range(2)]; rTG = [Res(), Res()]
        TV = [AR.get([128, 512], F32) for _ in range(2)]; rTV = [Res(), Res()]
        SG = [AR.get([128, 512], F32) for _ in range(2)]; rSG = [Res(), Res()]
        TX = AR.get([128, 512], F32); rTX = Res()
        wins = []
        if not last:
            wins.append((0, 258, 0, 0))
        for i in range(4):
            wins.append((258 + 510 * i, 512, 256 + 510 * i, None))
        wins.append((258 + 2040, 10, 256 + 2040, None))
        rXW = [Res() for _ in range(len(wins))]
        gi = [0]
        for hf in range(2):
            for pc in range(6):
                wd = 512 if pc < 5 else 256
                dma("pool", WUP[:, :, pc * 512:pc * 512 + wd], wup[l, hf, :, :, pc * 512:pc * 512 + wd], (), [rWUP[pc]])
            for pc in range(2):
                dma("pool", WDN[:, :, pc * 512:(pc + 1) * 512], wdown[l, hf, :, :, pc * 512:(pc + 1) * 512], (), [rWDN[pc]])
            for wi, (c0, w, tok0, _) in enumerate(wins):
                wo = w - 2
                m = ctx_m if tok0 < 256 else b
                hbres = [rHB[k][nn] for k in range(8) for nn in range(5)]
                for g in range(11):
                    pg_, rpg_ = psrot()
                    pv_, rpv_ = psrot()
                    for k in range(8):
                        mm(pg_[:, 0:w], WUP[:, k, g * 128:(g + 1) * 128], HB[:, k, c0:c0 + w], k == 0, k == 7,
                           [rWUP[(g * 128) // 512]] + (hbres if k == 0 else []), [rpg_])
                    for k in range(8):
                        mm(pv_[:, 0:w], WUP[:, k, 1408 + g * 128:1408 + (g + 1) * 128], HB[:, k, c0:c0 + w], k == 0, k == 7,
                           [rWUP[(1408 + g * 128) // 512]], [rpv_])
                    x_ = gi[0] % 2
                    gi[0] += 1
                    cg = CONVP[:, l, hf, g, :]
                    cv_ = CONVP[:, l, hf, 11 + g, :]
                    act(TG[x_][:, 0:wo], pg_[:, 1:1 + wo], AF.Identity, [rpg_, rCONVP], [rTG[x_]],
                        bias=cg[:, 3:4], scale=cg[:, 1:2])
                    stt("dve", TG[x_][:, 0:wo], pg_[:, 0:wo], cg[:, 0:1], TG[x_][:, 0:wo], ALU.mult, ALU.add,
                        [rpg_, rTG[x_]], [rTG[x_]])
                    stt("dve", TG[x_][:, 0:wo], pg_[:, 2:2 + wo], cg[:, 2:3], TG[x_][:, 0:wo], ALU.mult, ALU.add,
                        [rpg_, rTG[x_]], [rTG[x_]])
                    act(TV[x_][:, 0:wo], pv_[:, 1:1 + wo], AF.Identity, [rpv_, rCONVP], [rTV[x_]],
                        bias=cv_[:, 3:4], scale=cv_[:, 1:2])
                    stt("dve", TV[x_][:, 0:wo], pv_[:, 0:wo], cv_[:, 0:1], TV[x_][:, 0:wo], ALU.mult, ALU.add,
                        [rpv_, rTV[x_]], [rTV[x_]])
                    stt("dve", TV[x_][:, 0:wo], pv_[:, 2:2 + wo], cv_[:, 2:3], TV[x_][:, 0:wo], ALU.mult, ALU.add,
                        [rpv_, rTV[x_]], [rTV[x_]])
                    act(SG[x_][:, 0:wo], TG[x_][:, 0:wo], AF.Silu, [rTG[x_]], [rSG[x_]])
                    tt("dve", U[:, g, 0:wo], SG[x_][:, 0:wo], TV[x_][:, 0:wo], ALU.mult, [rSG[x_], rTV[x_]], [rU[g]])
                xt, rxt = xt_next()
                dma("sp", xt[:, :, 0:wo], xd[b, :, :, tok0:tok0 + wo], [rXW[wi]], [rxt])
                for mo in range(8):
                    pa, rpa = psrot()
                    for kc in range(11):
                        mm(pa[:, 0:wo], WDN[:, kc, mo * 128:(mo + 1) * 128], U[:, kc, 0:wo], kc == 0, kc == 10,
                           [rWDN[mo // 4], rU[kc]], [rpa])
                    stt("dve", xt[:, mo, 0:wo], pa[:, 0:wo], MOD[:, l, 40 + mo, m:m + 1], xt[:, mo, 0:wo],
                        ALU.mult, ALU.add, [rpa, rMOD, rxt], [rxt])
                if last and hf == 1:
                    tk = dma("sp", yout[b, :, :, tok0 - 256:tok0 - 256 + wo], xt[:, :, 0:wo], [rxt], [rXW[wi]])
                    out_toks.append(tk)
                else:
                    dma("sp", xd[b, :, :, tok0:tok0 + wo], xt[:, :, 0:wo], [rxt], [rXW[wi]])
        S.barrier()

    out_toks = []
    for b in range(nb):
        for l in range(nlayers):
            if stop > 0:
                layer(b, l)
    S.barrier()
    S.emit(nc, st)
    st.close()
    return nc


def _rope_tables():
    n = S_LAT
    grid_w = 64
    nf = HD // 4
    t = np.arange(n)
    row = (t // grid_w).astype(np.float32)
    col = (t % grid_w).astype(np.float32)
    inv = (10000.0 ** (-np.arange(nf, dtype=np.float32) / nf)).astype(np.float32)
    ang = np.concatenate([row[:, None] * inv[None], col[:, None] * inv[None]], axis=-1)
    cos = np.cos(ang).astype(np.float32).reshape(n, 2, nf)
    sin = np.sin(ang).astype(np.float32).reshape(n, 2, nf)
    C = np.ones((HD, NT), np.float32)
    Sg = np.zeros((HD, NT), np.float32)
    for ax in range(2):
        for half in range(2):
            rows = slice(ax * 32 + half * 16, ax * 32 + half * 16 + 16)
            C[rows, T_CTX:] = cos[:, ax, :].T
            Sg[rows, T_CTX:] = (-1.0 if half == 0 else 1.0) * sin[:, ax, :].T
    tab = np.stack([np.concatenate([C, C], 0), np.concatenate([Sg, Sg], 0)], axis=1)
    return np.ascontiguousarray(tab)


def _consts():
    c = np.zeros((128, 8, 128), np.float32)
    r = np.arange(128)[:, None]
    s = np.arange(128)[None, :]
    c[:, 0, :] = (r == s)
    c[:, 1, :] = -1.0 * (r <= s)
    c[:, 2, :] = -1.0 * (r >= s)
    c[:, 3, :] = -1.0
    c[:, 4, :] = (r <= s)
    c[:, 5, :] = (r >= s)
    c[:, 6, :] = 1.0
    c[:, 7, :] = ((r // 64) == (s // 64))
    return c


def _swap_idx():
    d = np.arange(HD)
    ax, half, f = d // 32, (d // 16) % 2, d % 16
    return ax * 32 + (1 - half) * 16 + f


def _prep_shared(inp):
    f = np.float32
    w_ada = inp["w_ada"]; w_in = inp["w_in"]; b_in = inp["b_in"]
    sh = {}
    sh["wada"] = np.ascontiguousarray(w_ada.reshape(DEPTH, 8, 128, 12, 512).transpose(0, 3, 2, 1, 4))
    sh["bada"] = np.ascontiguousarray(inp["b_ada"].reshape(DEPTH, 48, 128).transpose(2, 0, 1))
    gn = np.stack([inp["g_norm1"], inp["g_norm2"]], axis=1)
    sh["gn12"] = np.ascontiguousarray(gn.reshape(DEPTH, 2, 8, 128).transpose(3, 0, 1, 2))
    sw = _swap_idx()
    QA, KA, VAo, QMo, KMo, VMo, OMo, GTo, UCo, VCo = 0, 512, 640, 768, 1024, 1280, 1536, 1792, 1808, 2064
    fm_cols = []
    for c in range(4):
        h0, h1 = c, 4 + c
        q = np.concatenate([QA + h0 * 64 + np.arange(64), QA + h1 * 64 + np.arange(64)])
        qs = np.concatenate([QA + h0 * 64 + sw, QA + h1 * 64 + sw])
        fm_cols += [q, qs]
    k = np.concatenate([KA + np.arange(64), KA + 64 + np.arange(64)])
    ks = np.concatenate([KA + sw, KA + 64 + sw])
    fm_cols += [k, ks]
    fm_cols += [QMo + np.arange(128), QMo + 128 + np.arange(128), KMo + np.arange(128), KMo + 128 + np.arange(128)]
    fm_idx = np.concatenate(fm_cols)
    wfm = w_in[:, :, fm_idx]
    sh["winfm"] = np.ascontiguousarray(wfm.reshape(DEPTH, 8, 128, 7, 256).transpose(0, 3, 2, 1, 4))
    sh["binfm"] = np.ascontiguousarray(b_in[:, fm_idx].reshape(DEPTH, 14, 128).transpose(2, 0, 1))
    tm_idx = np.concatenate([GTo + np.arange(16), VMo + np.arange(256), KMo + np.arange(256), VAo + np.arange(128),
                             OMo + np.arange(256), UCo + np.arange(256), VCo + np.arange(256)])
    wtm = w_in[:, :, tm_idx]
    sh["wintm"] = np.ascontiguousarray(wtm.reshape(DEPTH, 8, 128, NTM).transpose(0, 2, 1, 3))
    sh["bintm"] = np.ascontiguousarray(b_in[:, tm_idx][None])
    gq = inp["g_q"]; gk = inp["g_k"]
    g4 = np.stack([np.tile(gq, (1, 2)), np.tile(gq[:, sw], (1, 2)), np.tile(gk, (1, 2)), np.tile(gk[:, sw], (1, 2))], axis=2)
    sh["gqk"] = np.ascontiguousarray(g4.transpose(1, 0, 2))
    sh["rope"] = _rope_tables()
    sh["gmv"] = np.ascontiguousarray(np.stack([inp["g_mh"], inp["g_v"]], axis=1))
    sh["wsp"] = np.ascontiguousarray(inp["w_sp"].transpose(0, 3, 1, 2))
    sh["bsp"] = np.ascontiguousarray(inp["b_sp"].transpose(2, 0, 1))
    rows = []
    for c in range(4):
        rows += [c * 64 + np.arange(64), (4 + c) * 64 + np.arange(64)]
    rows += [512 + np.arange(512)]
    ridx = np.concatenate(rows)
    wo = inp["w_out"][:, ridx, :]
    sh["wout"] = np.ascontiguousarray(wo.reshape(DEPTH, 8, 128, 1024).transpose(0, 2, 1, 3))
    w_up = inp["w_up"]
    ucols = np.stack([np.concatenate([hf * 1408 + np.arange(1408), DFF + hf * 1408 + np.arange(1408)]) for hf in range(2)])
    wu = w_up[:, :, ucols]
    sh["wup"] = np.ascontiguousarray(wu.reshape(DEPTH, 8, 128, 2, 2816).transpose(0, 3, 2, 1, 4))
    cw = inp["conv_w"]; cb = inp["conv_b"]
    cpar = np.concatenate([cw, cb[:, None, :]], axis=1)
    cpar = cpar[:, :, ucols]
    sh["convp"] = np.ascontiguousarray(cpar.reshape(DEPTH, 4, 2, 22, 128).transpose(4, 0, 2, 3, 1))
    sh["wdown"] = np.ascontiguousarray(inp["w_down"].reshape(DEPTH, 2, 11, 128, 1024).transpose(0, 1, 3, 2, 4))
    sh["consts"] = _consts()
    return {k: np.ascontiguousarray(v, dtype=f) for k, v in sh.items()}


def _prep_core(inp, core):
    b0 = 2 * core
    xs = []
    for b in (b0, b0 + 1):
        xcat = np.concatenate([inp["ctx"][b], inp["x"][b]], axis=0)
        xs.append(xcat.T.reshape(8, 128, NT).transpose(1, 0, 2))
    xin = np.ascontiguousarray(np.stack(xs), dtype=np.float32)
    cv = np.stack([inp["c"][b0], inp["c"][b0 + 1], inp["c_ctx"]], axis=1)
    cvec = np.ascontiguousarray(cv.reshape(8, 128, 3).transpose(1, 0, 2), dtype=np.float32)
    return {"xin": xin, "cvec": cvec}


_NC_CACHE = {}


def kernel(**inputs):
    inp = {k: np.asarray(v) for k, v in inputs.items()}
    shared = _prep_shared(inp)
    if "nc" not in _NC_CACHE:
        _NC_CACHE["nc"] = build()
    nc = _NC_CACHE["nc"]
    in_maps = []
    for core in range(8):
        m = dict(shared)
        m.update(_prep_core(inp, core))
        in_maps.append(m)
    res = run_bass_kernel_spmd(nc, in_maps, core_ids=list(range(8)))
    out = np.empty((16, S_LAT, D), np.float32)
    for core in range(8):
        y = np.asarray(res.results[core]["yout"])
        for i in range(2):
            out[2 * core + i] = y[i].transpose(1, 0, 2).reshape(D, S_LAT).T
    return out
```

```python
import contextlib
import numpy as np
import concourse.bass as bass
import concourse.mybir as mybir
from concourse.bass_utils import run_bass_kernel_spmd

F32 = mybir.dt.float32
BF16 = mybir.dt.bfloat16
AF = mybir.ActivationFunctionType
ALU = mybir.AluOpType
AX = mybir.AxisListType

D = 1024
S_LAT = 2048
T_CTX = 256
NT = 2304
DEPTH = 4
DFF = 2816
HD = 64
EPS = 1e-6
NTM = 1424
EW = 66
HBW = 2308
TT = [(0, 256), (256, 512), (768, 512), (1280, 512), (1792, 512)]
LN8 = float(np.log(0.125))


def hcol(t):
    return t + 1 if t < 256 else t + 3


class Tok:
    __slots__ = ("eng", "seq", "clk", "needed", "sem", "val")

    def __init__(self, eng, seq, clk):
        self.eng = eng
        self.seq = seq
        self.clk = clk
        self.needed = False
        self.sem = None
        self.val = 0


class Res:
    __slots__ = ("w", "r", "excl")

    def __init__(self, excl=False):
        self.w = None
        self.r = {}
        self.excl = excl


class Sched:
    ENGS = ("pe", "act", "dve", "pool", "sp")

    def __init__(self, n_dma_sems=8):
        self.q = {e: [] for e in self.ENGS}
        self.clock = {e: {} for e in self.ENGS}
        self.seq = {e: 0 for e in self.ENGS}
        self.toks = {e: [] for e in self.ENGS}
        self.n_dma = n_dma_sems
        self.dma_rr = {e: 0 for e in self.ENGS}
        self.dma_last = {}

    def _merge(self, clk, t):
        for k, v in t.clk.items():
            if clk.get(k, 0) < v:
                clk[k] = v
        if clk.get(t.eng, 0) < t.seq:
            clk[t.eng] = t.seq

    def _deps(self, eng, reads, writes):
        clk = self.clock[eng]
        waits = []
        cand = []
        for r in reads:
            if r.w is not None:
                cand.append(r.w)
        for w in writes:
            if w.w is not None:
                cand.append(w.w)
            cand.extend(w.r.values())
        for t in cand:
            if eng == "pe" and t.eng == "pe":
                continue
            if clk.get(t.eng, 0) < t.seq:
                waits.append(t)
                t.needed = True
                self._merge(clk, t)
        return waits

    def _mark(self, tok, reads, writes):
        for r in reads:
            r.r[tok.eng] = tok
        for w in writes:
            w.w = tok
            w.r = {}

    def op(self, eng, fn, reads=(), writes=()):
        ex = [r for r in reads if r.excl]
        if ex:
            writes = list(writes) + ex
        waits = self._deps(eng, reads, writes)
        self.seq[eng] += 1
        tok = Tok(eng, self.seq[eng], dict(self.clock[eng]))
        self.toks[eng].append(tok)
        self.q[eng].append((waits, fn, tok, False))
        self._mark(tok, reads, writes)
        return tok

    def dma(self, eng, fn, reads=(), writes=()):
        waits = self._deps(eng, reads, writes)
        j = self.dma_rr[eng]
        self.dma_rr[eng] = (j + 1) % self.n_dma
        key = ("dma", eng, j)
        last = self.dma_last.get(key)
        clk = self.clock[eng]
        if last is not None and clk.get(key, 0) < last.seq:
            waits.append(last)
            self._merge(clk, last)
        seq = (last.seq if last is not None else 0) + 1
        tok = Tok(key, seq, dict(clk))
        tok.needed = True
        self.dma_last[key] = tok
        self.q[eng].append((waits, fn, tok, True))
        self._mark(tok, reads, writes)
        return tok

    def barrier(self):
        lasts = [self.toks[e][-1] for e in self.ENGS if self.toks[e]]
        lasts += list(self.dma_last.values())
        for e in self.ENGS:
            clk = self.clock[e]
            waits = []
            for t in lasts:
                if e == "pe" and t.eng == "pe":
                    continue
                if clk.get(t.eng, 0) < t.seq:
                    waits.append(t)
                    t.needed = True
                    self._merge(clk, t)
            if waits:
                self.q[e].append((waits, None, None, False))

    def emit(self, nc, stack):
        esem = {}
        for e in self.ENGS:
            esem[e] = stack.enter_context(nc.semaphore("s_" + e))
            c = 0
            for t in self.toks[e]:
                if t.needed:
                    c += 1
                t.val = c
                t.sem = esem[e]
        dsem = {}
        for key in self.dma_last:
            dsem[key] = stack.enter_context(nc.semaphore("d_%s_%d" % (key[1], key[2])))
        block = stack.enter_context(nc.Block())
        q = self.q

        def run(e, engobj):
            for (waits, fn, tok, is_dma) in q[e]:
                for t in waits:
                    if isinstance(t.eng, tuple):
                        engobj.wait_ge(dsem[t.eng], 16 * t.seq)
                    else:
                        engobj.wait_ge(t.sem, t.val)
                if fn is None:
                    continue
                ins = fn(engobj)
                if is_dma:
                    ins.then_inc(dsem[tok.eng], 16)
                elif tok.needed:
                    ins.then_inc(tok.sem, 1)

        @block.tensor
        def _(e):
            run("pe", e)

        @block.scalar
        def _(e):
            run("act", e)

        @block.vector
        def _(e):
            run("dve", e)

        @block.gpsimd
        def _(e):
            run("pool", e)

        @block.sync
        def _(e):
            run("sp", e)


def build(nlayers=DEPTH, nb=2, dbg=(), stop=99):
    nc = bass.Bass("TRN2", target_bir_lowering=False)
    S = Sched()
    st = contextlib.ExitStack()

    def din(name, shape, dt=F32):
        return nc.dram_tensor(name, list(shape), dt, kind="ExternalInput").ap()

    def dscr(name, shape, dt):
        return nc.dram_tensor(name, list(shape), dt, kind="ExternalOutput").ap()

    xin = din("xin", [2, 128, 8, NT])
    cvec = din("cvec", [128, 8, 3])
    wada = din("wada", [DEPTH, 12, 128, 8, 512])
    bada = din("bada", [128, DEPTH, 48])
    gn12 = din("gn12", [128, DEPTH, 2, 8])
    winfm = din("winfm", [DEPTH, 7, 128, 8, 256])
    binfm = din("binfm", [128, DEPTH, 14])
    wintm = din("wintm", [DEPTH, 128, 8, NTM])
    bintm = din("bintm", [1, DEPTH, NTM])
    gqk = din("gqk", [128, DEPTH, 4])
    rope = din("rope", [128, 2, NT])
    gmv = din("gmv", [DEPTH, 2, 256])
    wsp = din("wsp", [DEPTH, 128, 4, 128])
    bsp = din("bsp", [128, DEPTH, 4])
    wout = din("wout", [DEPTH, 128, 8, 1024])
    wup = din("wup", [DEPTH, 2, 128, 8, 2816])
    convp = din("convp", [128, DEPTH, 2, 22, 4])
    wdown = din("wdown", [DEPTH, 2, 128, 11, 1024])
    consts = din("consts", [128, 8, 128])
    yout = nc.dram_tensor("yout", [2, 128, 8, S_LAT], F32, kind="ExternalOutput").ap()
    if dbg:
        xd = nc.dram_tensor("xd", [2, 128, 8, NT], F32, kind="ExternalOutput").ap()
        xm_dbg = nc.dram_tensor("xm_dbg", [2, 128, 8, NT], F32, kind="ExternalOutput").ap()
        mix_dbg = nc.dram_tensor("mix_dbg", [2, 128, 8, HBW], BF16, kind="ExternalOutput").ap()
    else:
        xd = dscr("xd", [2, 128, 8, NT], F32)
    vmad = dscr("vmad", [18, 128, (4 * EW)], BF16)
    omsd = dscr("omsd", [18, 128, 256], BF16)
    cmd = dscr("cmd", [18, 128, 256], BF16)
    dbg_out = {}

    def sb(name, shape, dt=F32):
        return st.enter_context(nc.sbuf_tensor(name, list(shape), dt))

    def psb(name):
        return st.enter_context(nc.psum_tensor(name, [128, 512], F32))

    def mm(out, lhsT, rhs, start, stop, rd, wr):
        S.op("pe", lambda e, o=out, l=lhsT, r=rhs, a=start, b=stop: e.matmul(o, lhsT=l, rhs=r, start=a, stop=b), rd, wr)

    def act(out, in_, func, rd, wr, bias=None, scale=None):
        kw = {}
        if bias is not None:
            kw["bias"] = bias
        if scale is not None:
            kw["scale"] = scale
        S.op("act", lambda e, o=out, i=in_, f=func, k=kw: e.activation(out=o, in_=i, func=f, **k), rd, wr)

    def tt(eng, out, in0, in1, op, rd, wr):
        S.op(eng, lambda e, o=out, a=in0, b=in1, p=op: e.tensor_tensor(out=o, in0=a, in1=b, op=p), rd, wr)

    def ts(eng, out, in0, s1, op0, rd, wr, s2=None, op1=None):
        if op1 is None:
            S.op(eng, lambda e, o=out, a=in0, x=s1, p=op0: e.tensor_scalar(out=o, in0=a, scalar1=x, scalar2=None, op0=p), rd, wr)
        else:
            S.op(eng, lambda e, o=out, a=in0, x=s1, y=s2, p=op0, q=op1: e.tensor_scalar(out=o, in0=a, scalar1=x, scalar2=y, op0=p, op1=q), rd, wr)

    def stt(eng, out, in0, scalar, in1, op0, op1, rd, wr, tmp=None):
        if eng == "pool":
            t_ = out if tmp is None else tmp
            S.op(eng, lambda e, o=t_, a=in0, x=scalar, p=op0: e.tensor_scalar(out=o, in0=a, scalar1=x, scalar2=None, op0=p), rd, wr)
            S.op(eng, lambda e, o=out, a=t_, b=in1, p=op1: e.tensor_tensor(out=o, in0=a, in1=b, op=p), rd, wr)
            return
        S.op(eng, lambda e, o=out, a=in0, s=scalar, b=in1, p=op0, q=op1: e.scalar_tensor_tensor(out=o, in0=a, scalar=s, in1=b, op0=p, op1=q), rd, wr)

    def cp(eng, out, in_, rd, wr):
        if eng == "act":
            S.op(eng, lambda e, o=out, i=in_: e.activation(out=o, in_=i, func=AF.Identity), rd, wr)
        else:
            S.op(eng, lambda e, o=out, i=in_: e.tensor_copy(out=o, in_=i), rd, wr)

    def recip(out, in_, rd, wr):
        S.op("dve", lambda e, o=out, i=in_: e.reciprocal(out=o, in_=i), rd, wr)

    def mset(eng, ap, val, wr):
        S.op(eng, lambda e, a=ap, v=val: e.memset(a, v), (), wr)

    def dma(eng, out, in_, rd, wr):
        return S.dma(eng, lambda e, o=out, i=in_: e.dma_start(out=o, in_=i), rd, wr)

    CONST = sb("CONST", [128, 8, 128]); rCONST = Res()
    CONSTB = sb("CONSTB", [128, 2, 128], BF16); rCONSTB = Res()
    ONE1 = sb("ONE1", [1, 128]); rONE1 = Res()
    CV = sb("CV", [128, 8, 3]); SCb = sb("SCb", [128, 8, 3], BF16); rCV = Res(); rSCb = Res()
    MOD = sb("MOD", [128, DEPTH, 48, 3]); rMOD = Res()
    BADA = sb("BADA", [128, DEPTH, 48]); rBADA = Res()
    GN = sb("GN", [128, DEPTH, 2, 8]); rGN = Res()
    A12 = sb("A12", [128, DEPTH, 2, 8, 3]); rA12 = Res()
    BFM = sb("BFM", [128, DEPTH, 14]); rBFM = Res()
    BTM = sb("BTM", [1, NTM]); rBTM = Res()
    GQK = sb("GQK", [128, DEPTH, 4]); rGQK = Res()
    BSP = sb("BSP", [128, DEPTH, 4]); rBSP = Res()
    CONVP = sb("CONVP", [128, 2, 22, 4]); rCONVP = Res()
    ROPE = sb("ROPE", [128, 2, NT], BF16); rROPE = Res()
    GMV = sb("GMV", [128, 2, 256]); rGMV = Res()
    WSPb = sb("WSPb", [128, 4, 128], BF16); rWSP = Res()
    HB = sb("HB", [128, 8, HBW], BF16)
    rHB = [[Res() for _ in range(5)] for _ in range(8)]
    XT = [sb("XT%d" % i, [128, 8, 512]) for i in range(2)]; rXT = [Res(), Res()]
    rSQ = Res()
    SD = sb("SD", [128, 512]); rSD = Res()
    RS = sb("RS", [128, 512]); rRS = Res()
    WP = [sb("WP%d" % i, [128, 8, 512], BF16) for i in range(2)]; rWP = [Res(), Res()]
    ARENA = sb("ARENA", [128, 46464], BF16)
    PS = [psb("PS%d" % i) for i in range(8)]
    rPS = [Res(excl=True) for _ in range(8)]

    IDENT = CONST[:, 0, :]
    TRIF = CONST[:, 1, :]
    TRIB = CONST[:, 2, :]
    NEG1 = CONST[:, 3, :]
    MASKF = CONST[:, 4, :]
    MASKB = CONST[:, 5, :]
    ONESb = CONSTB[:, 0, :]
    BLKb = CONSTB[:, 1, :]

    class Arena:
        def __init__(self):
            self.off = 0

        def reset(self):
            self.off = 0

        def get(self, shape, dt):
            n = int(np.prod(shape[1:]))
            if dt == F32:
                n2 = 2 * n
            else:
                n2 = n
            o = self.off + (self.off % 2)
            self.off = o + n2 + (n2 % 2)
            assert self.off <= 46464, ("arena overflow", self.off)
            v = ARENA[:, o:o + n2]
            if dt == F32:
                v = v.bitcast(F32)
            v = v[0:shape[0]]
            if len(shape) == 3:
                v = v.rearrange("p (a b) -> p a b", a=shape[1])
            elif len(shape) == 4:
                v = v.rearrange("p (a b c) -> p a b c", a=shape[1], b=shape[2])
            return v

    AR = Arena()
    wp_i = [0]

    def load_w(src_ap, width, kc=8):
        i = wp_i[0] % 2
        wp_i[0] += 1
        dma("pool", WP[i][:, 0:kc, 0:width], src_ap, (), [rWP[i]])
        return WP[i], rWP[i]

    ps_i = [0]

    def psrot(n=4):
        i = ps_i[0] % n
        ps_i[0] += 1
        return PS[i], rPS[i]

    dma("sp", CONST[:], consts[:, :, :], (), [rCONST])
    cp("dve", CONSTB[:], CONST[:, 6:8, :], [rCONST], [rCONSTB])
    mset("dve", ONE1[:], 1.0, [rONE1])
    dma("sp", CV[:], cvec[:, :, :], (), [rCV])
    dma("sp", BADA[:], bada[:, :, :], (), [rBADA])
    dma("sp", GN[:], gn12[:, :, :, :], (), [rGN])
    dma("sp", BFM[:], binfm[:, :, :], (), [rBFM])
    dma("sp", GQK[:], gqk[:, :, :], (), [rGQK])
    dma("sp", BSP[:], bsp[:, :, :], (), [rBSP])
    dma("pool", ROPE[:], rope[:, :, :], (), [rROPE])
    act(CV[:], CV[:], AF.Silu, [rCV], [rCV])
    cp("dve", SCb[:], CV[:], [rCV], [rSCb])
    for k in range(8):
        mset("pool", HB[:, k, 0:1], 0.0, [rHB[k][0]])
        mset("pool", HB[:, k, 257:259], 0.0, [rHB[k][0]])
        mset("pool", HB[:, k, 2307:2308], 0.0, [rHB[k][4]])

    for l in range(nlayers):
        pm = PS[0][:, 0:144]
        for pc in range(12):
            w, rw = load_w(wada[l, pc], 512)
            for j in range(4):
                jj = pc * 4 + j
                for k in range(8):
                    mm(pm[:, jj * 3:jj * 3 + 3], w[:, k, j * 128:(j + 1) * 128], SCb[:, k, :], k == 0, k == 7,
                       [rw, rSCb], [rPS[0]])
        tt("dve", MOD[:, l, :, :], pm.rearrange("p (a b) -> p a b", b=3),
           BADA[:, l, :].unsqueeze(2).to_broadcast([128, 48, 3]), ALU.add, [rPS[0], rBADA], [rMOD])
        for i in range(2):
            ts("dve", A12[:, l, i, :, :], MOD[:, l, 8 + 24 * i:16 + 24 * i, :], 1.0, ALU.add, [rMOD], [rA12])
            tt("dve", A12[:, l, i, :, :], A12[:, l, i, :, :],
               GN[:, l, i, :].unsqueeze(2).to_broadcast([128, 8, 3]), ALU.mult, [rA12, rGN], [rA12])

    def norm_tile(l, which, m, n, xt, rxt, SQ):
        t0, w = TT[n]
        c0 = hcol(t0)
        act(SQ[:, :, 0:w], xt[:, :, 0:w], AF.Square, [rxt], [rSQ])
        pss, rpss = PS[4], rPS[4]
        for k in range(8):
            mm(pss[:, 0:w], ONESb, SQ[:, k, 0:w], k == 0, k == 7, [rSQ, rCONSTB], [rpss])
        act(SD[:, 0:w], pss[:, 0:w], AF.Sqrt, [rpss], [rSD], bias=EPS, scale=1.0 / D)
        recip(RS[:, 0:w], SD[:, 0:w], [rSD], [rRS])
        tt("dve", xt[:, :, 0:w], xt[:, :, 0:w], RS[:, 0:w].unsqueeze(1).to_broadcast([128, 8, w]), ALU.mult,
           [rxt, rRS], [rxt])
        sh = 0 if which == 0 else 24
        for k in range(8):
            act(HB[:, k, c0:c0 + w], xt[:, k, 0:w], AF.Identity, [rxt, rA12, rMOD], [rHB[k][n]],
                bias=MOD[:, l, sh + k, m:m + 1], scale=A12[:, l, which, k, m:m + 1])

    xt_i = [0]

    def xt_next():
        i = xt_i[0] % 2
        xt_i[0] += 1
        return XT[i], rXT[i]

    rXD = [[Res() for _ in range(5)] for _ in range(2)]

    def layer(b, l):
        last = (l == DEPTH - 1)
        xsrc = xin if l == 0 else xd
        ctx_m = 2
        dma("sp", GMV[:, 0, :], gmv[l, 0:1, :].partition_broadcast(128), (), [rGMV])
        dma("sp", GMV[:, 1, :], gmv[l, 1:2, :].partition_broadcast(128), (), [rGMV])
        dma("pool", WSPb[:], wsp[l], (), [rWSP])
        dma("sp", BTM[:], bintm[0:1, l, :], (), [rBTM])
        dma("sp", CONVP[:], convp[:, l, :, :, :], (), [rCONVP])
        S.barrier()
        AR.reset()
        QK = AR.get([128, 5, NT], BF16); rQK = [[Res() for _ in range(5)] for _ in range(5)]
        QM = AR.get([128, 4, NT], BF16); rQM = [[Res() for _ in range(5)] for _ in range(4)]
        VA = AR.get([128, 18, 256], BF16); rVA = [Res() for _ in range(18)]
        GT = AR.get([128, 18, 16], F32); rGT = [Res() for _ in range(18)]
        DC = AR.get([128, 36, (2 * EW)], BF16); rDC = [Res() for _ in range(36)]
        CBF = AR.get([128, 36, (2 * EW)], BF16); rCBF = [Res() for _ in range(36)]
        SCAL = AR.get([128, 18, 32], F32); rSCAL = [Res() for _ in range(18)]
        SM = AR.get([128, 64], F32); rSM = Res()
        CST = AR.get([128, 2, (2 * EW)], F32); rCST = Res()
        mark_tmp = AR.off
        SQ = AR.get([128, 8, 512], BF16)
        for n in range(5):
            t0, w = TT[n]
            m = ctx_m if n == 0 else b
            xt, rxt = xt_next()
            dma("sp", xt[:, :, 0:w], xsrc[b, :, :, t0:t0 + w], [rXD[b][n]], [rxt])
            norm_tile(l, 0, m, n, xt, rxt, SQ)
        S.barrier()
        AR.off = mark_tmp
        F1 = AR.get([128, 512], F32); rF1 = Res()
        F2 = AR.get([128, 512], F32); rF2 = Res()
        SQH = AR.get([128, 512], BF16); rSQH = Res()
        RSQ = AR.get([128, 512], F32); rRSQ = Res()
        T1 = AR.get([128, 512], F32); rT1 = Res()
        T2 = AR.get([128, 512], F32); rT2 = Res()
        UG = AR.get([128, 512], F32); rUG = Res()
        VCN = AR.get([128, 256], BF16); rVCN = Res()
        KH = AR.get([128, 2, 256], BF16); rKH = Res()
        VMAt2 = AR.get([128, (4 * EW)], BF16); VMAt = VMAt2.rearrange("p (h d) -> p h d", h=4); rVMAt = Res()
        OMSt = AR.get([128, 256], BF16); rOMSt = Res()
        CMt = AR.get([128, 256], BF16); rCMt = Res()

        if stop <= 1:
            return
        for i in range(18):
            mset("pool", VA[:, i, :].rearrange("p (a b) -> p a b", a=2)[:, :, 64:128], 1.0, [rVA[i]])

        def fm_matmul(w, rw, cc, n, pst, rpst):
            t0, wd = TT[n]
            c0 = hcol(t0)
            for k in range(8):
                mm(pst[:, 0:wd], w[:, k, cc * 128:(cc + 1) * 128], HB[:, k, c0:c0 + wd], k == 0, k == 7,
                   [rw, rHB[k][n]], [rpst])

        for pi in range(7):
            w, rw = load_w(winfm[l, pi], 256)
            if pi < 5:
                gcol = 0 if pi < 4 else 2
                for n in range(5):
                    t0, wd = TT[n]
                    pa, rpa = psrot()
                    pb, rpb = psrot()
                    fm_matmul(w, rw, 0, n, pa, rpa)
                    fm_matmul(w, rw, 1, n, pb, rpb)
                    ba = BFM[:, l, 2 * pi:2 * pi + 1]
                    bb = BFM[:, l, 2 * pi + 1:2 * pi + 2]
                    act(F1[:, 0:wd], pa[:, 0:wd], AF.Identity, [rpa, rBFM], [rF1], bias=ba)
                    act(SQH[:, 0:wd], pa[:, 0:wd], AF.Square, [rpa, rBFM], [rSQH], bias=ba)
                    act(F2[:, 0:wd], pb[:, 0:wd], AF.Identity, [rpb, rBFM], [rF2], bias=bb)
                    pss, rpss = PS[4], rPS[4]
                    mm(pss[:, 0:wd], BLKb, SQH[:, 0:wd], True, True, [rSQH, rCONSTB], [rpss])
                    act(RSQ[:, 0:wd], pss[:, 0:wd], AF.Sqrt, [rpss], [rRSQ], bias=EPS, scale=1.0 / HD)
                    recip(RSQ[:, 0:wd], RSQ[:, 0:wd], [rRSQ], [rRSQ])
                    stt("dve", T1[:, 0:wd], F1[:, 0:wd], GQK[:, l, gcol:gcol + 1], ROPE[:, 0, t0:t0 + wd],
                        ALU.mult, ALU.mult, [rF1, rGQK, rROPE], [rT1])
                    stt("dve", T2[:, 0:wd], F2[:, 0:wd], GQK[:, l, gcol + 1:gcol + 2], ROPE[:, 1, t0:t0 + wd],
                        ALU.mult, ALU.mult, [rF2, rGQK, rROPE], [rT2])
                    tt("dve", T1[:, 0:wd], T1[:, 0:wd], T2[:, 0:wd], ALU.add, [rT1, rT2], [rT1])
                    tt("dve", QK[:, pi, t0:t0 + wd], T1[:, 0:wd], RSQ[:, 0:wd], ALU.mult, [rT1, rRSQ], [rQK[pi][n]])
            else:
                for cc in range(2):
                    ci = (pi - 5) * 2 + cc
                    for n in range(5):
                        t0, wd = TT[n]
                        pa, rpa = psrot()
                        fm_matmul(w, rw, cc, n, pa, rpa)
                        act(QM[:, ci, t0:t0 + wd], pa[:, 0:wd], AF.Identity, [rpa, rBFM], [rQM[ci][n]],
                            bias=BFM[:, l, 10 + ci:11 + ci])

        if stop <= 2:
            return
        def tm_matmul(w, rw, off, wd, i, pst, rpst):
            t0 = i * 128
            c0 = hcol(t0)
            n = 0 if i < 2 else 1 + (i - 2) // 4
            for k in range(8):
                mm(pst[:, 0:wd], HB[:, k, c0:c0 + 128], w[:, k, 0:wd], k == 0, False, [rw, rHB[k][n]], [rpst])
            mm(pst[:, 0:wd], ONE1[0:1, :], BTM[0:1, off:off + wd], False, True, [rONE1, rBTM], [rpst])

        w, rw = load_w(wintm[l, :, :, 0:16], 16)
        for i in range(18):
            pa, rpa = psrot()
            tm_matmul(w, rw, 0, 16, i, pa, rpa)
            cp("dve", GT[:, i, :], pa[:, 0:16], [rpa], [rGT[i]])
            gv = GT[:, i, :].rearrange("p (d t h) -> p d t h", d=2, t=2)
            fpre = gv[:, :, 1, :]
            ipre = gv[:, :, 0, :]
            spl = SM[:, 0:8].rearrange("p (d h) -> p d h", d=2)
            act(spl, fpre, AF.Exp, [rGT[i]], [rSM], scale=-1.0)
            act(spl, spl, AF.Ln, [rSM], [rSM], bias=1.0)
            pg, rpg = PS[5], rPS[5]
            mm(pg[:, 0:4], TRIF, SM[:, 0:4], True, True, [rSM, rCONST], [rpg])
            mm(pg[:, 4:8], TRIB, SM[:, 4:8], True, True, [rSM, rCONST], [rpg])
            mm(pg[:, 8:16], NEG1, SM[:, 0:8], True, True, [rSM, rCONST], [rpg])
            a_ = SM[:, 8:16]
            tt("dve", a_.rearrange("p (d h) -> p d h", d=2), ipre, pg[:, 0:8].rearrange("p (d h) -> p d h", d=2),
               ALU.subtract, [rGT[i], rpg], [rSM])
            act(SCAL[:, i, 0:8], a_, AF.Exp, [rSM], [rSCAL[i]], bias=LN8)
            tt("dve", SM[:, 16:24], a_, pg[:, 8:16], ALU.add, [rSM, rpg], [rSM])
            act(SCAL[:, i, 8:16], SM[:, 16:24], AF.Exp, [rSM], [rSCAL[i]], bias=LN8)
            act(SCAL[:, i, 16:24], pg[:, 0:8], AF.Exp, [rpg], [rSCAL[i]], scale=-1.0)
            act(SCAL[:, i, 24:32], pg[:, 8:16], AF.Exp, [rpg], [rSCAL[i]])

        if stop <= 3:
            return
        w, rw = load_w(wintm[l, :, :, 16:528], 512)
        for i in range(18):
            pa, rpa = psrot()
            tm_matmul(w, rw, 16, 512, i, pa, rpa)
            if stop <= 3.1:
                continue
            mset("pool", VMAt2, 0.0, [rVMAt])
            mset("pool", VMAt[:, :, 64:65], 1.0, [rVMAt])
            cp("act", VMAt[:, :, 0:64], pa[:, 0:256].rearrange("p (h d) -> p h d", h=4), [rpa], [rVMAt])
            dma("sp", vmad[i], VMAt2, [rVMAt], [])
            if stop <= 3.2:
                continue
            for d_ in range(2):
                tt("dve", KH[:, d_, :].rearrange("p (h d) -> p h d", h=4),
                   pa[:, 256:512].rearrange("p (h d) -> p h d", h=4),
                   SCAL[:, i, 8 + 4 * d_:12 + 4 * d_].unsqueeze(2).to_broadcast([128, 4, 64]), ALU.mult,
                   [rpa, rSCAL[i], rVMAt], [rKH])
            if stop <= 3.3:
                continue
            for d_ in range(2):
                for pr in range(2):
                    mm(PS[6 + d_][:, pr * (2 * EW):(pr + 1) * (2 * EW)], KH[:, d_, pr * 128:(pr + 1) * 128],
                       VMAt[:, 2 * pr:2 * pr + 2, :], True, True, [rKH, rVMAt], [rPS[6 + d_]])
            if stop <= 3.4:
                continue
            for d_ in range(2):
                for pr in range(2):
                    for hh in range(2):
                        cp("act", DC[64 * hh:64 * hh + 64, d_ * 18 + i, pr * EW:(pr + 1) * EW],
                           PS[6 + d_][64 * hh:64 * hh + 64, pr * (2 * EW) + hh * EW:pr * (2 * EW) + hh * EW + EW],
                           [rPS[6 + d_]], [rDC[d_ * 18 + i]])
        if stop <= 3.5:
            return

        mset("dve", CST[:], 0.0, [rCST])
        orders = [list(range(18)), [1, 0] + list(range(17, 1, -1))]
        for d_ in range(2):
            eng = "dve"
            for i in orders[d_]:
                cp(eng, CBF[:, d_ * 18 + i, :], CST[:, d_, :], [rCST], [rCBF[d_ * 18 + i]])
                for pr in range(2):
                    for hh in range(2):
                        col = 24 + d_ * 4 + 2 * pr + hh
                        stt(eng, CST[64 * hh:64 * hh + 64, d_, pr * EW:(pr + 1) * EW],
                            CST[64 * hh:64 * hh + 64, d_, pr * EW:(pr + 1) * EW],
                            SCAL[64 * hh:64 * hh + 64, i, col:col + 1],
                            DC[64 * hh:64 * hh + 64, d_ * 18 + i, pr * EW:(pr + 1) * EW],
                            ALU.mult, ALU.add, [rCST, rSCAL[i], rDC[d_ * 18 + i]], [rCST])

        if stop <= 4:
            return
        w, rw = load_w(wintm[l, :, :, 528:912], 384)
        for i in range(18):
            pa, rpa = psrot()
            tm_matmul(w, rw, 528, 384, i, pa, rpa)
            cp("act", VA[:, i, :].rearrange("p (a b) -> p a b", a=2)[:, :, 0:64],
               pa[:, 0:128].rearrange("p (a b) -> p a b", a=2), [rpa], [rVA[i]])
            act(OMSt[:], pa[:, 128:384], AF.Sigmoid, [rpa], [rOMSt])
            dma("sp", omsd[i], OMSt[:], [rOMSt], [])

        w, rw = load_w(wintm[l, :, :, 912:1424], 512)
        for i in range(18):
            pa, rpa = psrot()
            tm_matmul(w, rw, 912, 512, i, pa, rpa)
            act(UG[:], pa[:, 0:512], AF.Gelu_apprx_tanh, [rpa], [rUG])
            ugv = UG[:, 256:512].rearrange("p (g d) -> p g d", g=4)
            tt("dve", T1[:, 0:256], UG[:, 256:512], UG[:, 256:512], ALU.mult, [rUG], [rT1])
            S.op("dve", lambda e, o=SM[:, 32:36], a=T1[:, 0:256].rearrange("p (g d) -> p g d", g=4):
                 e.tensor_reduce(out=o, in_=a, axis=AX.X, op=ALU.add), [rT1], [rSM])
            act(SM[:, 32:36], SM[:, 32:36], AF.Sqrt, [rSM], [rSM], bias=EPS, scale=1.0 / 64)
            recip(SM[:, 36:40], SM[:, 32:36], [rSM], [rSM])
            tt("dve", T1[:, 0:256].rearrange("p (g d) -> p g d", g=4), ugv,
               SM[:, 36:40].unsqueeze(2).to_broadcast([128, 4, 64]), ALU.mult, [rUG, rSM], [rT1])
            tt("dve", VCN[:], T1[:, 0:256], GMV[:, 1, :], ALU.mult, [rT1, rGMV], [rVCN])
            pz, rpz = PS[5], rPS[5]
            for g in range(4):
                mm(pz[:, g * 64:(g + 1) * 64], WSPb[:, g, :], VCN[:, g * 64:(g + 1) * 64], True, True,
                   [rWSP, rVCN], [rpz])
            tt("dve", T2[:, 0:256].rearrange("p (g d) -> p g d", g=4), pz[:, 0:256].rearrange("p (g d) -> p g d", g=4),
               BSP[:, l, :].unsqueeze(2).to_broadcast([128, 4, 64]), ALU.add, [rpz, rBSP], [rT2])
            tt("dve", CMt[:], T2[:, 0:256], UG[:, 0:256], ALU.mult, [rT2, rUG], [rCMt])
            dma("sp", cmd[i], CMt[:], [rCMt], [])

        if stop <= 5:
            return
        S.barrier()
        AR.off = mark_tmp
        T1 = AR.get([128, 512], F32); rT1 = Res()
        T2 = AR.get([128, 512], F32); rT2 = Res()
        AT = AR.get([128, 2, 4, 128], BF16); rAT = Res()
        PT = [AR.get([128, 512], BF16) for _ in range(2)]; rPT = [Res(), Res()]
        OTM = AR.get([128, 512], F32); rOTM = Res()
        VMAt2 = AR.get([128, (4 * EW)], BF16); VMAt = VMAt2.rearrange("p (h d) -> p h d", h=4); rVMAt = Res()
        OMSt = AR.get([128, 256], BF16); rOMSt = Res()
        CMt = AR.get([128, 256], BF16); rCMt = Res()
        RD = AR.get([128, 512], F32); rRD = Res()
        HS = AR.get([128, 256], F32); rHS = Res()
        rMIX = [[Res() for _ in range(18)] for _ in range(8)]

        def mixcols(i):
            return hcol(i * 128)

        qblocks = list(range(2, 18)) + ([] if last else [0, 1])
        its = []
        for qb in qblocks:
            ktiles = list(range(18)) if qb >= 2 else [0, 1]
            for j in range(2):
                for kk, kt in enumerate(ktiles):
                    its.append((qb, j, kk, kt, len(ktiles)))

        def a_score(n_):
            qb, j, kk, kt, nkt = its[n_]
            q0 = qb * 128
            nq = 0 if qb < 2 else 1 + (qb - 2) // 4
            nk = 0 if kt < 2 else 1 + (kt - 2) // 4
            pscore, rpscore = psrot()
            mm(pscore[:, 0:512], QK[64 * j:64 * j + 64, 4, kt * 128:(kt + 1) * 128],
               QK[64 * j:64 * j + 64, 0:4, q0:q0 + 128], True, True,
               [rQK[4][nk]] + [rQK[c][nq] for c in range(4)], [rpscore])
            pi_ = n_ % 2
            act(PT[pi_][:], pscore[:, 0:512], AF.Exp, [rpscore], [rPT[pi_]], scale=0.125)

        def a_pv(n_):
            qb, j, kk, kt, nkt = its[n_]
            po, rpo = PS[4 + j], rPS[4 + j]
            pi_ = n_ % 2
            mm(po[:, 0:512], VA[:, kt, j * 128:(j + 1) * 128], PT[pi_][:], kk == 0, kk == nkt - 1,
               [rVA[kt], rPT[pi_]], [rpo])
            if kk == nkt - 1:
                recip(RD[64:128, :], po[64:128, :], [rpo], [rRD])
                c0 = mixcols(qb)
                tt("dve", HB[64 * j:64 * j + 64, 0:4, c0:c0 + 128],
                   po[0:64, 0:512].rearrange("p (c t) -> p c t", c=4),
                   RD[64:128, :].rearrange("p (c t) -> p c t", c=4), ALU.mult, [rpo, rRD],
                   [rMIX[c][qb] for c in range(4)])

        for n_ in range(len(its)):
            a_score(n_)
            if n_ >= 1:
                a_pv(n_ - 1)
        a_pv(len(its) - 1)

        if stop <= 6:
            return
        tiles2 = list(range(18)) if not last else list(range(2, 18))
        for i in tiles2:
            t0 = i * 128
            n = 0 if i < 2 else 1 + (i - 2) // 4
            if True:
                dma("sp", VMAt2, vmad[i], [], [rVMAt])
                dma("sp", OMSt[:], omsd[i], [], [rOMSt])
                dma("sp", CMt[:], cmd[i], [], [rCMt])
            pscs = [psrot(), psrot()]
            for h in (0, 2, 1, 3):
                ci, hh = h // 2, h % 2
                mm(pscs[hh][0][:, ci * 128:(ci + 1) * 128], QM[64 * hh:64 * hh + 64, 2 + ci, t0:t0 + 128],
                   QM[64 * hh:64 * hh + 64, ci, t0:t0 + 128], True, True, [rQM[2 + ci][n], rQM[ci][n]], [pscs[hh][1]])
            for d_ in range(2):
                msk = MASKF if d_ == 0 else MASKB
                for h in range(4):
                    ci, hh = h // 2, h % 2
                    stt("dve", AT[:, d_, h, :], pscs[hh][0][:, ci * 128:(ci + 1) * 128],
                        SCAL[:, i, d_ * 4 + h:d_ * 4 + h + 1], msk, ALU.mult, ALU.mult,
                        [pscs[hh][1], rSCAL[i], rCONST], [rAT])
            if stop <= 6.1:
                continue
            for d_ in range(2):
                pout, rpout = PS[6 + d_], rPS[6 + d_]
                for h in range(4):
                    ci, hh = h // 2, h % 2
                    mm(pout[:, h * EW:(h + 1) * EW], AT[:, d_, h, :], VMAt[:, h, :], True, False,
                       [rAT, rVMAt], [rpout])
                    mm(pout[:, h * EW:(h + 1) * EW], QM[64 * hh:64 * hh + 64, ci, t0:t0 + 128],
                       CBF[64 * hh:64 * hh + 64, d_ * 18 + i, ci * EW:(ci + 1) * EW], False, True,
                       [rQM[ci][n], rCBF[d_ * 18 + i]], [rpout])
            if stop <= 6.2:
                continue
            for d_ in range(2):
                pout, rpout = PS[6 + d_], rPS[6 + d_]
                pv = pout[:, 0:(4 * EW)].rearrange("p (h e) -> p h e", h=4)
                den = SM[:, 40 + 4 * d_:44 + 4 * d_]
                act(den, pv[:, :, 64], AF.Abs, [rpout], [rSM])
                tt("dve", den, den, SCAL[:, i, 16 + 4 * d_:20 + 4 * d_], ALU.max, [rSM, rSCAL[i]], [rSM])
                recip(den, den, [rSM], [rSM])
                dst = HS if d_ == 0 else T1[:, 0:256]
                tt("dve", dst.rearrange("p (h e) -> p h e", h=4), pv[:, :, 0:64],
                   den.unsqueeze(2).to_broadcast([128, 4, 64]), ALU.mult, [rpout, rSM], [rHS if d_ == 0 else rT1])
            tt("dve", HS[:], HS[:], T1[:, 0:256], ALU.add, [rHS, rT1], [rHS])
            if stop <= 6.3:
                continue
            tt("dve", T2[:, 0:256], HS[:], HS[:], ALU.mult, [rHS], [rT2])
            S.op("dve", lambda e, o=SM[:, 48:52], a=T2[:, 0:256].rearrange("p (g d) -> p g d", g=4):
                 e.tensor_reduce(out=o, in_=a, axis=AX.X, op=ALU.add), [rT2], [rSM])
            act(SM[:, 48:52], SM[:, 48:52], AF.Sqrt, [rSM], [rSM], bias=EPS, scale=1.0 / 64)
            recip(SM[:, 52:56], SM[:, 48:52], [rSM], [rSM])
            tt("dve", HS[:].rearrange("p (h e) -> p h e", h=4), HS[:].rearrange("p (h e) -> p h e", h=4),
               SM[:, 52:56].unsqueeze(2).to_broadcast([128, 4, 64]), ALU.mult, [rHS, rSM], [rHS])
            tt("dve", HS[:], HS[:], GMV[:, 0, :], ALU.mult, [rHS, rGMV], [rHS])
            tt("dve", OTM[:, 0:256], HS[:], OMSt[:], ALU.mult, [rHS, rOMSt], [rOTM])
            cp("pool", OTM[:, 256:512], CMt[:], [rCMt], [rOTM])
            if stop <= 6.4:
                continue
            ptr, rptr = psrot()
            for c in range(4):
                S.op("pe", lambda e, o=ptr[:, c * 128:(c + 1) * 128], a=OTM[:, c * 128:(c + 1) * 128]:
                     e.transpose(out=o, in_=a, identity=IDENT), [rOTM, rCONST], [rptr])
            c0 = mixcols(i)
            cp("act", HB[:, 4:8, c0:c0 + 128], ptr[:, 0:512].rearrange("p (c t) -> p c t", c=4), [rptr],
               [rMIX[c][i] for c in range(4, 8)])

        if stop <= 7:
            return
        S.barrier()
        if dbg:
            dma("sp", mix_dbg[b], HB[:], [], [])
            S.barrier()
        AR.reset()
        WOUT = AR.get([128, 8, 1024], BF16); rWOUT = [Res(), Res()]
        SQ = AR.get([128, 8, 512], BF16)
        for hf in range(2):
            dma("pool", WOUT[:, :, hf * 512:(hf + 1) * 512], wout[l, :, :, hf * 512:(hf + 1) * 512], (), [rWOUT[hf]])
        tiles3 = list(range(5)) if not last else list(range(1, 5))
        for n in tiles3:
            t0, w = TT[n]
            c0 = hcol(t0)
            m = ctx_m if n == 0 else b
            xt, rxt = xt_next()
            dma("sp", xt[:, :, 0:w], xsrc[b, :, :, t0:t0 + w], [rXD[b][n]], [rxt])
            for mo in range(8):
                pa, rpa = psrot()
                for k in range(8):
                    mm(pa[:, 0:w], WOUT[:, k, mo * 128:(mo + 1) * 128], HB[:, k, c0:c0 + w], k == 0, k == 7,
                       [rWOUT[mo // 4], rHB[k][n]], [rpa])
                stt("dve", xt[:, mo, 0:w], pa[:, 0:w], MOD[:, l, 16 + mo, m:m + 1], xt[:, mo, 0:w],
                    ALU.mult, ALU.add, [rpa, rMOD, rxt], [rxt])
            dma("sp", xd[b, :, :, t0:t0 + w], xt[:, :, 0:w], [rxt], [rXD[b][n]])
            if dbg:
                dma("sp", xm_dbg[b, :, :, t0:t0 + w], xt[:, :, 0:w], [rxt], [])
            norm_tile(l, 1, m, n, xt, rxt, SQ)

        if stop <= 8:
            return
        S.barrier()
        AR.reset()
        WUP = AR.get([128, 8, 2816], BF16); rWUP = [Res() for _ in range(6)]
        WDN = AR.get([128, 11, 1024], BF16); rWDN = [Res(), Res()]
        U = AR.get([128, 11, 512], BF16); rU = [Res() for _ in range(11)]
        TG = [AR.get([128, 512], F32) for _ in range(2)]; rTG = [Res(), Res()]
        TV = [AR.get([128, 512], F32) for _ in range(2)]; rTV = [Res(), Res()]
        SG = [AR.get([128, 512], F32) for _ in range(2)]; rSG = [Res(), Res()]
        wins = []
        if not last:
            wins.append((0, 258, 0, 0))
        for i in range(4):
            wins.append((258 + 510 * i, 512, 256 + 510 * i, None))
        wins.append((258 + 2040, 10, 256 + 2040, None))
        rXW = [Res() for _ in range(len(wins))]
        gi = [0]
        for hf in range(2):
            for pc in range(6):
                wd = 512 if pc < 5 else 256
                dma("pool", WUP[:, :, pc * 512:pc * 512 + wd], wup[l, hf, :, :, pc * 512:pc * 512 + wd], (), [rWUP[pc]])
            for pc in range(2):
                dma("pool", WDN[:, :, pc * 512:(pc + 1) * 512], wdown[l, hf, :, :, pc * 512:(pc + 1) * 512], (), [rWDN[pc]])
            for wi, (c0, w, tok0, _) in enumerate(wins):
                wo = w - 2
                m = ctx_m if tok0 < 256 else b
                hbres = [rHB[k][nn] for k in range(8) for nn in range(5)]
                for g in range(11):
                    pg_, rpg_ = psrot()
                    pv_, rpv_ = psrot()
                    for k in range(8):
                        mm(pg_[:, 0:w], WUP[:, k, g * 128:(g + 1) * 128], HB[:, k, c0:c0 + w], k == 0, k == 7,
                           [rWUP[(g * 128) // 512]] + (hbres if k == 0 else []), [rpg_])
                    for k in range(8):
                        mm(pv_[:, 0:w], WUP[:, k, 1408 + g * 128:1408 + (g + 1) * 128], HB[:, k, c0:c0 + w], k == 0, k == 7,
                           [rWUP[(1408 + g * 128) // 512]], [rpv_])
                    x_ = gi[0] % 2
                    gi[0] += 1
                    cg = CONVP[:, hf, g, :]
                    cv_ = CONVP[:, hf, 11 + g, :]
                    act(TG[x_][:, 0:wo], pg_[:, 1:1 + wo], AF.Identity, [rpg_, rCONVP], [rTG[x_]],
                        bias=cg[:, 3:4], scale=cg[:, 1:2])
                    stt("dve", TG[x_][:, 0:wo], pg_[:, 0:wo], cg[:, 0:1], TG[x_][:, 0:wo], ALU.mult, ALU.add,
                        [rpg_, rTG[x_]], [rTG[x_]])
                    stt("dve", TG[x_][:, 0:wo], pg_[:, 2:2 + wo], cg[:, 2:3], TG[x_][:, 0:wo], ALU.mult, ALU.add,
                        [rpg_, rTG[x_]], [rTG[x_]])
                    act(TV[x_][:, 0:wo], pv_[:, 1:1 + wo], AF.Identity, [rpv_, rCONVP], [rTV[x_]],
                        bias=cv_[:, 3:4], scale=cv_[:, 1:2])
                    stt("dve", TV[x_][:, 0:wo], pv_[:, 0:wo], cv_[:, 0:1], TV[x_][:, 0:wo], ALU.mult, ALU.add,
                        [rpv_, rTV[x_]], [rTV[x_]])
                    stt("dve", TV[x_][:, 0:wo], pv_[:, 2:2 + wo], cv_[:, 2:3], TV[x_][:, 0:wo], ALU.mult, ALU.add,
                        [rpv_, rTV[x_]], [rTV[x_]])
                    act(SG[x_][:, 0:wo], TG[x_][:, 0:wo], AF.Silu, [rTG[x_]], [rSG[x_]])
                    tt("dve", U[:, g, 0:wo], SG[x_][:, 0:wo], TV[x_][:, 0:wo], ALU.mult, [rSG[x_], rTV[x_]], [rU[g]])
                xt, rxt = xt_next()
                dma("sp", xt[:, :, 0:wo], xd[b, :, :, tok0:tok0 + wo], [rXW[wi]], [rxt])
                for mo in range(8):
                    pa, rpa = psrot()
                    for kc in range(11):
                        mm(pa[:, 0:wo], WDN[:, kc, mo * 128:(mo + 1) * 128], U[:, kc, 0:wo], kc == 0, kc == 10,
                           [rWDN[mo // 4], rU[kc]], [rpa])
                    stt("dve", xt[:, mo, 0:wo], pa[:, 0:wo], MOD[:, l, 40 + mo, m:m + 1], xt[:, mo, 0:wo],
                        ALU.mult, ALU.add, [rpa, rMOD, rxt], [rxt])
                if last and hf == 1:
                    tk = dma("sp", yout[b, :, :, tok0 - 256:tok0 - 256 + wo], xt[:, :, 0:wo], [rxt], [rXW[wi]])
                    out_toks.append(tk)
                else:
                    dma("sp", xd[b, :, :, tok0:tok0 + wo], xt[:, :, 0:wo], [rxt], [rXW[wi]])
        S.barrier()

    out_toks = []
    for b in range(nb):
        for l in range(nlayers):
            if stop > 0:
                layer(b, l)
    S.barrier()
    S.emit(nc, st)
    st.close()
    return nc


def _rope_tables():
    n = S_LAT
    grid_w = 64
    nf = HD // 4
    t = np.arange(n)
    row = (t // grid_w).astype(np.float32)
    col = (t % grid_w).astype(np.float32)
    inv = (10000.0 ** (-np.arange(nf, dtype=np.float32) / nf)).astype(np.float32)
    ang = np.concatenate([row[:, None] * inv[None], col[:, None] * inv[None]], axis=-1)
    cos = np.cos(ang).astype(np.float32).reshape(n, 2, nf)
    sin = np.sin(ang).astype(np.float32).reshape(n, 2, nf)
    C = np.ones((HD, NT), np.float32)
    Sg = np.zeros((HD, NT), np.float32)
    for ax in range(2):
        for half in range(2):
            rows = slice(ax * 32 + half * 16, ax * 32 + half * 16 + 16)
            C[rows, T_CTX:] = cos[:, ax, :].T
            Sg[rows, T_CTX:] = (-1.0 if half == 0 else 1.0) * sin[:, ax, :].T
    tab = np.stack([np.concatenate([C, C], 0), np.concatenate([Sg, Sg], 0)], axis=1)
    return np.ascontiguousarray(tab)


def _consts():
    c = np.zeros((128, 8, 128), np.float32)
    r = np.arange(128)[:, None]
    s = np.arange(128)[None, :]
    c[:, 0, :] = (r == s)
    c[:, 1, :] = -1.0 * (r <= s)
    c[:, 2, :] = -1.0 * (r >= s)
    c[:, 3, :] = -1.0
    c[:, 4, :] = (r <= s)
    c[:, 5, :] = (r >= s)
    c[:, 6, :] = 1.0
    c[:, 7, :] = ((r // 64) == (s // 64))
    return c


def _swap_idx():
    d = np.arange(HD)
    ax, half, f = d // 32, (d // 16) % 2, d % 16
    return ax * 32 + (1 - half) * 16 + f


def _prep_shared(inp):
    f = np.float32
    w_ada = inp["w_ada"]; w_in = inp["w_in"]; b_in = inp["b_in"]
    sh = {}
    sh["wada"] = np.ascontiguousarray(w_ada.reshape(DEPTH, 8, 128, 12, 512).transpose(0, 3, 2, 1, 4))
    sh["bada"] = np.ascontiguousarray(inp["b_ada"].reshape(DEPTH, 48, 128).transpose(2, 0, 1))
    gn = np.stack([inp["g_norm1"], inp["g_norm2"]], axis=1)
    sh["gn12"] = np.ascontiguousarray(gn.reshape(DEPTH, 2, 8, 128).transpose(3, 0, 1, 2))
    sw = _swap_idx()
    QA, KA, VAo, QMo, KMo, VMo, OMo, GTo, UCo, VCo = 0, 512, 640, 768, 1024, 1280, 1536, 1792, 1808, 2064
    fm_cols = []
    for c in range(4):
        h0, h1 = c, 4 + c
        q = np.concatenate([QA + h0 * 64 + np.arange(64), QA + h1 * 64 + np.arange(64)])
        qs = np.concatenate([QA + h0 * 64 + sw, QA + h1 * 64 + sw])
        fm_cols += [q, qs]
    k = np.concatenate([KA + np.arange(64), KA + 64 + np.arange(64)])
    ks = np.concatenate([KA + sw, KA + 64 + sw])
    fm_cols += [k, ks]
    fm_cols += [QMo + np.arange(128), QMo + 128 + np.arange(128), KMo + np.arange(128), KMo + 128 + np.arange(128)]
    fm_idx = np.concatenate(fm_cols)
    wfm = w_in[:, :, fm_idx]
    sh["winfm"] = np.ascontiguousarray(wfm.reshape(DEPTH, 8, 128, 7, 256).transpose(0, 3, 2, 1, 4))
    sh["binfm"] = np.ascontiguousarray(b_in[:, fm_idx].reshape(DEPTH, 14, 128).transpose(2, 0, 1))
    tm_idx = np.concatenate([GTo + np.arange(16), VMo + np.arange(256), KMo + np.arange(256), VAo + np.arange(128),
                             OMo + np.arange(256), UCo + np.arange(256), VCo + np.arange(256)])
    wtm = w_in[:, :, tm_idx]
    sh["wintm"] = np.ascontiguousarray(wtm.reshape(DEPTH, 8, 128, NTM).transpose(0, 2, 1, 3))
    sh["bintm"] = np.ascontiguousarray(b_in[:, tm_idx][None])
    gq = inp["g_q"]; gk = inp["g_k"]
    g4 = np.stack([np.tile(gq, (1, 2)), np.tile(gq[:, sw], (1, 2)), np.tile(gk, (1, 2)), np.tile(gk[:, sw], (1, 2))], axis=2)
    sh["gqk"] = np.ascontiguousarray(g4.transpose(1, 0, 2))
    sh["rope"] = _rope_tables()
    sh["gmv"] = np.ascontiguousarray(np.stack([inp["g_mh"], inp["g_v"]], axis=1))
    sh["wsp"] = np.ascontiguousarray(inp["w_sp"].transpose(0, 3, 1, 2))
    sh["bsp"] = np.ascontiguousarray(inp["b_sp"].transpose(2, 0, 1))
    rows = []
    for c in range(4):
        rows += [c * 64 + np.arange(64), (4 + c) * 64 + np.arange(64)]
    rows += [512 + np.arange(512)]
    ridx = np.concatenate(rows)
    wo = inp["w_out"][:, ridx, :]
    sh["wout"] = np.ascontiguousarray(wo.reshape(DEPTH, 8, 128, 1024).transpose(0, 2, 1, 3))
    w_up = inp["w_up"]
    ucols = np.stack([np.concatenate([hf * 1408 + np.arange(1408), DFF + hf * 1408 + np.arange(1408)]) for hf in range(2)])
    wu = w_up[:, :, ucols]
    sh["wup"] = np.ascontiguousarray(wu.reshape(DEPTH, 8, 128, 2, 2816).transpose(0, 3, 2, 1, 4))
    cw = inp["conv_w"]; cb = inp["conv_b"]
    cpar = np.concatenate([cw, cb[:, None, :]], axis=1)
    cpar = cpar[:, :, ucols]
    sh["convp"] = np.ascontiguousarray(cpar.reshape(DEPTH, 4, 2, 22, 128).transpose(4, 0, 2, 3, 1))
    sh["wdown"] = np.ascontiguousarray(inp["w_down"].reshape(DEPTH, 2, 11, 128, 1024).transpose(0, 1, 3, 2, 4))
    sh["consts"] = _consts()
    return {k: np.ascontiguousarray(v, dtype=f) for k, v in sh.items()}


def _prep_core(inp, core):
    b0 = 2 * core
    xs = []
    for b in (b0, b0 + 1):
        xcat = np.concatenate([inp["ctx"][b], inp["x"][b]], axis=0)
        xs.append(xcat.T.reshape(8, 128, NT).transpose(1, 0, 2))
    xin = np.ascontiguousarray(np.stack(xs), dtype=np.float32)
    cv = np.stack([inp["c"][b0], inp["c"][b0 + 1], inp["c_ctx"]], axis=1)
    cvec = np.ascontiguousarray(cv.reshape(8, 128, 3).transpose(1, 0, 2), dtype=np.float32)
    return {"xin": xin, "cvec": cvec}


_NC_CACHE = {}


def kernel(**inputs):
    inp = {k: np.asarray(v) for k, v in inputs.items()}
    shared = _prep_shared(inp)
    if "nc" not in _NC_CACHE:
        _NC_CACHE["nc"] = build()
    nc = _NC_CACHE["nc"]
    in_maps = []
    for core in range(8):
        m = dict(shared)
        m.update(_prep_core(inp, core))
        in_maps.append(m)
    res = run_bass_kernel_spmd(nc, in_maps, core_ids=list(range(8)))
    out = np.empty((16, S_LAT, D), np.float32)
    for core in range(8):
        y = np.asarray(res.results[core]["yout"])
        for i in range(2):
            out[2 * core + i] = y[i].transpose(1, 0, 2).reshape(D, S_LAT).T
    return out
```

```python
import contextlib
import numpy as np
import concourse.bass as bass
import concourse.mybir as mybir
from concourse.bass_utils import run_bass_kernel_spmd

F32 = mybir.dt.float32
BF16 = mybir.dt.bfloat16
AF = mybir.ActivationFunctionType
ALU = mybir.AluOpType
AX = mybir.AxisListType

D = 1024
S_LAT = 2048
T_CTX = 256
NT = 2304
DEPTH = 4
DFF = 2816
HD = 64
EPS = 1e-6
NTM = 1424
EW = 66
HBW = 2308
TT = [(0, 256), (256, 512), (768, 512), (1280, 512), (1792, 512)]
LN8 = float(np.log(0.125))


def hcol(t):
    return t + 1 if t < 256 else t + 3


class Tok:
    __slots__ = ("eng", "seq", "clk", "needed", "sem", "val")

    def __init__(self, eng, seq, clk):
        self.eng = eng
        self.seq = seq
        self.clk = clk
        self.needed = False
        self.sem = None
        self.val = 0


class Res:
    __slots__ = ("w", "r", "excl")

    def __init__(self, excl=False):
        self.w = None
        self.r = {}
        self.excl = excl


class Sched:
    ENGS = ("pe", "act", "dve", "pool", "sp")

    def __init__(self, n_dma_sems=8):
        self.q = {e: [] for e in self.ENGS}
        self.clock = {e: {} for e in self.ENGS}
        self.seq = {e: 0 for e in self.ENGS}
        self.toks = {e: [] for e in self.ENGS}
        self.n_dma = n_dma_sems
        self.dma_rr = {e: 0 for e in self.ENGS}
        self.dma_last = {}

    def _merge(self, clk, t):
        for k, v in t.clk.items():
            if clk.get(k, 0) < v:
                clk[k] = v
        if clk.get(t.eng, 0) < t.seq:
            clk[t.eng] = t.seq

    def _deps(self, eng, reads, writes):
        clk = self.clock[eng]
        waits = []
        cand = []
        for r in reads:
            if r.w is not None:
                cand.append(r.w)
        for w in writes:
            if w.w is not None:
                cand.append(w.w)
            cand.extend(w.r.values())
        for t in cand:
            if eng == "pe" and t.eng == "pe":
                continue
            if clk.get(t.eng, 0) < t.seq:
                waits.append(t)
                t.needed = True
                self._merge(clk, t)
        return waits

    def _mark(self, tok, reads, writes):
        for r in reads:
            r.r[tok.eng] = tok
        for w in writes:
            w.w = tok
            w.r = {}

    def op(self, eng, fn, reads=(), writes=()):
        ex = [r for r in reads if r.excl]
        if ex:
            writes = list(writes) + ex
        waits = self._deps(eng, reads, writes)
        self.seq[eng] += 1
        tok = Tok(eng, self.seq[eng], dict(self.clock[eng]))
        self.toks[eng].append(tok)
        self.q[eng].append((waits, fn, tok, False))
        self._mark(tok, reads, writes)
        return tok

    def dma(self, eng, fn, reads=(), writes=()):
        waits = self._deps(eng, reads, writes)
        j = self.dma_rr[eng]
        self.dma_rr[eng] = (j + 1) % self.n_dma
        key = ("dma", eng, j)
        last = self.dma_last.get(key)
        clk = self.clock[eng]
        if last is not None and clk.get(key, 0) < last.seq:
            waits.append(last)
            self._merge(clk, last)
        seq = (last.seq if last is not None else 0) + 1
        tok = Tok(key, seq, dict(clk))
        tok.needed = True
        self.dma_last[key] = tok
        self.q[eng].append((waits, fn, tok, True))
        self._mark(tok, reads, writes)
        return tok

    def barrier(self):
        lasts = [self.toks[e][-1] for e in self.ENGS if self.toks[e]]
        lasts += list(self.dma_last.values())
        for e in self.ENGS:
            clk = self.clock[e]
            waits = []
            for t in lasts:
                if e == "pe" and t.eng == "pe":
                    continue
                if clk.get(t.eng, 0) < t.seq:
                    waits.append(t)
                    t.needed = True
                    self._merge(clk, t)
            if waits:
                self.q[e].append((waits, None, None, False))

    def emit(self, nc, stack):
        esem = {}
        for e in self.ENGS:
            esem[e] = stack.enter_context(nc.semaphore("s_" + e))
            c = 0
            for t in self.toks[e]:
                if t.needed:
                    c += 1
                t.val = c
                t.sem = esem[e]
        dsem = {}
        for key in self.dma_last:
            dsem[key] = stack.enter_context(nc.semaphore("d_%s_%d" % (key[1], key[2])))
        block = stack.enter_context(nc.Block())
        q = self.q

        def run(e, engobj):
            for (waits, fn, tok, is_dma) in q[e]:
                for t in waits:
                    if isinstance(t.eng, tuple):
                        engobj.wait_ge(dsem[t.eng], 16 * t.seq)
                    else:
                        engobj.wait_ge(t.sem, t.val)
                if fn is None:
                    continue
                ins = fn(engobj)
                if is_dma:
                    ins.then_inc(dsem[tok.eng], 16)
                elif tok.needed:
                    ins.then_inc(tok.sem, 1)

        @block.tensor
        def _(e):
            run("pe", e)

        @block.scalar
        def _(e):
            run("act", e)

        @block.vector
        def _(e):
            run("dve", e)

        @block.gpsimd
        def _(e):
            run("pool", e)

        @block.sync
        def _(e):
            run("sp", e)


def build(nlayers=DEPTH, nb=2, dbg=(), stop=99):
    nc = bass.Bass("TRN2", target_bir_lowering=False)
    S = Sched()
    st = contextlib.ExitStack()

    def din(name, shape, dt=F32):
        return nc.dram_tensor(name, list(shape), dt, kind="ExternalInput").ap()

    def dscr(name, shape, dt):
        return nc.dram_tensor(name, list(shape), dt, kind="ExternalOutput").ap()

    xin = din("xin", [2, 128, 8, NT])
    cvec = din("cvec", [128, 8, 3])
    wada = din("wada", [DEPTH, 12, 128, 8, 512])
    bada = din("bada", [128, DEPTH, 48])
    gn12 = din("gn12", [128, DEPTH, 2, 8])
    winfm = din("winfm", [DEPTH, 7, 128, 8, 256])
    binfm = din("binfm", [128, DEPTH, 14])
    wintm = din("wintm", [DEPTH, 128, 8, NTM])
    bintm = din("bintm", [1, DEPTH, NTM])
    gqk = din("gqk", [128, DEPTH, 4])
    rope = din("rope", [128, 2, NT])
    gmv = din("gmv", [DEPTH, 2, 256])
    wsp = din("wsp", [DEPTH, 128, 4, 128])
    bsp = din("bsp", [128, DEPTH, 4])
    wout = din("wout", [DEPTH, 128, 8, 1024])
    wup = din("wup", [DEPTH, 2, 128, 8, 2816])
    convp = din("convp", [128, DEPTH, 2, 22, 4])
    wdown = din("wdown", [DEPTH, 2, 128, 11, 1024])
    consts = din("consts", [128, 8, 128])
    yout = nc.dram_tensor("yout", [2, 128, 8, S_LAT], F32, kind="ExternalOutput").ap()
    if dbg:
        xd = nc.dram_tensor("xd", [2, 128, 8, NT], F32, kind="ExternalOutput").ap()
        xm_dbg = nc.dram_tensor("xm_dbg", [2, 128, 8, NT], F32, kind="ExternalOutput").ap()
        mix_dbg = nc.dram_tensor("mix_dbg", [2, 128, 8, HBW], BF16, kind="ExternalOutput").ap()
    else:
        xd = dscr("xd", [2, 128, 8, NT], F32)
    vmad = dscr("vmad", [18, 128, (4 * EW)], BF16)
    omsd = dscr("omsd", [18, 128, 256], BF16)
    cmd = dscr("cmd", [18, 128, 256], BF16)
    dbg_out = {}

    def sb(name, shape, dt=F32):
        return st.enter_context(nc.sbuf_tensor(name, list(shape), dt))

    def psb(name):
        return st.enter_context(nc.psum_tensor(name, [128, 512], F32))

    def mm(out, lhsT, rhs, start, stop, rd, wr):
        S.op("pe", lambda e, o=out, l=lhsT, r=rhs, a=start, b=stop: e.matmul(o, lhsT=l, rhs=r, start=a, stop=b), rd, wr)

    def act(out, in_, func, rd, wr, bias=None, scale=None):
        kw = {}
        if bias is not None:
            kw["bias"] = bias
        if scale is not None:
            kw["scale"] = scale
        S.op("act", lambda e, o=out, i=in_, f=func, k=kw: e.activation(out=o, in_=i, func=f, **k), rd, wr)

    def tt(eng, out, in0, in1, op, rd, wr):
        S.op(eng, lambda e, o=out, a=in0, b=in1, p=op: e.tensor_tensor(out=o, in0=a, in1=b, op=p), rd, wr)

    def ts(eng, out, in0, s1, op0, rd, wr, s2=None, op1=None):
        if op1 is None:
            S.op(eng, lambda e, o=out, a=in0, x=s1, p=op0: e.tensor_scalar(out=o, in0=a, scalar1=x, scalar2=None, op0=p), rd, wr)
        else:
            S.op(eng, lambda e, o=out, a=in0, x=s1, y=s2, p=op0, q=op1: e.tensor_scalar(out=o, in0=a, scalar1=x, scalar2=y, op0=p, op1=q), rd, wr)

    def stt(eng, out, in0, scalar, in1, op0, op1, rd, wr, tmp=None):
        if eng == "pool":
            t_ = out if tmp is None else tmp
            S.op(eng, lambda e, o=t_, a=in0, x=scalar, p=op0: e.tensor_scalar(out=o, in0=a, scalar1=x, scalar2=None, op0=p), rd, wr)
            S.op(eng, lambda e, o=out, a=t_, b=in1, p=op1: e.tensor_tensor(out=o, in0=a, in1=b, op=p), rd, wr)
            return
        S.op(eng, lambda e, o=out, a=in0, s=scalar, b=in1, p=op0, q=op1: e.scalar_tensor_tensor(out=o, in0=a, scalar=s, in1=b, op0=p, op1=q), rd, wr)

    def cp(eng, out, in_, rd, wr):
        if eng == "act":
            S.op(eng, lambda e, o=out, i=in_: e.activation(out=o, in_=i, func=AF.Identity), rd, wr)
        else:
            S.op(eng, lambda e, o=out, i=in_: e.tensor_copy(out=o, in_=i), rd, wr)

    def recip(out, in_, rd, wr):
        S.op("dve", lambda e, o=out, i=in_: e.reciprocal(out=o, in_=i), rd, wr)

    def mset(eng, ap, val, wr):
        S.op(eng, lambda e, a=ap, v=val: e.memset(a, v), (), wr)

    def dma(eng, out, in_, rd, wr):
        return S.dma(eng, lambda e, o=out, i=in_: e.dma_start(out=o, in_=i), rd, wr)

    CONST = sb("CONST", [128, 8, 128]); rCONST = Res()
    CONSTB = sb("CONSTB", [128, 2, 128], BF16); rCONSTB = Res()
    ONE1 = sb("ONE1", [1, 128]); rONE1 = Res()
    CV = sb("CV", [128, 8, 3]); SCb = sb("SCb", [128, 8, 3], BF16); rCV = Res(); rSCb = Res()
    MOD = sb("MOD", [128, DEPTH, 48, 3]); rMOD = Res()
    BADA = sb("BADA", [128, DEPTH, 48]); rBADA = Res()
    GN = sb("GN", [128, DEPTH, 2, 8]); rGN = Res()
    A12 = sb("A12", [128, DEPTH, 2, 8, 3]); rA12 = Res()
    BFM = sb("BFM", [128, DEPTH, 14]); rBFM = Res()
    BTM = sb("BTM", [1, NTM]); rBTM = Res()
    GQK = sb("GQK", [128, DEPTH, 4]); rGQK = Res()
    BSP = sb("BSP", [128, DEPTH, 4]); rBSP = Res()
    CONVP = sb("CONVP", [128, 2, 22, 4]); rCONVP = Res()
    ROPE = sb("ROPE", [128, 2, NT], BF16); rROPE = Res()
    GMV = sb("GMV", [128, 2, 256]); rGMV = Res()
    WSPb = sb("WSPb", [128, 4, 128], BF16); rWSP = Res()
    HB = sb("HB", [128, 8, HBW], BF16)
    rHB = [[Res() for _ in range(5)] for _ in range(8)]
    XT = [sb("XT%d" % i, [128, 8, 512]) for i in range(2)]; rXT = [Res(), Res()]
    rSQ = Res()
    SD = sb("SD", [128, 512]); rSD = Res()
    RS = sb("RS", [128, 512]); rRS = Res()
    WP = [sb("WP%d" % i, [128, 8, 512], BF16) for i in range(2)]; rWP = [Res(), Res()]
    ARENA = sb("ARENA", [128, 46464], BF16)
    PS = [psb("PS%d" % i) for i in range(8)]
    rPS = [Res(excl=True) for _ in range(8)]

    IDENT = CONST[:, 0, :]
    TRIF = CONST[:, 1, :]
    TRIB = CONST[:, 2, :]
    NEG1 = CONST[:, 3, :]
    MASKF = CONST[:, 4, :]
    MASKB = CONST[:, 5, :]
    ONESb = CONSTB[:, 0, :]
    BLKb = CONSTB[:, 1, :]

    class Arena:
        def __init__(self):
            self.off = 0

        def reset(self):
            self.off = 0

        def get(self, shape, dt):
            n = int(np.prod(shape[1:]))
            if dt == F32:
                n2 = 2 * n
            else:
                n2 = n
            o = self.off + (self.off % 2)
            self.off = o + n2 + (n2 % 2)
            assert self.off <= 46464, ("arena overflow", self.off)
            v = ARENA[:, o:o + n2]
            if dt == F32:
                v = v.bitcast(F32)
            v = v[0:shape[0]]
            if len(shape) == 3:
                v = v.rearrange("p (a b) -> p a b", a=shape[1])
            elif len(shape) == 4:
                v = v.rearrange("p (a b c) -> p a b c", a=shape[1], b=shape[2])
            return v

    AR = Arena()
    wp_i = [0]

    def load_w(src_ap, width, kc=8):
        i = wp_i[0] % 2
        wp_i[0] += 1
        dma("pool", WP[i][:, 0:kc, 0:width], src_ap, (), [rWP[i]])
        return WP[i], rWP[i]

    ps_i = [0]

    def psrot(n=4):
        i = ps_i[0] % n
        ps_i[0] += 1
        return PS[i], rPS[i]

    dma("sp", CONST[:], consts[:, :, :], (), [rCONST])
    cp("dve", CONSTB[:], CONST[:, 6:8, :], [rCONST], [rCONSTB])
    mset("dve", ONE1[:], 1.0, [rONE1])
    dma("sp", CV[:], cvec[:, :, :], (), [rCV])
    dma("sp", BADA[:], bada[:, :, :], (), [rBADA])
    dma("sp", GN[:], gn12[:, :, :, :], (), [rGN])
    dma("sp", BFM[:], binfm[:, :, :], (), [rBFM])
    dma("sp", GQK[:], gqk[:, :, :], (), [rGQK])
    dma("sp", BSP[:], bsp[:, :, :], (), [rBSP])
    dma("pool", ROPE[:], rope[:, :, :], (), [rROPE])
    act(CV[:], CV[:], AF.Silu, [rCV], [rCV])
    cp("dve", SCb[:], CV[:], [rCV], [rSCb])
    for k in range(8):
        mset("pool", HB[:, k, 0:1], 0.0, [rHB[k][0]])
        mset("pool", HB[:, k, 257:259], 0.0, [rHB[k][0]])
        mset("pool", HB[:, k, 2307:2308], 0.0, [rHB[k][4]])

    for l in range(nlayers):
        pm = PS[0][:, 0:144]
        for pc in range(12):
            w, rw = load_w(wada[l, pc], 512)
            for j in range(4):
                jj = pc * 4 + j
                for k in range(8):
                    mm(pm[:, jj * 3:jj * 3 + 3], w[:, k, j * 128:(j + 1) * 128], SCb[:, k, :], k == 0, k == 7,
                       [rw, rSCb], [rPS[0]])
        tt("dve", MOD[:, l, :, :], pm.rearrange("p (a b) -> p a b", b=3),
           BADA[:, l, :].unsqueeze(2).to_broadcast([128, 48, 3]), ALU.add, [rPS[0], rBADA], [rMOD])
        for i in range(2):
            ts("dve", A12[:, l, i, :, :], MOD[:, l, 8 + 24 * i:16 + 24 * i, :], 1.0, ALU.add, [rMOD], [rA12])
            tt("dve", A12[:, l, i, :, :], A12[:, l, i, :, :],
               GN[:, l, i, :].unsqueeze(2).to_broadcast([128, 8, 3]), ALU.mult, [rA12, rGN], [rA12])

    def norm_tile(l, which, m, n, xt, rxt, SQ):
        t0, w = TT[n]
        c0 = hcol(t0)
        act(SQ[:, :, 0:w], xt[:, :, 0:w], AF.Square, [rxt], [rSQ])
        pss, rpss = PS[4], rPS[4]
        for k in range(8):
            mm(pss[:, 0:w], ONESb, SQ[:, k, 0:w], k == 0, k == 7, [rSQ, rCONSTB], [rpss])
        act(SD[:, 0:w], pss[:, 0:w], AF.Sqrt, [rpss], [rSD], bias=EPS, scale=1.0 / D)
        recip(RS[:, 0:w], SD[:, 0:w], [rSD], [rRS])
        tt("dve", xt[:, :, 0:w], xt[:, :, 0:w], RS[:, 0:w].unsqueeze(1).to_broadcast([128, 8, w]), ALU.mult,
           [rxt, rRS], [rxt])
        sh = 0 if which == 0 else 24
        for k in range(8):
            act(HB[:, k, c0:c0 + w], xt[:, k, 0:w], AF.Identity, [rxt, rA12, rMOD], [rHB[k][n]],
                bias=MOD[:, l, sh + k, m:m + 1], scale=A12[:, l, which, k, m:m + 1])

    xt_i = [0]

    def xt_next():
        i = xt_i[0] % 2
        xt_i[0] += 1
        return XT[i], rXT[i]

    rXD = [[Res() for _ in range(5)] for _ in range(2)]

    def layer(b, l):
        last = (l == DEPTH - 1)
        xsrc = xin if l == 0 else xd
        ctx_m = 2
        dma("sp", GMV[:, 0, :], gmv[l, 0:1, :].partition_broadcast(128), (), [rGMV])
        dma("sp", GMV[:, 1, :], gmv[l, 1:2, :].partition_broadcast(128), (), [rGMV])
        dma("pool", WSPb[:], wsp[l], (), [rWSP])
        dma("sp", BTM[:], bintm[0:1, l, :], (), [rBTM])
        dma("sp", CONVP[:], convp[:, l, :, :, :], (), [rCONVP])
        S.barrier()
        AR.reset()
        QK = AR.get([128, 5, NT], BF16); rQK = [[Res() for _ in range(5)] for _ in range(5)]
        QM = AR.get([128, 4, NT], BF16); rQM = [[Res() for _ in range(5)] for _ in range(4)]
        VA = AR.get([128, 18, 256], BF16); rVA = [Res() for _ in range(18)]
        GT = AR.get([128, 18, 16], F32); rGT = [Res() for _ in range(18)]
        DC = AR.get([128, 36, (2 * EW)], BF16); rDC = [Res() for _ in range(36)]
        CBF = AR.get([128, 36, (2 * EW)], BF16); rCBF = [Res() for _ in range(36)]
        SCAL = AR.get([128, 18, 32], F32); rSCAL = [Res() for _ in range(18)]
        SM = AR.get([128, 64], F32); rSM = Res()
        CST = AR.get([128, 2, (2 * EW)], F32); rCST = Res()
        mark_tmp = AR.off
        SQ = AR.get([128, 8, 512], BF16)
        for n in range(5):
            t0, w = TT[n]
            m = ctx_m if n == 0 else b
            xt, rxt = xt_next()
            dma("sp", xt[:, :, 0:w], xsrc[b, :, :, t0:t0 + w], [rXD[b][n]], [rxt])
            norm_tile(l, 0, m, n, xt, rxt, SQ)
        S.barrier()
        AR.off = mark_tmp
        F1 = AR.get([128, 512], F32); rF1 = Res()
        F2 = AR.get([128, 512], F32); rF2 = Res()
        SQH = AR.get([128, 512], BF16); rSQH = Res()
        RSQ = AR.get([128, 512], F32); rRSQ = Res()
        T1 = AR.get([128, 512], F32); rT1 = Res()
        T2 = AR.get([128, 512], F32); rT2 = Res()
        UG = AR.get([128, 512], F32); rUG = Res()
        VCN = AR.get([128, 256], BF16); rVCN = Res()
        KH = AR.get([128, 2, 256], BF16); rKH = Res()
        VMAt2 = AR.get([128, (4 * EW)], BF16); VMAt = VMAt2.rearrange("p (h d) -> p h d", h=4); rVMAt = Res()
        OMSt = AR.get([128, 256], BF16); rOMSt = Res()
        CMt = AR.get([128, 256], BF16); rCMt = Res()

        if stop <= 1:
            return
        for i in range(18):
            mset("pool", VA[:, i, :].rearrange("p (a b) -> p a b", a=2)[:, :, 64:128], 1.0, [rVA[i]])

        def fm_matmul(w, rw, cc, n, pst, rpst):
            t0, wd = TT[n]
            c0 = hcol(t0)
            for k in range(8):
                mm(pst[:, 0:wd], w[:, k, cc * 128:(cc + 1) * 128], HB[:, k, c0:c0 + wd], k == 0, k == 7,
                   [rw, rHB[k][n]], [rpst])

        for pi in range(7):
            w, rw = load_w(winfm[l, pi], 256)
            if pi < 5:
                gcol = 0 if pi < 4 else 2
                for n in range(5):
                    t0, wd = TT[n]
                    pa, rpa = psrot()
                    pb, rpb = psrot()
                    fm_matmul(w, rw, 0, n, pa, rpa)
                    fm_matmul(w, rw, 1, n, pb, rpb)
                    ba = BFM[:, l, 2 * pi:2 * pi + 1]
                    bb = BFM[:, l, 2 * pi + 1:2 * pi + 2]
                    act(F1[:, 0:wd], pa[:, 0:wd], AF.Identity, [rpa, rBFM], [rF1], bias=ba)
                    act(SQH[:, 0:wd], pa[:, 0:wd], AF.Square, [rpa, rBFM], [rSQH], bias=ba)
                    act(F2[:, 0:wd], pb[:, 0:wd], AF.Identity, [rpb, rBFM], [rF2], bias=bb)
                    pss, rpss = PS[4], rPS[4]
                    mm(pss[:, 0:wd], BLKb, SQH[:, 0:wd], True, True, [rSQH, rCONSTB], [rpss])
                    act(RSQ[:, 0:wd], pss[:, 0:wd], AF.Sqrt, [rpss], [rRSQ], bias=EPS, scale=1.0 / HD)
                    recip(RSQ[:, 0:wd], RSQ[:, 0:wd], [rRSQ], [rRSQ])
                    stt("dve", T1[:, 0:wd], F1[:, 0:wd], GQK[:, l, gcol:gcol + 1], ROPE[:, 0, t0:t0 + wd],
                        ALU.mult, ALU.mult, [rF1, rGQK, rROPE], [rT1])
                    stt("dve", T2[:, 0:wd], F2[:, 0:wd], GQK[:, l, gcol + 1:gcol + 2], ROPE[:, 1, t0:t0 + wd],
                        ALU.mult, ALU.mult, [rF2, rGQK, rROPE], [rT2])
                    tt("dve", T1[:, 0:wd], T1[:, 0:wd], T2[:, 0:wd], ALU.add, [rT1, rT2], [rT1])
                    tt("dve", QK[:, pi, t0:t0 + wd], T1[:, 0:wd], RSQ[:, 0:wd], ALU.mult, [rT1, rRSQ], [rQK[pi][n]])
            else:
                for cc in range(2):
                    ci = (pi - 5) * 2 + cc
                    for n in range(5):
                        t0, wd = TT[n]
                        pa, rpa = psrot()
                        fm_matmul(w, rw, cc, n, pa, rpa)
                        act(QM[:, ci, t0:t0 + wd], pa[:, 0:wd], AF.Identity, [rpa, rBFM], [rQM[ci][n]],
                            bias=BFM[:, l, 10 + ci:11 + ci])

        if stop <= 2:
            return
        def tm_matmul(w, rw, off, wd, i, pst, rpst):
            t0 = i * 128
            c0 = hcol(t0)
            n = 0 if i < 2 else 1 + (i - 2) // 4
            for k in range(8):
                mm(pst[:, 0:wd], HB[:, k, c0:c0 + 128], w[:, k, 0:wd], k == 0, False, [rw, rHB[k][n]], [rpst])
            mm(pst[:, 0:wd], ONE1[0:1, :], BTM[0:1, off:off + wd], False, True, [rONE1, rBTM], [rpst])

        w, rw = load_w(wintm[l, :, :, 0:16], 16)
        for i in range(18):
            pa, rpa = psrot()
            tm_matmul(w, rw, 0, 16, i, pa, rpa)
            cp("dve", GT[:, i, :], pa[:, 0:16], [rpa], [rGT[i]])
            gv = GT[:, i, :].rearrange("p (d t h) -> p d t h", d=2, t=2)
            fpre = gv[:, :, 1, :]
            ipre = gv[:, :, 0, :]
            spl = SM[:, 0:8].rearrange("p (d h) -> p d h", d=2)
            act(spl, fpre, AF.Exp, [rGT[i]], [rSM], scale=-1.0)
            act(spl, spl, AF.Ln, [rSM], [rSM], bias=1.0)
            pg, rpg = PS[5], rPS[5]
            mm(pg[:, 0:4], TRIF, SM[:, 0:4], True, True, [rSM, rCONST], [rpg])
            mm(pg[:, 4:8], TRIB, SM[:, 4:8], True, True, [rSM, rCONST], [rpg])
            mm(pg[:, 8:16], NEG1, SM[:, 0:8], True, True, [rSM, rCONST], [rpg])
            a_ = SM[:, 8:16]
            tt("dve", a_.rearrange("p (d h) -> p d h", d=2), ipre, pg[:, 0:8].rearrange("p (d h) -> p d h", d=2),
               ALU.subtract, [rGT[i], rpg], [rSM])
            act(SCAL[:, i, 0:8], a_, AF.Exp, [rSM], [rSCAL[i]], bias=LN8)
            tt("dve", SM[:, 16:24], a_, pg[:, 8:16], ALU.add, [rSM, rpg], [rSM])
            act(SCAL[:, i, 8:16], SM[:, 16:24], AF.Exp, [rSM], [rSCAL[i]], bias=LN8)
            act(SCAL[:, i, 16:24], pg[:, 0:8], AF.Exp, [rpg], [rSCAL[i]], scale=-1.0)
            act(SCAL[:, i, 24:32], pg[:, 8:16], AF.Exp, [rpg], [rSCAL[i]])

        if stop <= 3:
            return
        w, rw = load_w(wintm[l, :, :, 16:528], 512)
        for i in range(18):
            pa, rpa = psrot()
            tm_matmul(w, rw, 16, 512, i, pa, rpa)
            if stop <= 3.1:
                continue
            mset("pool", VMAt2, 0.0, [rVMAt])
            mset("pool", VMAt[:, :, 64:65], 1.0, [rVMAt])
            cp("act", VMAt[:, :, 0:64], pa[:, 0:256].rearrange("p (h d) -> p h d", h=4), [rpa], [rVMAt])
            dma("sp", vmad[i], VMAt2, [rVMAt], [])
            if stop <= 3.2:
                continue
            for d_ in range(2):
                tt("dve", KH[:, d_, :].rearrange("p (h d) -> p h d", h=4),
                   pa[:, 256:512].rearrange("p (h d) -> p h d", h=4),
                   SCAL[:, i, 8 + 4 * d_:12 + 4 * d_].unsqueeze(2).to_broadcast([128, 4, 64]), ALU.mult,
                   [rpa, rSCAL[i], rVMAt], [rKH])
            if stop <= 3.3:
                continue
            for d_ in range(2):
                for pr in range(2):
                    mm(PS[6 + d_][:, pr * (2 * EW):(pr + 1) * (2 * EW)], KH[:, d_, pr * 128:(pr + 1) * 128],
                       VMAt[:, 2 * pr:2 * pr + 2, :], True, True, [rKH, rVMAt], [rPS[6 + d_]])
            if stop <= 3.4:
                continue
            for d_ in range(2):
                for pr in range(2):
                    for hh in range(2):
                        cp("act", DC[64 * hh:64 * hh + 64, d_ * 18 + i, pr * EW:(pr + 1) * EW],
                           PS[6 + d_][64 * hh:64 * hh + 64, pr * (2 * EW) + hh * EW:pr * (2 * EW) + hh * EW + EW],
                           [rPS[6 + d_]], [rDC[d_ * 18 + i]])
        if stop <= 3.5:
            return

        mset("dve", CST[:], 0.0, [rCST])
        orders = [list(range(18)), [1, 0] + list(range(17, 1, -1))]
        for d_ in range(2):
            eng = "dve"
            for i in orders[d_]:
                cp(eng, CBF[:, d_ * 18 + i, :], CST[:, d_, :], [rCST], [rCBF[d_ * 18 + i]])
                for pr in range(2):
                    for hh in range(2):
                        col = 24 + d_ * 4 + 2 * pr + hh
                        stt(eng, CST[64 * hh:64 * hh + 64, d_, pr * EW:(pr + 1) * EW],
                            CST[64 * hh:64 * hh + 64, d_, pr * EW:(pr + 1) * EW],
                            SCAL[64 * hh:64 * hh + 64, i, col:col + 1],
                            DC[64 * hh:64 * hh + 64, d_ * 18 + i, pr * EW:(pr + 1) * EW],
                            ALU.mult, ALU.add, [rCST, rSCAL[i], rDC[d_ * 18 + i]], [rCST])

        if stop <= 4:
            return
        w, rw = load_w(wintm[l, :, :, 528:912], 384)
        for i in range(18):
            pa, rpa = psrot()
            tm_matmul(w, rw, 528, 384, i, pa, rpa)
            cp("act", VA[:, i, :].rearrange("p (a b) -> p a b", a=2)[:, :, 0:64],
               pa[:, 0:128].rearrange("p (a b) -> p a b", a=2), [rpa], [rVA[i]])
            act(OMSt[:], pa[:, 128:384], AF.Sigmoid, [rpa], [rOMSt])
            dma("sp", omsd[i], OMSt[:], [rOMSt], [])

        w, rw = load_w(wintm[l, :, :, 912:1424], 512)
        for i in range(18):
            pa, rpa = psrot()
            tm_matmul(w, rw, 912, 512, i, pa, rpa)
            act(UG[:], pa[:, 0:512], AF.Gelu_apprx_tanh, [rpa], [rUG])
            ugv = UG[:, 256:512].rearrange("p (g d) -> p g d", g=4)
            tt("dve", T1[:, 0:256], UG[:, 256:512], UG[:, 256:512], ALU.mult, [rUG], [rT1])
            S.op("dve", lambda e, o=SM[:, 32:36], a=T1[:, 0:256].rearrange("p (g d) -> p g d", g=4):
                 e.tensor_reduce(out=o, in_=a, axis=AX.X, op=ALU.add), [rT1], [rSM])
            act(SM[:, 32:36], SM[:, 32:36], AF.Sqrt, [rSM], [rSM], bias=EPS, scale=1.0 / 64)
            recip(SM[:, 36:40], SM[:, 32:36], [rSM], [rSM])
            tt("dve", T1[:, 0:256].rearrange("p (g d) -> p g d", g=4), ugv,
               SM[:, 36:40].unsqueeze(2).to_broadcast([128, 4, 64]), ALU.mult, [rUG, rSM], [rT1])
            tt("dve", VCN[:], T1[:, 0:256], GMV[:, 1, :], ALU.mult, [rT1, rGMV], [rVCN])
            pz, rpz = PS[5], rPS[5]
            for g in range(4):
                mm(pz[:, g * 64:(g + 1) * 64], WSPb[:, g, :], VCN[:, g * 64:(g + 1) * 64], True, True,
                   [rWSP, rVCN], [rpz])
            tt("dve", T2[:, 0:256].rearrange("p (g d) -> p g d", g=4), pz[:, 0:256].rearrange("p (g d) -> p g d", g=4),
               BSP[:, l, :].unsqueeze(2).to_broadcast([128, 4, 64]), ALU.add, [rpz, rBSP], [rT2])
            tt("dve", CMt[:], T2[:, 0:256], UG[:, 0:256], ALU.mult, [rT2, rUG], [rCMt])
            dma("sp", cmd[i], CMt[:], [rCMt], [])

        if stop <= 5:
            return
        S.barrier()
        AR.off = mark_tmp
        T1 = AR.get([128, 512], F32); rT1 = Res()
        T2 = AR.get([128, 512], F32); rT2 = Res()
        AT = AR.get([128, 2, 4, 128], BF16); rAT = Res()
        PT = [AR.get([128, 512], BF16) for _ in range(2)]; rPT = [Res(), Res()]
        OTM = AR.get([128, 512], F32); rOTM = Res()
        VMAt2 = AR.get([128, (4 * EW)], BF16); VMAt = VMAt2.rearrange("p (h d) -> p h d", h=4); rVMAt = Res()
        OMSt = AR.get([128, 256], BF16); rOMSt = Res()
        CMt = AR.get([128, 256], BF16); rCMt = Res()
        RD = AR.get([128, 512], F32); rRD = Res()
        HS = AR.get([128, 256], F32); rHS = Res()
        rMIX = [[Res() for _ in range(18)] for _ in range(8)]

        def mixcols(i):
            return hcol(i * 128)

        qblocks = list(range(2, 18)) + ([] if last else [0, 1])
        its = []
        for qb in qblocks:
            ktiles = list(range(18)) if qb >= 2 else [0, 1]
            for j in range(2):
                for kk, kt in enumerate(ktiles):
                    its.append((qb, j, kk, kt, len(ktiles)))

        def a_score(n_):
            qb, j, kk, kt, nkt = its[n_]
            q0 = qb * 128
            nq = 0 if qb < 2 else 1 + (qb - 2) // 4
            nk = 0 if kt < 2 else 1 + (kt - 2) // 4
            pscore, rpscore = psrot()
            mm(pscore[:, 0:512], QK[64 * j:64 * j + 64, 4, kt * 128:(kt + 1) * 128],
               QK[64 * j:64 * j + 64, 0:4, q0:q0 + 128], True, True,
               [rQK[4][nk]] + [rQK[c][nq] for c in range(4)], [rpscore])
            pi_ = n_ % 2
            act(PT[pi_][:], pscore[:, 0:512], AF.Exp, [rpscore], [rPT[pi_]], scale=0.125)

        def a_pv(n_):
            qb, j, kk, kt, nkt = its[n_]
            po, rpo = PS[4 + j], rPS[4 + j]
            pi_ = n_ % 2
            mm(po[:, 0:512], VA[:, kt, j * 128:(j + 1) * 128], PT[pi_][:], kk == 0, kk == nkt - 1,
               [rVA[kt], rPT[pi_]], [rpo])
            if kk == nkt - 1:
                recip(RD[64:128, :], po[64:128, :], [rpo], [rRD])
                c0 = mixcols(qb)
                tt("dve", HB[64 * j:64 * j + 64, 0:4, c0:c0 + 128],
                   po[0:64, 0:512].rearrange("p (c t) -> p c t", c=4),
                   RD[64:128, :].rearrange("p (c t) -> p c t", c=4), ALU.mult, [rpo, rRD],
                   [rMIX[c][qb] for c in range(4)])

        for n_ in range(len(its)):
            a_score(n_)
            if n_ >= 1:
                a_pv(n_ - 1)
        a_pv(len(its) - 1)

        if stop <= 6:
            return
        tiles2 = list(range(18)) if not last else list(range(2, 18))
        for i in tiles2:
            t0 = i * 128
            n = 0 if i < 2 else 1 + (i - 2) // 4
            if True:
                dma("sp", VMAt2, vmad[i], [], [rVMAt])
                dma("sp", OMSt[:], omsd[i], [], [rOMSt])
                dma("sp", CMt[:], cmd[i], [], [rCMt])
            pscs = [psrot(), psrot()]
            for h in (0, 2, 1, 3):
                ci, hh = h // 2, h % 2
                mm(pscs[hh][0][:, ci * 128:(ci + 1) * 128], QM[64 * hh:64 * hh + 64, 2 + ci, t0:t0 + 128],
                   QM[64 * hh:64 * hh + 64, ci, t0:t0 + 128], True, True, [rQM[2 + ci][n], rQM[ci][n]], [pscs[hh][1]])
            for d_ in range(2):
                msk = MASKF if d_ == 0 else MASKB
                for h in range(4):
                    ci, hh = h // 2, h % 2
                    stt("dve", AT[:, d_, h, :], pscs[hh][0][:, ci * 128:(ci + 1) * 128],
                        SCAL[:, i, d_ * 4 + h:d_ * 4 + h + 1], msk, ALU.mult, ALU.mult,
                        [pscs[hh][1], rSCAL[i], rCONST], [rAT])
            if stop <= 6.1:
                continue
            for d_ in range(2):
                pout, rpout = PS[6 + d_], rPS[6 + d_]
                for h in range(4):
                    ci, hh = h // 2, h % 2
                    mm(pout[:, h * EW:(h + 1) * EW], AT[:, d_, h, :], VMAt[:, h, :], True, False,
                       [rAT, rVMAt], [rpout])
                    mm(pout[:, h * EW:(h + 1) * EW], QM[64 * hh:64 * hh + 64, ci, t0:t0 + 128],
                       CBF[64 * hh:64 * hh + 64, d_ * 18 + i, ci * EW:(ci + 1) * EW], False, True,
                       [rQM[ci][n], rCBF[d_ * 18 + i]], [rpout])
            if stop <= 6.2:
                continue
            for d_ in range(2):
                pout, rpout = PS[6 + d_], rPS[6 + d_]
                pv = pout[:, 0:(4 * EW)].rearrange("p (h e) -> p h e", h=4)
                den = SM[:, 40 + 4 * d_:44 + 4 * d_]
                act(den, pv[:, :, 64], AF.Abs, [rpout], [rSM])
                tt("dve", den, den, SCAL[:, i, 16 + 4 * d_:20 + 4 * d_], ALU.max, [rSM, rSCAL[i]], [rSM])
                recip(den, den, [rSM], [rSM])
                dst = HS if d_ == 0 else T1[:, 0:256]
                tt("dve", dst.rearrange("p (h e) -> p h e", h=4), pv[:, :, 0:64],
                   den.unsqueeze(2).to_broadcast([128, 4, 64]), ALU.mult, [rpout, rSM], [rHS if d_ == 0 else rT1])
            tt("dve", HS[:], HS[:], T1[:, 0:256], ALU.add, [rHS, rT1], [rHS])
            if stop <= 6.3:
                continue
            tt("dve", T2[:, 0:256], HS[:], HS[:], ALU.mult, [rHS], [rT2])
            S.op("dve", lambda e, o=SM[:, 48:52], a=T2[:, 0:256].rearrange("p (g d) -> p g d", g=4):
                 e.tensor_reduce(out=o, in_=a, axis=AX.X, op=ALU.add), [rT2], [rSM])
            act(SM[:, 48:52], SM[:, 48:52], AF.Sqrt, [rSM], [rSM], bias=EPS, scale=1.0 / 64)
            recip(SM[:, 52:56], SM[:, 48:52], [rSM], [rSM])
            tt("dve", HS[:].rearrange("p (h e) -> p h e", h=4), HS[:].rearrange("p (h e) -> p h e", h=4),
               SM[:, 52:56].unsqueeze(2).to_broadcast([128, 4, 64]), ALU.mult, [rHS, rSM], [rHS])
            tt("dve", HS[:], HS[:], GMV[:, 0, :], ALU.mult, [rHS, rGMV], [rHS])
            tt("dve", OTM[:, 0:256], HS[:], OMSt[:], ALU.mult, [rHS, rOMSt], [rOTM])
            cp("pool", OTM[:, 256:512], CMt[:], [rCMt], [rOTM])
            if stop <= 6.4:
                continue
            ptr, rptr = psrot()
            for c in range(4):
                S.op("pe", lambda e, o=ptr[:, c * 128:(c + 1) * 128], a=OTM[:, c * 128:(c + 1) * 128]:
                     e.transpose(out=o, in_=a, identity=IDENT), [rOTM, rCONST], [rptr])
            c0 = mixcols(i)
            cp("act", HB[:, 4:8, c0:c0 + 128], ptr[:, 0:512].rearrange("p (c t) -> p c t", c=4), [rptr],
               [rMIX[c][i] for c in range(4, 8)])

        if stop <= 7:
            return
        S.barrier()
        if dbg:
            dma("sp", mix_dbg[b], HB[:], [], [])
            S.barrier()
        AR.reset()
        WUP = AR.get([128, 8, 2816], BF16); rWUP = [Res() for _ in range(6)]
        WDN = AR.get([128, 11, 1024], BF16); rWDN = [Res(), Res()]
        WOUT = AR.get([128, 8, 1024], BF16); rWOUT = [Res(), Res()]
        SQ = AR.get([128, 8, 512], BF16)
        for hf in range(2):
            dma("pool", WOUT[:, :, hf * 512:(hf + 1) * 512], wout[l, :, :, hf * 512:(hf + 1) * 512], (), [rWOUT[hf]])

        def load_ffn_w(hf):
            for pc in range(6):
                wd = 512 if pc < 5 else 256
                dma("pool", WUP[:, :, pc * 512:pc * 512 + wd], wup[l, hf, :, :, pc * 512:pc * 512 + wd], (), [rWUP[pc]])
            for pc in range(2):
                dma("pool", WDN[:, :, pc * 512:(pc + 1) * 512], wdown[l, hf, :, :, pc * 512:(pc + 1) * 512], (), [rWDN[pc]])

        load_ffn_w(0)
        tiles3 = list(range(5)) if not last else list(range(1, 5))
        for n in tiles3:
            t0, w = TT[n]
            c0 = hcol(t0)
            m = ctx_m if n == 0 else b
            xt, rxt = xt_next()
            dma("sp", xt[:, :, 0:w], xsrc[b, :, :, t0:t0 + w], [rXD[b][n]], [rxt])
            for mo in range(8):
                pa, rpa = psrot()
                for k in range(8):
                    mm(pa[:, 0:w], WOUT[:, k, mo * 128:(mo + 1) * 128], HB[:, k, c0:c0 + w], k == 0, k == 7,
                       [rWOUT[mo // 4], rHB[k][n]], [rpa])
                stt("dve", xt[:, mo, 0:w], pa[:, 0:w], MOD[:, l, 16 + mo, m:m + 1], xt[:, mo, 0:w],
                    ALU.mult, ALU.add, [rpa, rMOD, rxt], [rxt])
            dma("sp", xd[b, :, :, t0:t0 + w], xt[:, :, 0:w], [rxt], [rXD[b][n]])
            if dbg:
                dma("sp", xm_dbg[b, :, :, t0:t0 + w], xt[:, :, 0:w], [rxt], [])
            norm_tile(l, 1, m, n, xt, rxt, SQ)

        if stop <= 8:
            return
        S.barrier()
        AR.reset()
        WUP_ = AR.get([128, 8, 2816], BF16)
        WDN_ = AR.get([128, 11, 1024], BF16)
        U = AR.get([128, 11, 512], BF16); rU = [Res() for _ in range(11)]
        TG = [AR.get([128, 512], F32) for _ in range(2)]; rTG = [Res(), Res()]
        TV = [AR.get([128, 512], F32) for _ in range(2)]; rTV = [Res(), Res()]
        SG = [AR.get([128, 512], F32) for _ in range(2)]; rSG = [Res(), Res()]
        wins = []
        if not last:
            wins.append((0, 258, 0, 0))
        for i in range(4):
            wins.append((258 + 510 * i, 512, 256 + 510 * i, None))
        wins.append((258 + 2040, 10, 256 + 2040, None))
        rXW = [Res() for _ in range(len(wins))]
        gi = [0]
        for hf in range(2):
            if hf == 1:
                load_ffn_w(1)
            for wi, (c0, w, tok0, _) in enumerate(wins):
                wo = w - 2
                m = ctx_m if tok0 < 256 else b
                hbres = [rHB[k][nn] for k in range(8) for nn in range(5)]
                for g in range(11):
                    pg_, rpg_ = psrot()
                    pv_, rpv_ = psrot()
                    for k in range(8):
                        mm(pg_[:, 0:w], WUP[:, k, g * 128:(g + 1) * 128], HB[:, k, c0:c0 + w], k == 0, k == 7,
                           [rWUP[(g * 128) // 512]] + (hbres if k == 0 else []), [rpg_])
                    for k in range(8):
                        mm(pv_[:, 0:w], WUP[:, k, 1408 + g * 128:1408 + (g + 1) * 128], HB[:, k, c0:c0 + w], k == 0, k == 7,
                           [rWUP[(1408 + g * 128) // 512]], [rpv_])
                    x_ = gi[0] % 2
                    gi[0] += 1
                    cg = CONVP[:, hf, g, :]
                    cv_ = CONVP[:, hf, 11 + g, :]
                    act(TG[x_][:, 0:wo], pg_[:, 1:1 + wo], AF.Identity, [rpg_, rCONVP], [rTG[x_]],
                        bias=cg[:, 3:4], scale=cg[:, 1:2])
                    stt("dve", TG[x_][:, 0:wo], pg_[:, 0:wo], cg[:, 0:1], TG[x_][:, 0:wo], ALU.mult, ALU.add,
                        [rpg_, rTG[x_]], [rTG[x_]])
                    stt("dve", TG[x_][:, 0:wo], pg_[:, 2:2 + wo], cg[:, 2:3], TG[x_][:, 0:wo], ALU.mult, ALU.add,
                        [rpg_, rTG[x_]], [rTG[x_]])
                    act(TV[x_][:, 0:wo], pv_[:, 1:1 + wo], AF.Identity, [rpv_, rCONVP], [rTV[x_]],
                        bias=cv_[:, 3:4], scale=cv_[:, 1:2])
                    stt("dve", TV[x_][:, 0:wo], pv_[:, 0:wo], cv_[:, 0:1], TV[x_][:, 0:wo], ALU.mult, ALU.add,
                        [rpv_, rTV[x_]], [rTV[x_]])
                    stt("dve", TV[x_][:, 0:wo], pv_[:, 2:2 + wo], cv_[:, 2:3], TV[x_][:, 0:wo], ALU.mult, ALU.add,
                        [rpv_, rTV[x_]], [rTV[x_]])
                    act(SG[x_][:, 0:wo], TG[x_][:, 0:wo], AF.Silu, [rTG[x_]], [rSG[x_]])
                    tt("dve", U[:, g, 0:wo], SG[x_][:, 0:wo], TV[x_][:, 0:wo], ALU.mult, [rSG[x_], rTV[x_]], [rU[g]])
                xt, rxt = xt_next()
                dma("sp", xt[:, :, 0:wo], xd[b, :, :, tok0:tok0 + wo], [rXW[wi]], [rxt])
                for mo in range(8):
                    pa, rpa = psrot()
                    for kc in range(11):
                        mm(pa[:, 0:wo], WDN[:, kc, mo * 128:(mo + 1) * 128], U[:, kc, 0:wo], kc == 0, kc == 10,
                           [rWDN[mo // 4], rU[kc]], [rpa])
                    stt("dve", xt[:, mo, 0:wo], pa[:, 0:wo], MOD[:, l, 40 + mo, m:m + 1], xt[:, mo, 0:wo],
                        ALU.mult, ALU.add, [rpa, rMOD, rxt], [rxt])
                if last and hf == 1:
                    tk = dma("sp", yout[b, :, :, tok0 - 256:tok0 - 256 + wo], xt[:, :, 0:wo], [rxt], [rXW[wi]])
                    out_toks.append(tk)
                else:
                    dma("sp", xd[b, :, :, tok0:tok0 + wo], xt[:, :, 0:wo], [rxt], [rXW[wi]])
        S.barrier()

    out_toks = []
    for b in range(nb):
        for l in range(nlayers):
            if stop > 0:
                layer(b, l)
    S.barrier()
    S.emit(nc, st)
    st.close()
    return nc


def _rope_tables():
    n = S_LAT
    grid_w = 64
    nf = HD // 4
    t = np.arange(n)
    row = (t // grid_w).astype(np.float32)
    col = (t % grid_w).astype(np.float32)
    inv = (10000.0 ** (-np.arange(nf, dtype=np.float32) / nf)).astype(np.float32)
    ang = np.concatenate([row[:, None] * inv[None], col[:, None] * inv[None]], axis=-1)
    cos = np.cos(ang).astype(np.float32).reshape(n, 2, nf)
    sin = np.sin(ang).astype(np.float32).reshape(n, 2, nf)
    C = np.ones((HD, NT), np.float32)
    Sg = np.zeros((HD, NT), np.float32)
    for ax in range(2):
        for half in range(2):
            rows = slice(ax * 32 + half * 16, ax * 32 + half * 16 + 16)
            C[rows, T_CTX:] = cos[:, ax, :].T
            Sg[rows, T_CTX:] = (-1.0 if half == 0 else 1.0) * sin[:, ax, :].T
    tab = np.stack([np.concatenate([C, C], 0), np.concatenate([Sg, Sg], 0)], axis=1)
    return np.ascontiguousarray(tab)


def _consts():
    c = np.zeros((128, 8, 128), np.float32)
    r = np.arange(128)[:, None]
    s = np.arange(128)[None, :]
    c[:, 0, :] = (r == s)
    c[:, 1, :] = -1.0 * (r <= s)
    c[:, 2, :] = -1.0 * (r >= s)
    c[:, 3, :] = -1.0
    c[:, 4, :] = (r <= s)
    c[:, 5, :] = (r >= s)
    c[:, 6, :] = 1.0
    c[:, 7, :] = ((r // 64) == (s // 64))
    return c


def _swap_idx():
    d = np.arange(HD)
    ax, half, f = d // 32, (d // 16) % 2, d % 16
    return ax * 32 + (1 - half) * 16 + f


def _prep_shared(inp):
    f = np.float32
    w_ada = inp["w_ada"]; w_in = inp["w_in"]; b_in = inp["b_in"]
    sh = {}
    sh["wada"] = np.ascontiguousarray(w_ada.reshape(DEPTH, 8, 128, 12, 512).transpose(0, 3, 2, 1, 4))
    sh["bada"] = np.ascontiguousarray(inp["b_ada"].reshape(DEPTH, 48, 128).transpose(2, 0, 1))
    gn = np.stack([inp["g_norm1"], inp["g_norm2"]], axis=1)
    sh["gn12"] = np.ascontiguousarray(gn.reshape(DEPTH, 2, 8, 128).transpose(3, 0, 1, 2))
    sw = _swap_idx()
    QA, KA, VAo, QMo, KMo, VMo, OMo, GTo, UCo, VCo = 0, 512, 640, 768, 1024, 1280, 1536, 1792, 1808, 2064
    fm_cols = []
    for c in range(4):
        h0, h1 = c, 4 + c
        q = np.concatenate([QA + h0 * 64 + np.arange(64), QA + h1 * 64 + np.arange(64)])
        qs = np.concatenate([QA + h0 * 64 + sw, QA + h1 * 64 + sw])
        fm_cols += [q, qs]
    k = np.concatenate([KA + np.arange(64), KA + 64 + np.arange(64)])
    ks = np.concatenate([KA + sw, KA + 64 + sw])
    fm_cols += [k, ks]
    fm_cols += [QMo + np.arange(128), QMo + 128 + np.arange(128), KMo + np.arange(128), KMo + 128 + np.arange(128)]
    fm_idx = np.concatenate(fm_cols)
    wfm = w_in[:, :, fm_idx]
    sh["winfm"] = np.ascontiguousarray(wfm.reshape(DEPTH, 8, 128, 7, 256).transpose(0, 3, 2, 1, 4))
    sh["binfm"] = np.ascontiguousarray(b_in[:, fm_idx].reshape(DEPTH, 14, 128).transpose(2, 0, 1))
    tm_idx = np.concatenate([GTo + np.arange(16), VMo + np.arange(256), KMo + np.arange(256), VAo + np.arange(128),
                             OMo + np.arange(256), UCo + np.arange(256), VCo + np.arange(256)])
    wtm = w_in[:, :, tm_idx]
    sh["wintm"] = np.ascontiguousarray(wtm.reshape(DEPTH, 8, 128, NTM).transpose(0, 2, 1, 3))
    sh["bintm"] = np.ascontiguousarray(b_in[:, tm_idx][None])
    gq = inp["g_q"]; gk = inp["g_k"]
    g4 = np.stack([np.tile(gq, (1, 2)), np.tile(gq[:, sw], (1, 2)), np.tile(gk, (1, 2)), np.tile(gk[:, sw], (1, 2))], axis=2)
    sh["gqk"] = np.ascontiguousarray(g4.transpose(1, 0, 2))
    sh["rope"] = _rope_tables()
    sh["gmv"] = np.ascontiguousarray(np.stack([inp["g_mh"], inp["g_v"]], axis=1))
    sh["wsp"] = np.ascontiguousarray(inp["w_sp"].transpose(0, 3, 1, 2))
    sh["bsp"] = np.ascontiguousarray(inp["b_sp"].transpose(2, 0, 1))
    rows = []
    for c in range(4):
        rows += [c * 64 + np.arange(64), (4 + c) * 64 + np.arange(64)]
    rows += [512 + np.arange(512)]
    ridx = np.concatenate(rows)
    wo = inp["w_out"][:, ridx, :]
    sh["wout"] = np.ascontiguousarray(wo.reshape(DEPTH, 8, 128, 1024).transpose(0, 2, 1, 3))
    w_up = inp["w_up"]
    ucols = np.stack([np.concatenate([hf * 1408 + np.arange(1408), DFF + hf * 1408 + np.arange(1408)]) for hf in range(2)])
    wu = w_up[:, :, ucols]
    sh["wup"] = np.ascontiguousarray(wu.reshape(DEPTH, 8, 128, 2, 2816).transpose(0, 3, 2, 1, 4))
    cw = inp["conv_w"]; cb = inp["conv_b"]
    cpar = np.concatenate([cw, cb[:, None, :]], axis=1)
    cpar = cpar[:, :, ucols]
    sh["convp"] = np.ascontiguousarray(cpar.reshape(DEPTH, 4, 2, 22, 128).transpose(4, 0, 2, 3, 1))
    sh["wdown"] = np.ascontiguousarray(inp["w_down"].reshape(DEPTH, 2, 11, 128, 1024).transpose(0, 1, 3, 2, 4))
    sh["consts"] = _consts()
    return {k: np.ascontiguousarray(v, dtype=f) for k, v in sh.items()}


def _prep_core(inp, core):
    b0 = 2 * core
    xs = []
    for b in (b0, b0 + 1):
        xcat = np.concatenate([inp["ctx"][b], inp["x"][b]], axis=0)
        xs.append(xcat.T.reshape(8, 128, NT).transpose(1, 0, 2))
    xin = np.ascontiguousarray(np.stack(xs), dtype=np.float32)
    cv = np.stack([inp["c"][b0], inp["c"][b0 + 1], inp["c_ctx"]], axis=1)
    cvec = np.ascontiguousarray(cv.reshape(8, 128, 3).transpose(1, 0, 2), dtype=np.float32)
    return {"xin": xin, "cvec": cvec}


_NC_CACHE = {}


def kernel(**inputs):
    inp = {k: np.asarray(v) for k, v in inputs.items()}
    shared = _prep_shared(inp)
    if "nc" not in _NC_CACHE:
        _NC_CACHE["nc"] = build()
    nc = _NC_CACHE["nc"]
    in_maps = []
    for core in range(8):
        m = dict(shared)
        m.update(_prep_core(inp, core))
        in_maps.append(m)
    res = run_bass_kernel_spmd(nc, in_maps, core_ids=list(range(8)))
    out = np.empty((16, S_LAT, D), np.float32)
    for core in range(8):
        y = np.asarray(res.results[core]["yout"])
        for i in range(2):
            out[2 * core + i] = y[i].transpose(1, 0, 2).reshape(D, S_LAT).T
    return out
```

```python
import contextlib
import numpy as np
import concourse.bass as bass
import concourse.mybir as mybir
from concourse.bass_utils import run_bass_kernel_spmd

F32 = mybir.dt.float32
BF16 = mybir.dt.bfloat16
AF = mybir.ActivationFunctionType
ALU = mybir.AluOpType
AX = mybir.AxisListType

D = 1024
S_LAT = 2048
T_CTX = 256
NT = 2304
DEPTH = 4
DFF = 2816
HD = 64
EPS = 1e-6
NTM = 1424
EW = 66
HBW = 2308
TT = [(0, 256), (256, 512), (768, 512), (1280, 512), (1792, 512)]
LN8 = float(np.log(0.125))


def hcol(t):
    return t + 1 if t < 256 else t + 3


class Tok:
    __slots__ = ("eng", "seq", "clk", "needed", "sem", "val")

    def __init__(self, eng, seq, clk):
        self.eng = eng
        self.seq = seq
        self.clk = clk
        self.needed = False
        self.sem = None
        self.val = 0


class Res:
    __slots__ = ("w", "r", "excl")

    def __init__(self, excl=False):
        self.w = None
        self.r = {}
        self.excl = excl


class Sched:
    ENGS = ("pe", "act", "dve", "pool", "sp")

    def __init__(self, n_dma_sems=8):
        self.q = {e: [] for e in self.ENGS}
        self.clock = {e: {} for e in self.ENGS}
        self.seq = {e: 0 for e in self.ENGS}
        self.toks = {e: [] for e in self.ENGS}
        self.n_dma = n_dma_sems
        self.dma_rr = {e: 0 for e in self.ENGS}
        self.dma_last = {}

    def _merge(self, clk, t):
        for k, v in t.clk.items():
            if clk.get(k, 0) < v:
                clk[k] = v
        if clk.get(t.eng, 0) < t.seq:
            clk[t.eng] = t.seq

    def _deps(self, eng, reads, writes):
        clk = self.clock[eng]
        waits = []
        cand = []
        for r in reads:
            if r.w is not None:
                cand.append(r.w)
        for w in writes:
            if w.w is not None:
                cand.append(w.w)
            cand.extend(w.r.values())
        for t in cand:
            if eng == "pe" and t.eng == "pe":
                continue
            if clk.get(t.eng, 0) < t.seq:
                waits.append(t)
                t.needed = True
                self._merge(clk, t)
        return waits

    def _mark(self, tok, reads, writes):
        for r in reads:
            r.r[tok.eng] = tok
        for w in writes:
            w.w = tok
            w.r = {}

    def op(self, eng, fn, reads=(), writes=()):
        ex = [r for r in reads if r.excl]
        if ex:
            writes = list(writes) + ex
        waits = self._deps(eng, reads, writes)
        self.seq[eng] += 1
        tok = Tok(eng, self.seq[eng], dict(self.clock[eng]))
        self.toks[eng].append(tok)
        self.q[eng].append((waits, fn, tok, False))
        self._mark(tok, reads, writes)
        return tok

    def dma(self, eng, fn, reads=(), writes=()):
        waits = self._deps(eng, reads, writes)
        j = self.dma_rr[eng]
        self.dma_rr[eng] = (j + 1) % self.n_dma
        key = ("dma", eng, j)
        last = self.dma_last.get(key)
        clk = self.clock[eng]
        if last is not None and clk.get(key, 0) < last.seq:
            waits.append(last)
            self._merge(clk, last)
        seq = (last.seq if last is not None else 0) + 1
        tok = Tok(key, seq, dict(clk))
        tok.needed = True
        self.dma_last[key] = tok
        self.q[eng].append((waits, fn, tok, True))
        self._mark(tok, reads, writes)
        return tok

    def barrier(self):
        lasts = [self.toks[e][-1] for e in self.ENGS if self.toks[e]]
        lasts += list(self.dma_last.values())
        for e in self.ENGS:
            clk = self.clock[e]
            waits = []
            for t in lasts:
                if e == "pe" and t.eng == "pe":
                    continue
                if clk.get(t.eng, 0) < t.seq:
                    waits.append(t)
                    t.needed = True
                    self._merge(clk, t)
            if waits:
                self.q[e].append((waits, None, None, False))

    def emit(self, nc, stack):
        esem = {}
        for e in self.ENGS:
            esem[e] = stack.enter_context(nc.semaphore("s_" + e))
            c = 0
            for t in self.toks[e]:
                if t.needed:
                    c += 1
                t.val = c
                t.sem = esem[e]
        dsem = {}
        for key in self.dma_last:
            dsem[key] = stack.enter_context(nc.semaphore("d_%s_%d" % (key[1], key[2])))
        block = stack.enter_context(nc.Block())
        q = self.q

        def run(e, engobj):
            for (waits, fn, tok, is_dma) in q[e]:
                for t in waits:
                    if isinstance(t.eng, tuple):
                        engobj.wait_ge(dsem[t.eng], 16 * t.seq)
                    else:
                        engobj.wait_ge(t.sem, t.val)
                if fn is None:
                    continue
                ins = fn(engobj)
                if is_dma:
                    ins.then_inc(dsem[tok.eng], 16)
                elif tok.needed:
                    ins.then_inc(tok.sem, 1)

        @block.tensor
        def _(e):
            run("pe", e)

        @block.scalar
        def _(e):
            run("act", e)

        @block.vector
        def _(e):
            run("dve", e)

        @block.gpsimd
        def _(e):
            run("pool", e)

        @block.sync
        def _(e):
            run("sp", e)


def build(nlayers=DEPTH, nb=2, dbg=(), stop=99):
    nc = bass.Bass("TRN2", target_bir_lowering=False)
    S = Sched()
    st = contextlib.ExitStack()

    def din(name, shape, dt=F32):
        return nc.dram_tensor(name, list(shape), dt, kind="ExternalInput").ap()

    def dscr(name, shape, dt):
        return nc.dram_tensor(name, list(shape), dt, kind="ExternalOutput").ap()

    xin = din("xin", [2, 128, 8, NT])
    cvec = din("cvec", [128, 8, 3])
    wada = din("wada", [DEPTH, 12, 128, 8, 512])
    bada = din("bada", [128, DEPTH, 48])
    gn12 = din("gn12", [128, DEPTH, 2, 8])
    winfm = din("winfm", [DEPTH, 7, 128, 8, 256])
    binfm = din("binfm", [128, DEPTH, 14])
    wintm = din("wintm", [DEPTH, 128, 8, NTM])
    bintm = din("bintm", [1, DEPTH, NTM])
    gqk = din("gqk", [128, DEPTH, 4])
    rope = din("rope", [128, 2, NT])
    gmv = din("gmv", [DEPTH, 2, 256])
    wsp = din("wsp", [DEPTH, 128, 4, 128])
    bsp = din("bsp", [128, DEPTH, 4])
    wout = din("wout", [DEPTH, 128, 8, 1024])
    wup = din("wup", [DEPTH, 2, 128, 8, 2816])
    convp = din("convp", [128, DEPTH, 2, 22, 4])
    wdown = din("wdown", [DEPTH, 2, 128, 11, 1024])
    consts = din("consts", [128, 8, 128])
    yout = nc.dram_tensor("yout", [2, 128, 8, S_LAT], F32, kind="ExternalOutput").ap()
    if dbg:
        xd = nc.dram_tensor("xd", [2, 128, 8, NT], F32, kind="ExternalOutput").ap()
        xm_dbg = nc.dram_tensor("xm_dbg", [2, 128, 8, NT], F32, kind="ExternalOutput").ap()
        mix_dbg = nc.dram_tensor("mix_dbg", [2, 128, 8, HBW], BF16, kind="ExternalOutput").ap()
    else:
        xd = dscr("xd", [2, 128, 8, NT], F32)
    vmad = dscr("vmad", [18, 128, (4 * EW)], BF16)
    omsd = dscr("omsd", [18, 128, 256], BF16)
    cmd = dscr("cmd", [18, 128, 256], BF16)
    dbg_out = {}

    def sb(name, shape, dt=F32):
        return st.enter_context(nc.sbuf_tensor(name, list(shape), dt))

    def psb(name):
        return st.enter_context(nc.psum_tensor(name, [128, 512], F32))

    def mm(out, lhsT, rhs, start, stop, rd, wr):
        S.op("pe", lambda e, o=out, l=lhsT, r=rhs, a=start, b=stop: e.matmul(o, lhsT=l, rhs=r, start=a, stop=b), rd, wr)

    def act(out, in_, func, rd, wr, bias=None, scale=None):
        kw = {}
        if bias is not None:
            kw["bias"] = bias
        if scale is not None:
            kw["scale"] = scale
        S.op("act", lambda e, o=out, i=in_, f=func, k=kw: e.activation(out=o, in_=i, func=f, **k), rd, wr)

    def tt(eng, out, in0, in1, op, rd, wr):
        S.op(eng, lambda e, o=out, a=in0, b=in1, p=op: e.tensor_tensor(out=o, in0=a, in1=b, op=p), rd, wr)

    def ts(eng, out, in0, s1, op0, rd, wr, s2=None, op1=None):
        if op1 is None:
            S.op(eng, lambda e, o=out, a=in0, x=s1, p=op0: e.tensor_scalar(out=o, in0=a, scalar1=x, scalar2=None, op0=p), rd, wr)
        else:
            S.op(eng, lambda e, o=out, a=in0, x=s1, y=s2, p=op0, q=op1: e.tensor_scalar(out=o, in0=a, scalar1=x, scalar2=y, op0=p, op1=q), rd, wr)

    def stt(eng, out, in0, scalar, in1, op0, op1, rd, wr, tmp=None):
        if eng == "pool":
            t_ = out if tmp is None else tmp
            S.op(eng, lambda e, o=t_, a=in0, x=scalar, p=op0: e.tensor_scalar(out=o, in0=a, scalar1=x, scalar2=None, op0=p), rd, wr)
            S.op(eng, lambda e, o=out, a=t_, b=in1, p=op1: e.tensor_tensor(out=o, in0=a, in1=b, op=p), rd, wr)
            return
        S.op(eng, lambda e, o=out, a=in0, s=scalar, b=in1, p=op0, q=op1: e.scalar_tensor_tensor(out=o, in0=a, scalar=s, in1=b, op0=p, op1=q), rd, wr)

    def cp(eng, out, in_, rd, wr):
        if eng == "act":
            S.op(eng, lambda e, o=out, i=in_: e.activation(out=o, in_=i, func=AF.Identity), rd, wr)
        else:
            S.op(eng, lambda e, o=out, i=in_: e.tensor_copy(out=o, in_=i), rd, wr)

    def recip(out, in_, rd, wr):
        S.op("dve", lambda e, o=out, i=in_: e.reciprocal(out=o, in_=i), rd, wr)

    def mset(eng, ap, val, wr):
        S.op(eng, lambda e, a=ap, v=val: e.memset(a, v), (), wr)

    def dma(eng, out, in_, rd, wr):
        return S.dma(eng, lambda e, o=out, i=in_: e.dma_start(out=o, in_=i), rd, wr)

    CONST = sb("CONST", [128, 8, 128]); rCONST = Res()
    CONSTB = sb("CONSTB", [128, 2, 128], BF16); rCONSTB = Res()
    ONE1 = sb("ONE1", [1, 128]); rONE1 = Res()
    CV = sb("CV", [128, 8, 3]); SCb = sb("SCb", [128, 8, 3], BF16); rCV = Res(); rSCb = Res()
    MOD = sb("MOD", [128, DEPTH, 48, 3]); rMOD = Res()
    BADA = sb("BADA", [128, DEPTH, 48]); rBADA = Res()
    GN = sb("GN", [128, DEPTH, 2, 8]); rGN = Res()
    A12 = sb("A12", [128, DEPTH, 2, 8, 3]); rA12 = Res()
    BFM = sb("BFM", [128, DEPTH, 14]); rBFM = Res()
    BTM = sb("BTM", [1, NTM]); rBTM = Res()
    GQK = sb("GQK", [128, DEPTH, 4]); rGQK = Res()
    BSP = sb("BSP", [128, DEPTH, 4]); rBSP = Res()
    CONVP = sb("CONVP", [128, 2, 22, 4]); rCONVP = Res()
    ROPE = sb("ROPE", [128, 2, NT], BF16); rROPE = Res()
    GMV = sb("GMV", [128, 2, 256]); rGMV = Res()
    WSPb = sb("WSPb", [128, 4, 128], BF16); rWSP = Res()
    HB = sb("HB", [128, 8, HBW], BF16)
    rHB = [[Res() for _ in range(5)] for _ in range(8)]
    XT = [sb("XT%d" % i, [128, 8, 512]) for i in range(2)]; rXT = [Res(), Res()]
    rSQ = Res()
    SD = sb("SD", [128, 512]); rSD = Res()
    RS = sb("RS", [128, 512]); rRS = Res()
    WP = [sb("WP%d" % i, [128, 8, 512], BF16) for i in range(2)]; rWP = [Res(), Res()]
    ARENA = sb("ARENA", [128, 46464], BF16)
    PS = [psb("PS%d" % i) for i in range(8)]
    rPS = [Res(excl=True) for _ in range(8)]

    IDENT = CONST[:, 0, :]
    TRIF = CONST[:, 1, :]
    TRIB = CONST[:, 2, :]
    NEG1 = CONST[:, 3, :]
    MASKF = CONST[:, 4, :]
    MASKB = CONST[:, 5, :]
    ONESb = CONSTB[:, 0, :]
    BLKb = CONSTB[:, 1, :]

    class Arena:
        def __init__(self):
            self.off = 0

        def reset(self):
            self.off = 0

        def get(self, shape, dt):
            n = int(np.prod(shape[1:]))
            if dt == F32:
                n2 = 2 * n
            else:
                n2 = n
            o = self.off + (self.off % 2)
            self.off = o + n2 + (n2 % 2)
            assert self.off <= 46464, ("arena overflow", self.off)
            v = ARENA[:, o:o + n2]
            if dt == F32:
                v = v.bitcast(F32)
            v = v[0:shape[0]]
            if len(shape) == 3:
                v = v.rearrange("p (a b) -> p a b", a=shape[1])
            elif len(shape) == 4:
                v = v.rearrange("p (a b c) -> p a b c", a=shape[1], b=shape[2])
            return v

    AR = Arena()
    wp_i = [0]

    def load_w(src_ap, width, kc=8):
        i = wp_i[0] % 2
        wp_i[0] += 1
        dma("pool", WP[i][:, 0:kc, 0:width], src_ap, (), [rWP[i]])
        return WP[i], rWP[i]

    ps_i = [0]

    def psrot(n=4):
        i = ps_i[0] % n
        ps_i[0] += 1
        return PS[i], rPS[i]

    dma("sp", CONST[:], consts[:, :, :], (), [rCONST])
    cp("dve", CONSTB[:], CONST[:, 6:8, :], [rCONST], [rCONSTB])
    mset("dve", ONE1[:], 1.0, [rONE1])
    dma("sp", CV[:], cvec[:, :, :], (), [rCV])
    dma("sp", BADA[:], bada[:, :, :], (), [rBADA])
    dma("sp", GN[:], gn12[:, :, :, :], (), [rGN])
    dma("sp", BFM[:], binfm[:, :, :], (), [rBFM])
    dma("sp", GQK[:], gqk[:, :, :], (), [rGQK])
    dma("sp", BSP[:], bsp[:, :, :], (), [rBSP])
    dma("pool", ROPE[:], rope[:, :, :], (), [rROPE])
    act(CV[:], CV[:], AF.Silu, [rCV], [rCV])
    cp("dve", SCb[:], CV[:], [rCV], [rSCb])
    for k in range(8):
        mset("pool", HB[:, k, 0:1], 0.0, [rHB[k][0]])
        mset("pool", HB[:, k, 257:259], 0.0, [rHB[k][0]])
        mset("pool", HB[:, k, 2307:2308], 0.0, [rHB[k][4]])

    for l in range(nlayers):
        pm = PS[0][:, 0:144]
        for pc in range(12):
            w, rw = load_w(wada[l, pc], 512)
            for j in range(4):
                jj = pc * 4 + j
                for k in range(8):
                    mm(pm[:, jj * 3:jj * 3 + 3], w[:, k, j * 128:(j + 1) * 128], SCb[:, k, :], k == 0, k == 7,
                       [rw, rSCb], [rPS[0]])
        tt("dve", MOD[:, l, :, :], pm.rearrange("p (a b) -> p a b", b=3),
           BADA[:, l, :].unsqueeze(2).to_broadcast([128, 48, 3]), ALU.add, [rPS[0], rBADA], [rMOD])
        for i in range(2):
            ts("dve", A12[:, l, i, :, :], MOD[:, l, 8 + 24 * i:16 + 24 * i, :], 1.0, ALU.add, [rMOD], [rA12])
            tt("dve", A12[:, l, i, :, :], A12[:, l, i, :, :],
               GN[:, l, i, :].unsqueeze(2).to_broadcast([128, 8, 3]), ALU.mult, [rA12, rGN], [rA12])

    def norm_tile(l, which, m, n, xt, rxt, SQ, rsq=None):
        rSQ_ = rsq if rsq is not None else rSQ
        t0, w = TT[n]
        c0 = hcol(t0)
        act(SQ[:, :, 0:w], xt[:, :, 0:w], AF.Square, [rxt], [rSQ_])
        pss, rpss = PS[4], rPS[4]
        for k in range(8):
            mm(pss[:, 0:w], ONESb, SQ[:, k, 0:w], k == 0, k == 7, [rSQ_, rCONSTB], [rpss])
        act(SD[:, 0:w], pss[:, 0:w], AF.Sqrt, [rpss], [rSD], bias=EPS, scale=1.0 / D)
        recip(RS[:, 0:w], SD[:, 0:w], [rSD], [rRS])
        tt("dve", xt[:, :, 0:w], xt[:, :, 0:w], RS[:, 0:w].unsqueeze(1).to_broadcast([128, 8, w]), ALU.mult,
           [rxt, rRS], [rxt])
        sh = 0 if which == 0 else 24
        for k in range(8):
            act(HB[:, k, c0:c0 + w], xt[:, k, 0:w], AF.Identity, [rxt, rA12, rMOD], [rHB[k][n]],
                bias=MOD[:, l, sh + k, m:m + 1], scale=A12[:, l, which, k, m:m + 1])

    xt_i = [0]

    def xt_next():
        i = xt_i[0] % 2
        xt_i[0] += 1
        return XT[i], rXT[i]

    rXD = [[Res() for _ in range(5)] for _ in range(2)]

    def layer(b, l):
        last = (l == DEPTH - 1)
        xsrc = xin if l == 0 else xd
        ctx_m = 2
        dma("sp", GMV[:, 0, :], gmv[l, 0:1, :].partition_broadcast(128), (), [rGMV])
        dma("sp", GMV[:, 1, :], gmv[l, 1:2, :].partition_broadcast(128), (), [rGMV])
        dma("pool", WSPb[:], wsp[l], (), [rWSP])
        dma("sp", BTM[:], bintm[0:1, l, :], (), [rBTM])
        dma("sp", CONVP[:], convp[:, l, :, :, :], (), [rCONVP])
        S.barrier()
        AR.reset()
        QK = AR.get([128, 5, NT], BF16); rQK = [[Res() for _ in range(5)] for _ in range(5)]
        QM = AR.get([128, 4, NT], BF16); rQM = [[Res() for _ in range(5)] for _ in range(4)]
        VA = AR.get([128, 18, 256], BF16); rVA = [Res() for _ in range(18)]
        GT = AR.get([128, 18, 16], F32); rGT = [Res() for _ in range(18)]
        DC = AR.get([128, 36, (2 * EW)], BF16); rDC = [Res() for _ in range(36)]
        CBF = AR.get([128, 36, (2 * EW)], BF16); rCBF = [Res() for _ in range(36)]
        SCAL = AR.get([128, 18, 32], F32); rSCAL = [Res() for _ in range(18)]
        SM = AR.get([128, 64], F32); rSM = Res()
        CST = AR.get([128, 2, (2 * EW)], F32); rCST = Res()
        mark_tmp = AR.off
        norm_done = [0]

        def ensure_norm(upto):
            while norm_done[0] <= upto and norm_done[0] < 5:
                n = norm_done[0]
                norm_done[0] += 1
                t0, w = TT[n]
                m = ctx_m if n == 0 else b
                xt, rxt = xt_next()
                dma("sp", xt[:, :, 0:w], xsrc[b, :, :, t0:t0 + w], [rXD[b][n]], [rxt])
                norm_tile(l, 0, m, n, xt, rxt, WP[1], rWP[1])

        ensure_norm(1)
        wp_i[0] += wp_i[0] % 2
        F1 = AR.get([128, 512], F32); rF1 = Res()
        F2 = AR.get([128, 512], F32); rF2 = Res()
        SQH = AR.get([128, 512], BF16); rSQH = Res()
        RSQ = AR.get([128, 512], F32); rRSQ = Res()
        T1 = AR.get([128, 512], F32); rT1 = Res()
        T2 = AR.get([128, 512], F32); rT2 = Res()
        UG = AR.get([128, 512], F32); rUG = Res()
        VCN = AR.get([128, 256], BF16); rVCN = Res()
        KH = AR.get([128, 2, 256], BF16); rKH = Res()
        VMAt2 = AR.get([128, (4 * EW)], BF16); VMAt = VMAt2.rearrange("p (h d) -> p h d", h=4); rVMAt = Res()
        OMSt = AR.get([128, 256], BF16); rOMSt = Res()
        CMt = AR.get([128, 256], BF16); rCMt = Res()

        if stop <= 1:
            return
        for i in range(18):
            mset("pool", VA[:, i, :].rearrange("p (a b) -> p a b", a=2)[:, :, 64:128], 1.0, [rVA[i]])

        def fm_matmul(w, rw, cc, n, pst, rpst):
            t0, wd = TT[n]
            c0 = hcol(t0)
            for k in range(8):
                mm(pst[:, 0:wd], w[:, k, cc * 128:(cc + 1) * 128], HB[:, k, c0:c0 + wd], k == 0, k == 7,
                   [rw, rHB[k][n]], [rpst])

        for pi in range(7):
            w, rw = load_w(winfm[l, pi], 256)
            if pi < 5:
                gcol = 0 if pi < 4 else 2
                for n in range(5):
                    ensure_norm(n + 2)
                    t0, wd = TT[n]
                    pa, rpa = psrot()
                    pb, rpb = psrot()
                    fm_matmul(w, rw, 0, n, pa, rpa)
                    fm_matmul(w, rw, 1, n, pb, rpb)
                    ba = BFM[:, l, 2 * pi:2 * pi + 1]
                    bb = BFM[:, l, 2 * pi + 1:2 * pi + 2]
                    act(SQH[:, 0:wd], pa[:, 0:wd], AF.Square, [rpa, rBFM], [rSQH], bias=ba)
                    act(F1[:, 0:wd], pa[:, 0:wd], AF.Identity, [rpa, rBFM], [rF1], bias=ba)
                    act(F2[:, 0:wd], pb[:, 0:wd], AF.Identity, [rpb, rBFM], [rF2], bias=bb)
                    pss, rpss = PS[4], rPS[4]
                    mm(pss[:, 0:wd], BLKb, SQH[:, 0:wd], True, True, [rSQH, rCONSTB], [rpss])
                    act(RSQ[:, 0:wd], pss[:, 0:wd], AF.Sqrt, [rpss], [rRSQ], bias=EPS, scale=1.0 / HD)
                    recip(RSQ[:, 0:wd], RSQ[:, 0:wd], [rRSQ], [rRSQ])
                    stt("dve", T1[:, 0:wd], F1[:, 0:wd], GQK[:, l, gcol:gcol + 1], ROPE[:, 0, t0:t0 + wd],
                        ALU.mult, ALU.mult, [rF1, rGQK, rROPE], [rT1])
                    stt("dve", T2[:, 0:wd], F2[:, 0:wd], GQK[:, l, gcol + 1:gcol + 2], ROPE[:, 1, t0:t0 + wd],
                        ALU.mult, ALU.mult, [rF2, rGQK, rROPE], [rT2])
                    tt("dve", T1[:, 0:wd], T1[:, 0:wd], T2[:, 0:wd], ALU.add, [rT1, rT2], [rT1])
                    tt("dve", QK[:, pi, t0:t0 + wd], T1[:, 0:wd], RSQ[:, 0:wd], ALU.mult, [rT1, rRSQ], [rQK[pi][n]])
            else:
                for cc in range(2):
                    ci = (pi - 5) * 2 + cc
                    for n in range(5):
                        t0, wd = TT[n]
                        pa, rpa = psrot()
                        fm_matmul(w, rw, cc, n, pa, rpa)
                        act(QM[:, ci, t0:t0 + wd], pa[:, 0:wd], AF.Identity, [rpa, rBFM], [rQM[ci][n]],
                            bias=BFM[:, l, 10 + ci:11 + ci])

        if stop <= 2:
            return
        def tm_matmul(w, rw, off, wd, i, pst, rpst):
            t0 = i * 128
            c0 = hcol(t0)
            n = 0 if i < 2 else 1 + (i - 2) // 4
            for k in range(8):
                mm(pst[:, 0:wd], HB[:, k, c0:c0 + 128], w[:, k, 0:wd], k == 0, False, [rw, rHB[k][n]], [rpst])
            mm(pst[:, 0:wd], ONE1[0:1, :], BTM[0:1, off:off + wd], False, True, [rONE1, rBTM], [rpst])

        w, rw = load_w(wintm[l, :, :, 0:16], 16)
        for i in range(18):
            pa, rpa = psrot()
            tm_matmul(w, rw, 0, 16, i, pa, rpa)
            cp("dve", GT[:, i, :], pa[:, 0:16], [rpa], [rGT[i]])
            gv = GT[:, i, :].rearrange("p (d t h) -> p d t h", d=2, t=2)
            fpre = gv[:, :, 1, :]
            ipre = gv[:, :, 0, :]
            spl = SM[:, 0:8].rearrange("p (d h) -> p d h", d=2)
            act(spl, fpre, AF.Exp, [rGT[i]], [rSM], scale=-1.0)
            act(spl, spl, AF.Ln, [rSM], [rSM], bias=1.0)
            pg, rpg = PS[5], rPS[5]
            mm(pg[:, 0:4], TRIF, SM[:, 0:4], True, True, [rSM, rCONST], [rpg])
            mm(pg[:, 4:8], TRIB, SM[:, 4:8], True, True, [rSM, rCONST], [rpg])
            mm(pg[:, 8:16], NEG1, SM[:, 0:8], True, True, [rSM, rCONST], [rpg])
            a_ = SM[:, 8:16]
            tt("dve", a_.rearrange("p (d h) -> p d h", d=2), ipre, pg[:, 0:8].rearrange("p (d h) -> p d h", d=2),
               ALU.subtract, [rGT[i], rpg], [rSM])
            act(SCAL[:, i, 0:8], a_, AF.Exp, [rSM], [rSCAL[i]], bias=LN8)
            tt("dve", SM[:, 16:24], a_, pg[:, 8:16], ALU.add, [rSM, rpg], [rSM])
            act(SCAL[:, i, 8:16], SM[:, 16:24], AF.Exp, [rSM], [rSCAL[i]], bias=LN8)
            act(SCAL[:, i, 16:24], pg[:, 0:8], AF.Exp, [rpg], [rSCAL[i]], scale=-1.0)
            act(SCAL[:, i, 24:32], pg[:, 8:16], AF.Exp, [rpg], [rSCAL[i]])

        if stop <= 3:
            return
        w, rw = load_w(wintm[l, :, :, 16:528], 512)
        for i in range(18):
            pa, rpa = psrot()
            tm_matmul(w, rw, 16, 512, i, pa, rpa)
            if stop <= 3.1:
                continue
            mset("pool", VMAt2, 0.0, [rVMAt])
            mset("pool", VMAt[:, :, 64:65], 1.0, [rVMAt])
            cp("act", VMAt[:, :, 0:64], pa[:, 0:256].rearrange("p (h d) -> p h d", h=4), [rpa], [rVMAt])
            dma("sp", vmad[i], VMAt2, [rVMAt], [])
            if stop <= 3.2:
                continue
            for d_ in range(2):
                tt("dve", KH[:, d_, :].rearrange("p (h d) -> p h d", h=4),
                   pa[:, 256:512].rearrange("p (h d) -> p h d", h=4),
                   SCAL[:, i, 8 + 4 * d_:12 + 4 * d_].unsqueeze(2).to_broadcast([128, 4, 64]), ALU.mult,
                   [rpa, rSCAL[i], rVMAt], [rKH])
            if stop <= 3.3:
                continue
            for d_ in range(2):
                for pr in range(2):
                    mm(PS[6 + d_][:, pr * (2 * EW):(pr + 1) * (2 * EW)], KH[:, d_, pr * 128:(pr + 1) * 128],
                       VMAt[:, 2 * pr:2 * pr + 2, :], True, True, [rKH, rVMAt], [rPS[6 + d_]])
            if stop <= 3.4:
                continue
            for d_ in range(2):
                for pr in range(2):
                    for hh in range(2):
                        cp("act", DC[64 * hh:64 * hh + 64, d_ * 18 + i, pr * EW:(pr + 1) * EW],
                           PS[6 + d_][64 * hh:64 * hh + 64, pr * (2 * EW) + hh * EW:pr * (2 * EW) + hh * EW + EW],
                           [rPS[6 + d_]], [rDC[d_ * 18 + i]])
        if stop <= 3.5:
            return

        mset("dve", CST[:], 0.0, [rCST])
        orders = [list(range(18)), [1, 0] + list(range(17, 1, -1))]
        for d_ in range(2):
            eng = "dve"
            for i in orders[d_]:
                cp(eng, CBF[:, d_ * 18 + i, :], CST[:, d_, :], [rCST], [rCBF[d_ * 18 + i]])
                for pr in range(2):
                    for hh in range(2):
                        col = 24 + d_ * 4 + 2 * pr + hh
                        stt(eng, CST[64 * hh:64 * hh + 64, d_, pr * EW:(pr + 1) * EW],
                            CST[64 * hh:64 * hh + 64, d_, pr * EW:(pr + 1) * EW],
                            SCAL[64 * hh:64 * hh + 64, i, col:col + 1],
                            DC[64 * hh:64 * hh + 64, d_ * 18 + i, pr * EW:(pr + 1) * EW],
                            ALU.mult, ALU.add, [rCST, rSCAL[i], rDC[d_ * 18 + i]], [rCST])

        if stop <= 4:
            return
        w, rw = load_w(wintm[l, :, :, 528:912], 384)
        for i in range(18):
            pa, rpa = psrot()
            tm_matmul(w, rw, 528, 384, i, pa, rpa)
            cp("act", VA[:, i, :].rearrange("p (a b) -> p a b", a=2)[:, :, 0:64],
               pa[:, 0:128].rearrange("p (a b) -> p a b", a=2), [rpa], [rVA[i]])
            act(OMSt[:], pa[:, 128:384], AF.Sigmoid, [rpa], [rOMSt])
            dma("sp", omsd[i], OMSt[:], [rOMSt], [])

        w, rw = load_w(wintm[l, :, :, 912:1424], 512)
        for i in range(18):
            pa, rpa = psrot()
            tm_matmul(w, rw, 912, 512, i, pa, rpa)
            act(UG[:], pa[:, 0:512], AF.Gelu_apprx_tanh, [rpa], [rUG])
            ugv = UG[:, 256:512].rearrange("p (g d) -> p g d", g=4)
            tt("dve", T1[:, 0:256], UG[:, 256:512], UG[:, 256:512], ALU.mult, [rUG], [rT1])
            S.op("dve", lambda e, o=SM[:, 32:36], a=T1[:, 0:256].rearrange("p (g d) -> p g d", g=4):
                 e.tensor_reduce(out=o, in_=a, axis=AX.X, op=ALU.add), [rT1], [rSM])
            act(SM[:, 32:36], SM[:, 32:36], AF.Sqrt, [rSM], [rSM], bias=EPS, scale=1.0 / 64)
            recip(SM[:, 36:40], SM[:, 32:36], [rSM], [rSM])
            tt("dve", T1[:, 0:256].rearrange("p (g d) -> p g d", g=4), ugv,
               SM[:, 36:40].unsqueeze(2).to_broadcast([128, 4, 64]), ALU.mult, [rUG, rSM], [rT1])
            tt("dve", VCN[:], T1[:, 0:256], GMV[:, 1, :], ALU.mult, [rT1, rGMV], [rVCN])
            pz, rpz = PS[5], rPS[5]
            for g in range(4):
                mm(pz[:, g * 64:(g + 1) * 64], WSPb[:, g, :], VCN[:, g * 64:(g + 1) * 64], True, True,
                   [rWSP, rVCN], [rpz])
            tt("dve", T2[:, 0:256].rearrange("p (g d) -> p g d", g=4), pz[:, 0:256].rearrange("p (g d) -> p g d", g=4),
               BSP[:, l, :].unsqueeze(2).to_broadcast([128, 4, 64]), ALU.add, [rpz, rBSP], [rT2])
            tt("dve", CMt[:], T2[:, 0:256], UG[:, 0:256], ALU.mult, [rT2, rUG], [rCMt])
            dma("sp", cmd[i], CMt[:], [rCMt], [])

        if stop <= 5:
            return
        S.barrier()
        AR.off = mark_tmp
        T1 = AR.get([128, 512], F32); rT1 = Res()
        T2 = AR.get([128, 512], F32); rT2 = Res()
        AT = AR.get([128, 2, 4, 128], BF16); rAT = Res()
        PT = [AR.get([128, 512], BF16) for _ in range(3)]; rPT = [Res(), Res(), Res()]
        OTM = AR.get([128, 512], F32); rOTM = Res()
        VMAt2 = AR.get([128, (4 * EW)], BF16); VMAt = VMAt2.rearrange("p (h d) -> p h d", h=4); rVMAt = Res()
        OMSt = AR.get([128, 256], BF16); rOMSt = Res()
        CMt = AR.get([128, 256], BF16); rCMt = Res()
        RD = AR.get([128, 512], F32); rRD = Res()
        HS = AR.get([128, 256], F32); rHS = Res()
        rMIX = [[Res() for _ in range(18)] for _ in range(8)]

        def mixcols(i):
            return hcol(i * 128)

        qblocks = list(range(2, 18)) + ([] if last else [0, 1])
        its = []
        for qb in qblocks:
            ktiles = list(range(18)) if qb >= 2 else [0, 1]
            for j in range(2):
                for kk, kt in enumerate(ktiles):
                    its.append((qb, j, kk, kt, len(ktiles)))

        def a_score(n_):
            qb, j, kk, kt, nkt = its[n_]
            q0 = qb * 128
            nq = 0 if qb < 2 else 1 + (qb - 2) // 4
            nk = 0 if kt < 2 else 1 + (kt - 2) // 4
            pscore, rpscore = psrot()
            mm(pscore[:, 0:512], QK[64 * j:64 * j + 64, 4, kt * 128:(kt + 1) * 128],
               QK[64 * j:64 * j + 64, 0:4, q0:q0 + 128], True, True,
               [rQK[4][nk]] + [rQK[c][nq] for c in range(4)], [rpscore])
            pi_ = n_ % 3
            act(PT[pi_][:], pscore[:, 0:512], AF.Exp, [rpscore], [rPT[pi_]], scale=0.125)

        def a_pv(n_):
            qb, j, kk, kt, nkt = its[n_]
            po, rpo = PS[4 + j], rPS[4 + j]
            pi_ = n_ % 3
            mm(po[:, 0:512], VA[:, kt, j * 128:(j + 1) * 128], PT[pi_][:], kk == 0, kk == nkt - 1,
               [rVA[kt], rPT[pi_]], [rpo])
            if kk == nkt - 1:
                recip(RD[64:128, :], po[64:128, :], [rpo], [rRD])
                c0 = mixcols(qb)
                tt("dve", HB[64 * j:64 * j + 64, 0:4, c0:c0 + 128],
                   po[0:64, 0:512].rearrange("p (c t) -> p c t", c=4),
                   RD[64:128, :].rearrange("p (c t) -> p c t", c=4), ALU.mult, [rpo, rRD],
                   [rMIX[c][qb] for c in range(4)])

        for n_ in range(len(its)):
            a_score(n_)
            if n_ >= 2:
                a_pv(n_ - 2)
        a_pv(len(its) - 2)
        a_pv(len(its) - 1)

        if stop <= 6:
            return
        tiles2 = list(range(18)) if not last else list(range(2, 18))
        for i in tiles2:
            t0 = i * 128
            n = 0 if i < 2 else 1 + (i - 2) // 4
            if True:
                dma("sp", VMAt2, vmad[i], [], [rVMAt])
                dma("sp", OMSt[:], omsd[i], [], [rOMSt])
                dma("sp", CMt[:], cmd[i], [], [rCMt])
            pscs = [psrot(), psrot()]
            for h in (0, 2, 1, 3):
                ci, hh = h // 2, h % 2
                mm(pscs[hh][0][:, ci * 128:(ci + 1) * 128], QM[64 * hh:64 * hh + 64, 2 + ci, t0:t0 + 128],
                   QM[64 * hh:64 * hh + 64, ci, t0:t0 + 128], True, True, [rQM[2 + ci][n], rQM[ci][n]], [pscs[hh][1]])
            for d_ in range(2):
                msk = MASKF if d_ == 0 else MASKB
                for h in range(4):
                    ci, hh = h // 2, h % 2
                    stt("dve", AT[:, d_, h, :], pscs[hh][0][:, ci * 128:(ci + 1) * 128],
                        SCAL[:, i, d_ * 4 + h:d_ * 4 + h + 1], msk, ALU.mult, ALU.mult,
                        [pscs[hh][1], rSCAL[i], rCONST], [rAT])
            if stop <= 6.1:
                continue
            for d_ in range(2):
                pout, rpout = PS[6 + d_], rPS[6 + d_]
                for h in range(4):
                    ci, hh = h // 2, h % 2
                    mm(pout[:, h * EW:(h + 1) * EW], AT[:, d_, h, :], VMAt[:, h, :], True, False,
                       [rAT, rVMAt], [rpout])
                    mm(pout[:, h * EW:(h + 1) * EW], QM[64 * hh:64 * hh + 64, ci, t0:t0 + 128],
                       CBF[64 * hh:64 * hh + 64, d_ * 18 + i, ci * EW:(ci + 1) * EW], False, True,
                       [rQM[ci][n], rCBF[d_ * 18 + i]], [rpout])
            if stop <= 6.2:
                continue
            for d_ in range(2):
                pout, rpout = PS[6 + d_], rPS[6 + d_]
                pv = pout[:, 0:(4 * EW)].rearrange("p (h e) -> p h e", h=4)
                den = SM[:, 40 + 4 * d_:44 + 4 * d_]
                act(den, pv[:, :, 64], AF.Abs, [rpout], [rSM])
                tt("dve", den, den, SCAL[:, i, 16 + 4 * d_:20 + 4 * d_], ALU.max, [rSM, rSCAL[i]], [rSM])
                recip(den, den, [rSM], [rSM])
                dst = HS if d_ == 0 else T1[:, 0:256]
                tt("dve", dst.rearrange("p (h e) -> p h e", h=4), pv[:, :, 0:64],
                   den.unsqueeze(2).to_broadcast([128, 4, 64]), ALU.mult, [rpout, rSM], [rHS if d_ == 0 else rT1])
            tt("dve", HS[:], HS[:], T1[:, 0:256], ALU.add, [rHS, rT1], [rHS])
            if stop <= 6.3:
                continue
            tt("dve", T2[:, 0:256], HS[:], HS[:], ALU.mult, [rHS], [rT2])
            S.op("dve", lambda e, o=SM[:, 48:52], a=T2[:, 0:256].rearrange("p (g d) -> p g d", g=4):
                 e.tensor_reduce(out=o, in_=a, axis=AX.X, op=ALU.add), [rT2], [rSM])
            act(SM[:, 48:52], SM[:, 48:52], AF.Sqrt, [rSM], [rSM], bias=EPS, scale=1.0 / 64)
            recip(SM[:, 52:56], SM[:, 48:52], [rSM], [rSM])
            tt("dve", HS[:].rearrange("p (h e) -> p h e", h=4), HS[:].rearrange("p (h e) -> p h e", h=4),
               SM[:, 52:56].unsqueeze(2).to_broadcast([128, 4, 64]), ALU.mult, [rHS, rSM], [rHS])
            tt("dve", HS[:], HS[:], GMV[:, 0, :], ALU.mult, [rHS, rGMV], [rHS])
            tt("dve", OTM[:, 0:256], HS[:], OMSt[:], ALU.mult, [rHS, rOMSt], [rOTM])
            cp("pool", OTM[:, 256:512], CMt[:], [rCMt], [rOTM])
            if stop <= 6.4:
                continue
            ptr, rptr = psrot()
            for c in range(4):
                S.op("pe", lambda e, o=ptr[:, c * 128:(c + 1) * 128], a=OTM[:, c * 128:(c + 1) * 128]:
                     e.transpose(out=o, in_=a, identity=IDENT), [rOTM, rCONST], [rptr])
            c0 = mixcols(i)
            cp("act", HB[:, 4:8, c0:c0 + 128], ptr[:, 0:512].rearrange("p (c t) -> p c t", c=4), [rptr],
               [rMIX[c][i] for c in range(4, 8)])

        if stop <= 7:
            return
        S.barrier()
        if dbg:
            dma("sp", mix_dbg[b], HB[:], [], [])
            S.barrier()
        AR.reset()
        WOUT = AR.get([128, 8, 1024], BF16); rWOUT = [Res(), Res()]
        SQ = AR.get([128, 8, 512], BF16)
        for hf in range(2):
            dma("pool", WOUT[:, :, hf * 512:(hf + 1) * 512], wout[l, :, :, hf * 512:(hf + 1) * 512], (), [rWOUT[hf]])
        tiles3 = list(range(5)) if not last else list(range(1, 5))
        for n in tiles3:
            t0, w = TT[n]
            c0 = hcol(t0)
            m = ctx_m if n == 0 else b
            xt, rxt = xt_next()
            dma("sp", xt[:, :, 0:w], xsrc[b, :, :, t0:t0 + w], [rXD[b][n]], [rxt])
            for mo in range(8):
                pa, rpa = psrot()
                for k in range(8):
                    mm(pa[:, 0:w], WOUT[:, k, mo * 128:(mo + 1) * 128], HB[:, k, c0:c0 + w], k == 0, k == 7,
                       [rWOUT[mo // 4], rHB[k][n]], [rpa])
                stt("dve", xt[:, mo, 0:w], pa[:, 0:w], MOD[:, l, 16 + mo, m:m + 1], xt[:, mo, 0:w],
                    ALU.mult, ALU.add, [rpa, rMOD, rxt], [rxt])
            dma("sp", xd[b, :, :, t0:t0 + w], xt[:, :, 0:w], [rxt], [rXD[b][n]])
            if dbg:
                dma("sp", xm_dbg[b, :, :, t0:t0 + w], xt[:, :, 0:w], [rxt], [])
            norm_tile(l, 1, m, n, xt, rxt, SQ)

        if stop <= 8:
            return
        S.barrier()
        AR.reset()
        WUP = AR.get([128, 8, 2816], BF16); rWUP = [Res() for _ in range(6)]
        WDN = AR.get([128, 11, 1024], BF16); rWDN = [Res(), Res()]
        U = AR.get([128, 11, 512], BF16); rU = [Res() for _ in range(11)]
        TG = [AR.get([128, 512], F32) for _ in range(2)]; rTG = [Res(), Res()]
        TV = [AR.get([128, 512], F32) for _ in range(2)]; rTV = [Res(), Res()]
        SG = [AR.get([128, 512], F32) for _ in range(2)]; rSG = [Res(), Res()]
        wins = []
        if not last:
            wins.append((0, 258, 0, 0))
        for i in range(4):
            wins.append((258 + 510 * i, 512, 256 + 510 * i, None))
        wins.append((258 + 2040, 10, 256 + 2040, None))
        rXW = [Res() for _ in range(len(wins))]
        gi = [0]
        for hf in range(2):
            for pc in range(6):
                wd = 512 if pc < 5 else 256
                dma("pool", WUP[:, :, pc * 512:pc * 512 + wd], wup[l, hf, :, :, pc * 512:pc * 512 + wd], (), [rWUP[pc]])
            for pc in range(2):
                dma("pool", WDN[:, :, pc * 512:(pc + 1) * 512], wdown[l, hf, :, :, pc * 512:(pc + 1) * 512], (), [rWDN[pc]])
            for wi, (c0, w, tok0, _) in enumerate(wins):
                wo = w - 2
                m = ctx_m if tok0 < 256 else b
                hbres = [rHB[k][nn] for k in range(8) for nn in range(5)]
                for g in range(11):
                    pg_, rpg_ = psrot()
                    pv_, rpv_ = psrot()
                    for k in range(8):
                        mm(pg_[:, 0:w], WUP[:, k, g * 128:(g + 1) * 128], HB[:, k, c0:c0 + w], k == 0, k == 7,
                           [rWUP[(g * 128) // 512]] + (hbres if k == 0 else []), [rpg_])
                    for k in range(8):
                        mm(pv_[:, 0:w], WUP[:, k, 1408 + g * 128:1408 + (g + 1) * 128], HB[:, k, c0:c0 + w], k == 0, k == 7,
                           [rWUP[(1408 + g * 128) // 512]], [rpv_])
                    x_ = gi[0] % 2
                    gi[0] += 1
                    cg = CONVP[:, hf, g, :]
                    cv_ = CONVP[:, hf, 11 + g, :]
                    act(TG[x_][:, 0:wo], pg_[:, 1:1 + wo], AF.Identity, [rpg_, rCONVP], [rTG[x_]],
                        bias=cg[:, 3:4], scale=cg[:, 1:2])
                    stt("dve", TG[x_][:, 0:wo], pg_[:, 0:wo], cg[:, 0:1], TG[x_][:, 0:wo], ALU.mult, ALU.add,
                        [rpg_, rTG[x_]], [rTG[x_]])
                    stt("dve", TG[x_][:, 0:wo], pg_[:, 2:2 + wo], cg[:, 2:3], TG[x_][:, 0:wo], ALU.mult, ALU.add,
                        [rpg_, rTG[x_]], [rTG[x_]])
                    act(TV[x_][:, 0:wo], pv_[:, 1:1 + wo], AF.Identity, [rpv_, rCONVP], [rTV[x_]],
                        bias=cv_[:, 3:4], scale=cv_[:, 1:2])
                    stt("dve", TV[x_][:, 0:wo], pv_[:, 0:wo], cv_[:, 0:1], TV[x_][:, 0:wo], ALU.mult, ALU.add,
                        [rpv_, rTV[x_]], [rTV[x_]])
                    stt("dve", TV[x_][:, 0:wo], pv_[:, 2:2 + wo], cv_[:, 2:3], TV[x_][:, 0:wo], ALU.mult, ALU.add,
                        [rpv_, rTV[x_]], [rTV[x_]])
                    act(SG[x_][:, 0:wo], TG[x_][:, 0:wo], AF.Silu, [rTG[x_]], [rSG[x_]])
                    tt("dve", U[:, g, 0:wo], SG[x_][:, 0:wo], TV[x_][:, 0:wo], ALU.mult, [rSG[x_], rTV[x_]], [rU[g]])
                xt, rxt = xt_next()
                dma("sp", xt[:, :, 0:wo], xd[b, :, :, tok0:tok0 + wo], [rXW[wi]], [rxt])
                for mo in range(8):
                    pa, rpa = psrot()
                    for kc in range(11):
                        mm(pa[:, 0:wo], WDN[:, kc, mo * 128:(mo + 1) * 128], U[:, kc, 0:wo], kc == 0, kc == 10,
                           [rWDN[mo // 4], rU[kc]], [rpa])
                    stt("dve", xt[:, mo, 0:wo], pa[:, 0:wo], MOD[:, l, 40 + mo, m:m + 1], xt[:, mo, 0:wo],
                        ALU.mult, ALU.add, [rpa, rMOD, rxt], [rxt])
                if last and hf == 1:
                    tk = dma("sp", yout[b, :, :, tok0 - 256:tok0 - 256 + wo], xt[:, :, 0:wo], [rxt], [rXW[wi]])
                    out_toks.append(tk)
                else:
                    dma("sp", xd[b, :, :, tok0:tok0 + wo], xt[:, :, 0:wo], [rxt], [rXW[wi]])
        S.barrier()

    out_toks = []
    for b in range(nb):
        for l in range(nlayers):
            if stop > 0:
                layer(b, l)
    S.barrier()
    S.emit(nc, st)
    st.close()
    return nc


def _rope_tables():
    n = S_LAT
    grid_w = 64
    nf = HD // 4
    t = np.arange(n)
    row = (t // grid_w).astype(np.float32)
    col = (t % grid_w).astype(np.float32)
    inv = (10000.0 ** (-np.arange(nf, dtype=np.float32) / nf)).astype(np.float32)
    ang = np.concatenate([row[:, None] * inv[None], col[:, None] * inv[None]], axis=-1)
    cos = np.cos(ang).astype(np.float32).reshape(n, 2, nf)
    sin = np.sin(ang).astype(np.float32).reshape(n, 2, nf)
    C = np.ones((HD, NT), np.float32)
    Sg = np.zeros((HD, NT), np.float32)
    for ax in range(2):
        for half in range(2):
            rows = slice(ax * 32 + half * 16, ax * 32 + half * 16 + 16)
            C[rows, T_CTX:] = cos[:, ax, :].T
            Sg[rows, T_CTX:] = (-1.0 if half == 0 else 1.0) * sin[:, ax, :].T
    tab = np.stack([np.concatenate([C, C], 0), np.concatenate([Sg, Sg], 0)], axis=1)
    return np.ascontiguousarray(tab)


def _consts():
    c = np.zeros((128, 8, 128), np.float32)
    r = np.arange(128)[:, None]
    s = np.arange(128)[None, :]
    c[:, 0, :] = (r == s)
    c[:, 1, :] = -1.0 * (r <= s)
    c[:, 2, :] = -1.0 * (r >= s)
    c[:, 3, :] = -1.0
    c[:, 4, :] = (r <= s)
    c[:, 5, :] = (r >= s)
    c[:, 6, :] = 1.0
    c[:, 7, :] = ((r // 64) == (s // 64))
    return c


def _swap_idx():
    d = np.arange(HD)
    ax, half, f = d // 32, (d // 16) % 2, d % 16
    return ax * 32 + (1 - half) * 16 + f


def _prep_shared(inp):
    f = np.float32
    w_ada = inp["w_ada"]; w_in = inp["w_in"]; b_in = inp["b_in"]
    sh = {}
    sh["wada"] = np.ascontiguousarray(w_ada.reshape(DEPTH, 8, 128, 12, 512).transpose(0, 3, 2, 1, 4))
    sh["bada"] = np.ascontiguousarray(inp["b_ada"].reshape(DEPTH, 48, 128).transpose(2, 0, 1))
    gn = np.stack([inp["g_norm1"], inp["g_norm2"]], axis=1)
    sh["gn12"] = np.ascontiguousarray(gn.reshape(DEPTH, 2, 8, 128).transpose(3, 0, 1, 2))
    sw = _swap_idx()
    QA, KA, VAo, QMo, KMo, VMo, OMo, GTo, UCo, VCo = 0, 512, 640, 768, 1024, 1280, 1536, 1792, 1808, 2064
    fm_cols = []
    for c in range(4):
        h0, h1 = c, 4 + c
        q = np.concatenate([QA + h0 * 64 + np.arange(64), QA + h1 * 64 + np.arange(64)])
        qs = np.concatenate([QA + h0 * 64 + sw, QA + h1 * 64 + sw])
        fm_cols += [q, qs]
    k = np.concatenate([KA + np.arange(64), KA + 64 + np.arange(64)])
    ks = np.concatenate([KA + sw, KA + 64 + sw])
    fm_cols += [k, ks]
    fm_cols += [QMo + np.arange(128), QMo + 128 + np.arange(128), KMo + np.arange(128), KMo + 128 + np.arange(128)]
    fm_idx = np.concatenate(fm_cols)
    wfm = w_in[:, :, fm_idx]
    sh["winfm"] = np.ascontiguousarray(wfm.reshape(DEPTH, 8, 128, 7, 256).transpose(0, 3, 2, 1, 4))
    sh["binfm"] = np.ascontiguousarray(b_in[:, fm_idx].reshape(DEPTH, 14, 128).transpose(2, 0, 1))
    tm_idx = np.concatenate([GTo + np.arange(16), VMo + np.arange(256), KMo + np.arange(256), VAo + np.arange(128),
                             OMo + np.arange(256), UCo + np.arange(256), VCo + np.arange(256)])
    wtm = w_in[:, :, tm_idx]
    sh["wintm"] = np.ascontiguousarray(wtm.reshape(DEPTH, 8, 128, NTM).transpose(0, 2, 1, 3))
    sh["bintm"] = np.ascontiguousarray(b_in[:, tm_idx][None])
    gq = inp["g_q"]; gk = inp["g_k"]
    g4 = np.stack([np.tile(gq, (1, 2)), np.tile(gq[:, sw], (1, 2)), np.tile(gk, (1, 2)), np.tile(gk[:, sw], (1, 2))], axis=2)
    sh["gqk"] = np.ascontiguousarray(g4.transpose(1, 0, 2))
    sh["rope"] = _rope_tables()
    sh["gmv"] = np.ascontiguousarray(np.stack([inp["g_mh"], inp["g_v"]], axis=1))
    sh["wsp"] = np.ascontiguousarray(inp["w_sp"].transpose(0, 3, 1, 2))
    sh["bsp"] = np.ascontiguousarray(inp["b_sp"].transpose(2, 0, 1))
    rows = []
    for c in range(4):
        rows += [c * 64 + np.arange(64), (4 + c) * 64 + np.arange(64)]
    rows += [512 + np.arange(512)]
    ridx = np.concatenate(rows)
    wo = inp["w_out"][:, ridx, :]
    sh["wout"] = np.ascontiguousarray(wo.reshape(DEPTH, 8, 128, 1024).transpose(0, 2, 1, 3))
    w_up = inp["w_up"]
    ucols = np.stack([np.concatenate([hf * 1408 + np.arange(1408), DFF + hf * 1408 + np.arange(1408)]) for hf in range(2)])
    wu = w_up[:, :, ucols]
    sh["wup"] = np.ascontiguousarray(wu.reshape(DEPTH, 8, 128, 2, 2816).transpose(0, 3, 2, 1, 4))
    cw = inp["conv_w"]; cb = inp["conv_b"]
    cpar = np.concatenate([cw, cb[:, None, :]], axis=1)
    cpar = cpar[:, :, ucols]
    sh["convp"] = np.ascontiguousarray(cpar.reshape(DEPTH, 4, 2, 22, 128).transpose(4, 0, 2, 3, 1))
    sh["wdown"] = np.ascontiguousarray(inp["w_down"].reshape(DEPTH, 2, 11, 128, 1024).transpose(0, 1, 3, 2, 4))
    sh["consts"] = _consts()
    return {k: np.ascontiguousarray(v, dtype=f) for k, v in sh.items()}


def _prep_core(inp, core):
    b0 = 2 * core
    xs = []
    for b in (b0, b0 + 1):
        xcat = np.concatenate([inp["ctx"][b], inp["x"][b]], axis=0)
        xs.append(xcat.T.reshape(8, 128, NT).transpose(1, 0, 2))
    xin = np.ascontiguousarray(np.stack(xs), dtype=np.float32)
    cv = np.stack([inp["c"][b0], inp["c"][b0 + 1], inp["c_ctx"]], axis=1)
    cvec = np.ascontiguousarray(cv.reshape(8, 128, 3).transpose(1, 0, 2), dtype=np.float32)
    return {"xin": xin, "cvec": cvec}


_NC_CACHE = {}


def kernel(**inputs):
    inp = {k: np.asarray(v) for k, v in inputs.items()}
    shared = _prep_shared(inp)
    if "nc" not in _NC_CACHE:
        _NC_CACHE["nc"] = build()
    nc = _NC_CACHE["nc"]
    in_maps = []
    for core in range(8):
        m = dict(shared)
        m.update(_prep_core(inp, core))
        in_maps.append(m)
    res = run_bass_kernel_spmd(nc, in_maps, core_ids=list(range(8)))
    out = np.empty((16, S_LAT, D), np.float32)
    for core in range(8):
        y = np.asarray(res.results[core]["yout"])
        for i in range(2):
            out[2 * core + i] = y[i].transpose(1, 0, 2).reshape(D, S_LAT).T
    return out
```

```python
import contextlib
import numpy as np
import concourse.bass as bass
import concourse.mybir as mybir
from concourse.bass_utils import run_bass_kernel_spmd

F32 = mybir.dt.float32
BF16 = mybir.dt.bfloat16
AF = mybir.ActivationFunctionType
ALU = mybir.AluOpType
AX = mybir.AxisListType

D = 1024
S_LAT = 2048
T_CTX = 256
NT = 2304
DEPTH = 4
DFF = 2816
HD = 64
EPS = 1e-6
NTM = 1424
EW = 66
HBW = 2308
TT = [(0, 256), (256, 512), (768, 512), (1280, 512), (1792, 512)]
LN8 = float(np.log(0.125))


def hcol(t):
    return t + 1 if t < 256 else t + 3


class Tok:
    __slots__ = ("eng", "seq", "clk", "needed", "sem", "val")

    def __init__(self, eng, seq, clk):
        self.eng = eng
        self.seq = seq
        self.clk = clk
        self.needed = False
        self.sem = None
        self.val = 0


class Res:
    __slots__ = ("w", "r", "excl")

    def __init__(self, excl=False):
        self.w = None
        self.r = {}
        self.excl = excl


class Sched:
    ENGS = ("pe", "act", "dve", "pool", "sp")

    def __init__(self, n_dma_sems=8):
        self.q = {e: [] for e in self.ENGS}
        self.clock = {e: {} for e in self.ENGS}
        self.seq = {e: 0 for e in self.ENGS}
        self.toks = {e: [] for e in self.ENGS}
        self.n_dma = n_dma_sems
        self.dma_rr = {e: 0 for e in self.ENGS}
        self.dma_last = {}

    def _merge(self, clk, t):
        for k, v in t.clk.items():
            if clk.get(k, 0) < v:
                clk[k] = v
        if clk.get(t.eng, 0) < t.seq:
            clk[t.eng] = t.seq

    def _deps(self, eng, reads, writes):
        clk = self.clock[eng]
        waits = []
        cand = []
        for r in reads:
            if r.w is not None:
                cand.append(r.w)
        for w in writes:
            if w.w is not None:
                cand.append(w.w)
            cand.extend(w.r.values())
        for t in cand:
            if eng == "pe" and t.eng == "pe":
                continue
            if clk.get(t.eng, 0) < t.seq:
                waits.append(t)
                t.needed = True
                self._merge(clk, t)
        return waits

    def _mark(self, tok, reads, writes):
        for r in reads:
            r.r[tok.eng] = tok
        for w in writes:
            w.w = tok
            w.r = {}

    def op(self, eng, fn, reads=(), writes=()):
        ex = [r for r in reads if r.excl]
        if ex:
            writes = list(writes) + ex
        waits = self._deps(eng, reads, writes)
        self.seq[eng] += 1
        tok = Tok(eng, self.seq[eng], dict(self.clock[eng]))
        self.toks[eng].append(tok)
        self.q[eng].append((waits, fn, tok, False))
        self._mark(tok, reads, writes)
        return tok

    def dma(self, eng, fn, reads=(), writes=()):
        waits = self._deps(eng, reads, writes)
        j = self.dma_rr[eng]
        self.dma_rr[eng] = (j + 1) % self.n_dma
        key = ("dma", eng, j)
        last = self.dma_last.get(key)
        clk = self.clock[eng]
        if last is not None and clk.get(key, 0) < last.seq:
            waits.append(last)
            self._merge(clk, last)
        seq = (last.seq if last is not None else 0) + 1
        tok = Tok(key, seq, dict(clk))
        tok.needed = True
        self.dma_last[key] = tok
        self.q[eng].append((waits, fn, tok, True))
        self._mark(tok, reads, writes)
        return tok

    def barrier(self):
        lasts = [self.toks[e][-1] for e in self.ENGS if self.toks[e]]
        lasts += list(self.dma_last.values())
        for e in self.ENGS:
            clk = self.clock[e]
            waits = []
            for t in lasts:
                if e == "pe" and t.eng == "pe":
                    continue
                if clk.get(t.eng, 0) < t.seq:
                    waits.append(t)
                    t.needed = True
                    self._merge(clk, t)
            if waits:
                self.q[e].append((waits, None, None, False))

    def emit(self, nc, stack):
        esem = {}
        for e in self.ENGS:
            esem[e] = stack.enter_context(nc.semaphore("s_" + e))
            c = 0
            for t in self.toks[e]:
                if t.needed:
                    c += 1
                t.val = c
                t.sem = esem[e]
        dsem = {}
        for key in self.dma_last:
            dsem[key] = stack.enter_context(nc.semaphore("d_%s_%d" % (key[1], key[2])))
        block = stack.enter_context(nc.Block())
        q = self.q

        def run(e, engobj):
            for (waits, fn, tok, is_dma) in q[e]:
                for t in waits:
                    if isinstance(t.eng, tuple):
                        engobj.wait_ge(dsem[t.eng], 16 * t.seq)
                    else:
                        engobj.wait_ge(t.sem, t.val)
                if fn is None:
                    continue
                ins = fn(engobj)
                if is_dma:
                    ins.then_inc(dsem[tok.eng], 16)
                elif tok.needed:
                    ins.then_inc(tok.sem, 1)

        @block.tensor
        def _(e):
            run("pe", e)

        @block.scalar
        def _(e):
            run("act", e)

        @block.vector
        def _(e):
            run("dve", e)

        @block.gpsimd
        def _(e):
            run("pool", e)

        @block.sync
        def _(e):
            run("sp", e)


def build(nlayers=DEPTH, nb=2, dbg=(), stop=99):
    nc = bass.Bass("TRN2", target_bir_lowering=False)
    S = Sched()
    st = contextlib.ExitStack()

    def din(name, shape, dt=F32):
        return nc.dram_tensor(name, list(shape), dt, kind="ExternalInput").ap()

    def dscr(name, shape, dt):
        return nc.dram_tensor(name, list(shape), dt, kind="ExternalOutput").ap()

    xin = din("xin", [2, 128, 8, NT])
    cvec = din("cvec", [128, 8, 3])
    wada = din("wada", [DEPTH, 12, 128, 8, 512])
    bada = din("bada", [128, DEPTH, 48])
    gn12 = din("gn12", [128, DEPTH, 2, 8])
    winfm = din("winfm", [DEPTH, 7, 128, 8, 256])
    binfm = din("binfm", [128, DEPTH, 14])
    wintm = din("wintm", [DEPTH, 128, 8, NTM])
    bintm = din("bintm", [1, DEPTH, NTM])
    gqk = din("gqk", [128, DEPTH, 4])
    rope = din("rope", [128, 2, NT])
    gmv = din("gmv", [DEPTH, 2, 256])
    wsp = din("wsp", [DEPTH, 128, 4, 128])
    bsp = din("bsp", [128, DEPTH, 4])
    wout = din("wout", [DEPTH, 128, 8, 1024])
    wup = din("wup", [DEPTH, 2, 128, 8, 2816])
    convp = din("convp", [128, DEPTH, 2, 22, 4])
    wdown = din("wdown", [DEPTH, 2, 128, 11, 1024])
    consts = din("consts", [128, 8, 128])
    yout = nc.dram_tensor("yout", [2, 128, 8, S_LAT], F32, kind="ExternalOutput").ap()
    if dbg:
        xd = nc.dram_tensor("xd", [2, 128, 8, NT], F32, kind="ExternalOutput").ap()
        xm_dbg = nc.dram_tensor("xm_dbg", [2, 128, 8, NT], F32, kind="ExternalOutput").ap()
        mix_dbg = nc.dram_tensor("mix_dbg", [2, 128, 8, HBW], BF16, kind="ExternalOutput").ap()
    else:
        xd = dscr("xd", [2, 128, 8, NT], F32)
    vmad = dscr("vmad", [18, 128, (4 * EW)], BF16)
    omsd = dscr("omsd", [18, 128, 256], BF16)
    cmd = dscr("cmd", [18, 128, 256], BF16)
    dbg_out = {}

    def sb(name, shape, dt=F32):
        return st.enter_context(nc.sbuf_tensor(name, list(shape), dt))

    def psb(name):
        return st.enter_context(nc.psum_tensor(name, [128, 512], F32))

    def mm(out, lhsT, rhs, start, stop, rd, wr):
        S.op("pe", lambda e, o=out, l=lhsT, r=rhs, a=start, b=stop: e.matmul(o, lhsT=l, rhs=r, start=a, stop=b), rd, wr)

    def act(out, in_, func, rd, wr, bias=None, scale=None):
        kw = {}
        if bias is not None:
            kw["bias"] = bias
        if scale is not None:
            kw["scale"] = scale
        S.op("act", lambda e, o=out, i=in_, f=func, k=kw: e.activation(out=o, in_=i, func=f, **k), rd, wr)

    def tt(eng, out, in0, in1, op, rd, wr):
        S.op(eng, lambda e, o=out, a=in0, b=in1, p=op: e.tensor_tensor(out=o, in0=a, in1=b, op=p), rd, wr)

    def ts(eng, out, in0, s1, op0, rd, wr, s2=None, op1=None):
        if op1 is None:
            S.op(eng, lambda e, o=out, a=in0, x=s1, p=op0: e.tensor_scalar(out=o, in0=a, scalar1=x, scalar2=None, op0=p), rd, wr)
        else:
            S.op(eng, lambda e, o=out, a=in0, x=s1, y=s2, p=op0, q=op1: e.tensor_scalar(out=o, in0=a, scalar1=x, scalar2=y, op0=p, op1=q), rd, wr)

    def stt(eng, out, in0, scalar, in1, op0, op1, rd, wr, tmp=None):
        if eng == "pool":
            t_ = out if tmp is None else tmp
            S.op(eng, lambda e, o=t_, a=in0, x=scalar, p=op0: e.tensor_scalar(out=o, in0=a, scalar1=x, scalar2=None, op0=p), rd, wr)
            S.op(eng, lambda e, o=out, a=t_, b=in1, p=op1: e.tensor_tensor(out=o, in0=a, in1=b, op=p), rd, wr)
            return
        S.op(eng, lambda e, o=out, a=in0, s=scalar, b=in1, p=op0, q=op1: e.scalar_tensor_tensor(out=o, in0=a, scalar=s, in1=b, op0=p, op1=q), rd, wr)

    def cp(eng, out, in_, rd, wr):
        if eng == "act":
            S.op(eng, lambda e, o=out, i=in_: e.activation(out=o, in_=i, func=AF.Identity), rd, wr)
        else:
            S.op(eng, lambda e, o=out, i=in_: e.tensor_copy(out=o, in_=i), rd, wr)

    def recip(out, in_, rd, wr):
        S.op("dve", lambda e, o=out, i=in_: e.reciprocal(out=o, in_=i), rd, wr)

    def mset(eng, ap, val, wr):
        S.op(eng, lambda e, a=ap, v=val: e.memset(a, v), (), wr)

    def dma(eng, out, in_, rd, wr):
        return S.dma(eng, lambda e, o=out, i=in_: e.dma_start(out=o, in_=i), rd, wr)

    CONST = sb("CONST", [128, 8, 128]); rCONST = Res()
    CONSTB = sb("CONSTB", [128, 2, 128], BF16); rCONSTB = Res()
    ONE1 = sb("ONE1", [1, 128]); rONE1 = Res()
    CV = sb("CV", [128, 8, 3]); SCb = sb("SCb", [128, 8, 3], BF16); rCV = Res(); rSCb = Res()
    MOD = sb("MOD", [128, DEPTH, 48, 3]); rMOD = Res()
    BADA = sb("BADA", [128, DEPTH, 48]); rBADA = Res()
    GN = sb("GN", [128, DEPTH, 2, 8]); rGN = Res()
    A12 = sb("A12", [128, DEPTH, 2, 8, 3]); rA12 = Res()
    BFM = sb("BFM", [128, DEPTH, 14]); rBFM = Res()
    BTM = sb("BTM", [1, NTM]); rBTM = Res()
    GQK = sb("GQK", [128, DEPTH, 4]); rGQK = Res()
    BSP = sb("BSP", [128, DEPTH, 4]); rBSP = Res()
    CONVP = sb("CONVP", [128, 2, 22, 4]); rCONVP = Res()
    ROPE = sb("ROPE", [128, 2, NT], BF16); rROPE = Res()
    GMV = sb("GMV", [128, 2, 256]); rGMV = Res()
    WSPb = sb("WSPb", [128, 4, 128], BF16); rWSP = Res()
    HB = sb("HB", [128, 8, HBW], BF16)
    rHB = [[Res() for _ in range(5)] for _ in range(8)]
    XT = [sb("XT%d" % i, [128, 8, 512]) for i in range(2)]; rXT = [Res(), Res()]
    rSQ = Res()
    SD = sb("SD", [128, 512]); rSD = Res()
    RS = sb("RS", [128, 512]); rRS = Res()
    WP = [sb("WP%d" % i, [128, 8, 512], BF16) for i in range(2)]; rWP = [Res(), Res()]
    ARENA = sb("ARENA", [128, 46464], BF16)
    PS = [psb("PS%d" % i) for i in range(8)]
    rPS = [Res(excl=True) for _ in range(8)]

    IDENT = CONST[:, 0, :]
    TRIF = CONST[:, 1, :]
    TRIB = CONST[:, 2, :]
    NEG1 = CONST[:, 3, :]
    MASKF = CONST[:, 4, :]
    MASKB = CONST[:, 5, :]
    ONESb = CONSTB[:, 0, :]
    BLKb = CONSTB[:, 1, :]

    class Arena:
        def __init__(self):
            self.off = 0

        def reset(self):
            self.off = 0

        def get(self, shape, dt):
            n = int(np.prod(shape[1:]))
            if dt == F32:
                n2 = 2 * n
            else:
                n2 = n
            o = self.off + (self.off % 2)
            self.off = o + n2 + (n2 % 2)
            assert self.off <= 46464, ("arena overflow", self.off)
            v = ARENA[:, o:o + n2]
            if dt == F32:
                v = v.bitcast(F32)
            v = v[0:shape[0]]
            if len(shape) == 3:
                v = v.rearrange("p (a b) -> p a b", a=shape[1])
            elif len(shape) == 4:
                v = v.rearrange("p (a b c) -> p a b c", a=shape[1], b=shape[2])
            return v

    AR = Arena()
    wp_i = [0]

    def load_w(src_ap, width, kc=8):
        i = wp_i[0] % 2
        wp_i[0] += 1
        dma("pool", WP[i][:, 0:kc, 0:width], src_ap, (), [rWP[i]])
        return WP[i], rWP[i]

    ps_i = [0]

    def psrot(n=4):
        i = ps_i[0] % n
        ps_i[0] += 1
        return PS[i], rPS[i]

    dma("sp", CONST[:], consts[:, :, :], (), [rCONST])
    cp("dve", CONSTB[:], CONST[:, 6:8, :], [rCONST], [rCONSTB])
    mset("dve", ONE1[:], 1.0, [rONE1])
    dma("sp", CV[:], cvec[:, :, :], (), [rCV])
    dma("sp", BADA[:], bada[:, :, :], (), [rBADA])
    dma("sp", GN[:], gn12[:, :, :, :], (), [rGN])
    dma("sp", BFM[:], binfm[:, :, :], (), [rBFM])
    dma("sp", GQK[:], gqk[:, :, :], (), [rGQK])
    dma("sp", BSP[:], bsp[:, :, :], (), [rBSP])
    dma("pool", ROPE[:], rope[:, :, :], (), [rROPE])
    act(CV[:], CV[:], AF.Silu, [rCV], [rCV])
    cp("dve", SCb[:], CV[:], [rCV], [rSCb])
    for k in range(8):
        mset("pool", HB[:, k, 0:1], 0.0, [rHB[k][0]])
        mset("pool", HB[:, k, 257:259], 0.0, [rHB[k][0]])
        mset("pool", HB[:, k, 2307:2308], 0.0, [rHB[k][4]])

    for l in range(nlayers):
        pm = PS[0][:, 0:144]
        for pc in range(12):
            w, rw = load_w(wada[l, pc], 512)
            for j in range(4):
                jj = pc * 4 + j
                for k in range(8):
                    mm(pm[:, jj * 3:jj * 3 + 3], w[:, k, j * 128:(j + 1) * 128], SCb[:, k, :], k == 0, k == 7,
                       [rw, rSCb], [rPS[0]])
        tt("dve", MOD[:, l, :, :], pm.rearrange("p (a b) -> p a b", b=3),
           BADA[:, l, :].unsqueeze(2).to_broadcast([128, 48, 3]), ALU.add, [rPS[0], rBADA], [rMOD])
        for i in range(2):
            ts("dve", A12[:, l, i, :, :], MOD[:, l, 8 + 24 * i:16 + 24 * i, :], 1.0, ALU.add, [rMOD], [rA12])
            tt("dve", A12[:, l, i, :, :], A12[:, l, i, :, :],
               GN[:, l, i, :].unsqueeze(2).to_broadcast([128, 8, 3]), ALU.mult, [rA12, rGN], [rA12])

    def norm_tile(l, which, m, n, xt, rxt, SQ, rsq=None):
        rSQ_ = rsq if rsq is not None else rSQ
        t0, w = TT[n]
        c0 = hcol(t0)
        act(SQ[:, :, 0:w], xt[:, :, 0:w], AF.Square, [rxt], [rSQ_])
        pss, rpss = PS[4], rPS[4]
        for k in range(8):
            mm(pss[:, 0:w], ONESb, SQ[:, k, 0:w], k == 0, k == 7, [rSQ_, rCONSTB], [rpss])
        act(SD[:, 0:w], pss[:, 0:w], AF.Sqrt, [rpss], [rSD], bias=EPS, scale=1.0 / D)
        recip(RS[:, 0:w], SD[:, 0:w], [rSD], [rRS])
        tt("dve", xt[:, :, 0:w], xt[:, :, 0:w], RS[:, 0:w].unsqueeze(1).to_broadcast([128, 8, w]), ALU.mult,
           [rxt, rRS], [rxt])
        sh = 0 if which == 0 else 24
        for k in range(8):
            act(HB[:, k, c0:c0 + w], xt[:, k, 0:w], AF.Identity, [rxt, rA12, rMOD], [rHB[k][n]],
                bias=MOD[:, l, sh + k, m:m + 1], scale=A12[:, l, which, k, m:m + 1])

    xt_i = [0]

    def xt_next():
        i = xt_i[0] % 2
        xt_i[0] += 1
        return XT[i], rXT[i]

    rXD = [[Res() for _ in range(5)] for _ in range(2)]

    def layer(b, l):
        last = (l == DEPTH - 1)
        xsrc = xin if l == 0 else xd
        ctx_m = 2
        dma("sp", GMV[:, 0, :], gmv[l, 0:1, :].partition_broadcast(128), (), [rGMV])
        dma("sp", GMV[:, 1, :], gmv[l, 1:2, :].partition_broadcast(128), (), [rGMV])
        dma("pool", WSPb[:], wsp[l], (), [rWSP])
        dma("sp", BTM[:], bintm[0:1, l, :], (), [rBTM])
        dma("sp", CONVP[:], convp[:, l, :, :, :], (), [rCONVP])
        S.barrier()
        AR.reset()
        QK = AR.get([128, 5, NT], BF16); rQK = [[Res() for _ in range(5)] for _ in range(5)]
        QM = AR.get([128, 4, NT], BF16); rQM = [[Res() for _ in range(5)] for _ in range(4)]
        VA = AR.get([128, 18, 256], BF16); rVA = [Res() for _ in range(18)]
        GT = AR.get([128, 18, 16], F32); rGT = [Res() for _ in range(18)]
        DC = AR.get([128, 36, (2 * EW)], BF16); rDC = [Res() for _ in range(36)]
        CBF = AR.get([128, 36, (2 * EW)], BF16); rCBF = [Res() for _ in range(36)]
        SCAL = AR.get([128, 18, 32], F32); rSCAL = [Res() for _ in range(18)]
        SM = AR.get([128, 64], F32); rSM = Res()
        CST = AR.get([128, 2, (2 * EW)], F32); rCST = Res()
        mark_tmp = AR.off
        norm_done = [0]

        def ensure_norm(upto):
            while norm_done[0] <= upto and norm_done[0] < 5:
                n = norm_done[0]
                norm_done[0] += 1
                t0, w = TT[n]
                m = ctx_m if n == 0 else b
                xt, rxt = xt_next()
                dma("sp", xt[:, :, 0:w], xsrc[b, :, :, t0:t0 + w], [rXD[b][n]], [rxt])
                norm_tile(l, 0, m, n, xt, rxt, WP[1], rWP[1])

        ensure_norm(1)
        wp_i[0] += wp_i[0] % 2
        F1 = AR.get([128, 512], F32); rF1 = Res()
        F2 = AR.get([128, 512], F32); rF2 = Res()
        SQH = AR.get([128, 512], BF16); rSQH = Res()
        RSQ = AR.get([128, 512], F32); rRSQ = Res()
        T1 = AR.get([128, 512], F32); rT1 = Res()
        T2 = AR.get([128, 512], F32); rT2 = Res()
        UG = AR.get([128, 512], F32); rUG = Res()
        VCN = AR.get([128, 256], BF16); rVCN = Res()
        KH = AR.get([128, 2, 256], BF16); rKH = Res()
        VMAt2 = AR.get([128, (4 * EW)], BF16); VMAt = VMAt2.rearrange("p (h d) -> p h d", h=4); rVMAt = Res()
        OMSt = AR.get([128, 256], BF16); rOMSt = Res()
        CMt = AR.get([128, 256], BF16); rCMt = Res()

        if stop <= 1:
            return
        for i in range(18):
            mset("pool", VA[:, i, :].rearrange("p (a b) -> p a b", a=2)[:, :, 64:128], 1.0, [rVA[i]])

        def fm_matmul(w, rw, cc, n, pst, rpst):
            t0, wd = TT[n]
            c0 = hcol(t0)
            for k in range(8):
                mm(pst[:, 0:wd], w[:, k, cc * 128:(cc + 1) * 128], HB[:, k, c0:c0 + wd], k == 0, k == 7,
                   [rw, rHB[k][n]], [rpst])

        for pi in range(7):
            w, rw = load_w(winfm[l, pi], 256)
            if pi < 5:
                gcol = 0 if pi < 4 else 2
                for n in range(5):
                    ensure_norm(n + 2)
                    t0, wd = TT[n]
                    pa, rpa = psrot()
                    pb, rpb = psrot()
                    fm_matmul(w, rw, 0, n, pa, rpa)
                    fm_matmul(w, rw, 1, n, pb, rpb)
                    ba = BFM[:, l, 2 * pi:2 * pi + 1]
                    bb = BFM[:, l, 2 * pi + 1:2 * pi + 2]
                    act(SQH[:, 0:wd], pa[:, 0:wd], AF.Square, [rpa, rBFM], [rSQH], bias=ba)
                    act(F1[:, 0:wd], pa[:, 0:wd], AF.Identity, [rpa, rBFM], [rF1], bias=ba)
                    act(F2[:, 0:wd], pb[:, 0:wd], AF.Identity, [rpb, rBFM], [rF2], bias=bb)
                    pss, rpss = PS[4], rPS[4]
                    mm(pss[:, 0:wd], BLKb, SQH[:, 0:wd], True, True, [rSQH, rCONSTB], [rpss])
                    act(RSQ[:, 0:wd], pss[:, 0:wd], AF.Sqrt, [rpss], [rRSQ], bias=EPS, scale=1.0 / HD)
                    recip(RSQ[:, 0:wd], RSQ[:, 0:wd], [rRSQ], [rRSQ])
                    stt("dve", T1[:, 0:wd], F1[:, 0:wd], GQK[:, l, gcol:gcol + 1], ROPE[:, 0, t0:t0 + wd],
                        ALU.mult, ALU.mult, [rF1, rGQK, rROPE], [rT1])
                    stt("dve", T2[:, 0:wd], F2[:, 0:wd], GQK[:, l, gcol + 1:gcol + 2], ROPE[:, 1, t0:t0 + wd],
                        ALU.mult, ALU.mult, [rF2, rGQK, rROPE], [rT2])
                    tt("dve", T1[:, 0:wd], T1[:, 0:wd], T2[:, 0:wd], ALU.add, [rT1, rT2], [rT1])
                    tt("dve", QK[:, pi, t0:t0 + wd], T1[:, 0:wd], RSQ[:, 0:wd], ALU.mult, [rT1, rRSQ], [rQK[pi][n]])
            else:
                for cc in range(2):
                    ci = (pi - 5) * 2 + cc
                    for n in range(5):
                        t0, wd = TT[n]
                        pa, rpa = psrot()
                        fm_matmul(w, rw, cc, n, pa, rpa)
                        act(QM[:, ci, t0:t0 + wd], pa[:, 0:wd], AF.Identity, [rpa, rBFM], [rQM[ci][n]],
                            bias=BFM[:, l, 10 + ci:11 + ci])

        if stop <= 2:
            return
        def tm_matmul(w, rw, off, wd, i, pst, rpst):
            t0 = i * 128
            c0 = hcol(t0)
            n = 0 if i < 2 else 1 + (i - 2) // 4
            for k in range(8):
                mm(pst[:, 0:wd], HB[:, k, c0:c0 + 128], w[:, k, 0:wd], k == 0, False, [rw, rHB[k][n]], [rpst])
            mm(pst[:, 0:wd], ONE1[0:1, :], BTM[0:1, off:off + wd], False, True, [rONE1, rBTM], [rpst])

        w, rw = load_w(wintm[l, :, :, 0:16], 16)
        for i in range(18):
            pa, rpa = psrot()
            tm_matmul(w, rw, 0, 16, i, pa, rpa)
            cp("dve", GT[:, i, :], pa[:, 0:16], [rpa], [rGT[i]])
            gv = GT[:, i, :].rearrange("p (d t h) -> p d t h", d=2, t=2)
            fpre = gv[:, :, 1, :]
            ipre = gv[:, :, 0, :]
            spl = SM[:, 0:8].rearrange("p (d h) -> p d h", d=2)
            act(spl, fpre, AF.Exp, [rGT[i]], [rSM], scale=-1.0)
            act(spl, spl, AF.Ln, [rSM], [rSM], bias=1.0)
            pg, rpg = PS[5], rPS[5]
            mm(pg[:, 0:4], TRIF, SM[:, 0:4], True, True, [rSM, rCONST], [rpg])
            mm(pg[:, 4:8], TRIB, SM[:, 4:8], True, True, [rSM, rCONST], [rpg])
            mm(pg[:, 8:16], NEG1, SM[:, 0:8], True, True, [rSM, rCONST], [rpg])
            a_ = SM[:, 8:16]
            tt("dve", a_.rearrange("p (d h) -> p d h", d=2), ipre, pg[:, 0:8].rearrange("p (d h) -> p d h", d=2),
               ALU.subtract, [rGT[i], rpg], [rSM])
            act(SCAL[:, i, 0:8], a_, AF.Exp, [rSM], [rSCAL[i]], bias=LN8)
            tt("dve", SM[:, 16:24], a_, pg[:, 8:16], ALU.add, [rSM, rpg], [rSM])
            act(SCAL[:, i, 8:16], SM[:, 16:24], AF.Exp, [rSM], [rSCAL[i]], bias=LN8)
            act(SCAL[:, i, 16:24], pg[:, 0:8], AF.Exp, [rpg], [rSCAL[i]], scale=-1.0)
            act(SCAL[:, i, 24:32], pg[:, 8:16], AF.Exp, [rpg], [rSCAL[i]])

        if stop <= 3:
            return
        w, rw = load_w(wintm[l, :, :, 16:528], 512)
        for i in range(18):
            pa, rpa = psrot()
            tm_matmul(w, rw, 16, 512, i, pa, rpa)
            if stop <= 3.1:
                continue
            mset("pool", VMAt2, 0.0, [rVMAt])
            mset("pool", VMAt[:, :, 64:65], 1.0, [rVMAt])
            cp("act", VMAt[:, :, 0:64], pa[:, 0:256].rearrange("p (h d) -> p h d", h=4), [rpa], [rVMAt])
            dma("sp", vmad[i], VMAt2, [rVMAt], [])
            if stop <= 3.2:
                continue
            for d_ in range(2):
                tt("dve", KH[:, d_, :].rearrange("p (h d) -> p h d", h=4),
                   pa[:, 256:512].rearrange("p (h d) -> p h d", h=4),
                   SCAL[:, i, 8 + 4 * d_:12 + 4 * d_].unsqueeze(2).to_broadcast([128, 4, 64]), ALU.mult,
                   [rpa, rSCAL[i], rVMAt], [rKH])
            if stop <= 3.3:
                continue
            for d_ in range(2):
                for pr in range(2):
                    mm(PS[6 + d_][:, pr * (2 * EW):(pr + 1) * (2 * EW)], KH[:, d_, pr * 128:(pr + 1) * 128],
                       VMAt[:, 2 * pr:2 * pr + 2, :], True, True, [rKH, rVMAt], [rPS[6 + d_]])
            if stop <= 3.4:
                continue
            for d_ in range(2):
                for pr in range(2):
                    for hh in range(2):
                        cp("act", DC[64 * hh:64 * hh + 64, d_ * 18 + i, pr * EW:(pr + 1) * EW],
                           PS[6 + d_][64 * hh:64 * hh + 64, pr * (2 * EW) + hh * EW:pr * (2 * EW) + hh * EW + EW],
                           [rPS[6 + d_]], [rDC[d_ * 18 + i]])
        if stop <= 3.5:
            return

        mset("dve", CST[:], 0.0, [rCST])
        orders = [list(range(18)), [1, 0] + list(range(17, 1, -1))]
        for d_ in range(2):
            eng = "dve"
            for i in orders[d_]:
                cp(eng, CBF[:, d_ * 18 + i, :], CST[:, d_, :], [rCST], [rCBF[d_ * 18 + i]])
                for pr in range(2):
                    for hh in range(2):
                        col = 24 + d_ * 4 + 2 * pr + hh
                        stt(eng, CST[64 * hh:64 * hh + 64, d_, pr * EW:(pr + 1) * EW],
                            CST[64 * hh:64 * hh + 64, d_, pr * EW:(pr + 1) * EW],
                            SCAL[64 * hh:64 * hh + 64, i, col:col + 1],
                            DC[64 * hh:64 * hh + 64, d_ * 18 + i, pr * EW:(pr + 1) * EW],
                            ALU.mult, ALU.add, [rCST, rSCAL[i], rDC[d_ * 18 + i]], [rCST])

        if stop <= 4:
            return
        w, rw = load_w(wintm[l, :, :, 528:912], 384)
        for i in range(18):
            pa, rpa = psrot()
            tm_matmul(w, rw, 528, 384, i, pa, rpa)
            cp("act", VA[:, i, :].rearrange("p (a b) -> p a b", a=2)[:, :, 0:64],
               pa[:, 0:128].rearrange("p (a b) -> p a b", a=2), [rpa], [rVA[i]])
            act(OMSt[:], pa[:, 128:384], AF.Sigmoid, [rpa], [rOMSt])
            dma("sp", omsd[i], OMSt[:], [rOMSt], [])

        w, rw = load_w(wintm[l, :, :, 912:1424], 512)
        for i in range(18):
            pa, rpa = psrot()
            tm_matmul(w, rw, 912, 512, i, pa, rpa)
            act(UG[:], pa[:, 0:512], AF.Gelu_apprx_tanh, [rpa], [rUG])
            ugv = UG[:, 256:512].rearrange("p (g d) -> p g d", g=4)
            tt("dve", T1[:, 0:256], UG[:, 256:512], UG[:, 256:512], ALU.mult, [rUG], [rT1])
            S.op("dve", lambda e, o=SM[:, 32:36], a=T1[:, 0:256].rearrange("p (g d) -> p g d", g=4):
                 e.tensor_reduce(out=o, in_=a, axis=AX.X, op=ALU.add), [rT1], [rSM])
            act(SM[:, 32:36], SM[:, 32:36], AF.Sqrt, [rSM], [rSM], bias=EPS, scale=1.0 / 64)
            recip(SM[:, 36:40], SM[:, 32:36], [rSM], [rSM])
            tt("dve", T1[:, 0:256].rearrange("p (g d) -> p g d", g=4), ugv,
               SM[:, 36:40].unsqueeze(2).to_broadcast([128, 4, 64]), ALU.mult, [rUG, rSM], [rT1])
            tt("dve", VCN[:], T1[:, 0:256], GMV[:, 1, :], ALU.mult, [rT1, rGMV], [rVCN])
            pz, rpz = PS[5], rPS[5]
            for g in range(4):
                mm(pz[:, g * 64:(g + 1) * 64], WSPb[:, g, :], VCN[:, g * 64:(g + 1) * 64], True, True,
                   [rWSP, rVCN], [rpz])
            tt("dve", T2[:, 0:256].rearrange("p (g d) -> p g d", g=4), pz[:, 0:256].rearrange("p (g d) -> p g d", g=4),
               BSP[:, l, :].unsqueeze(2).to_broadcast([128, 4, 64]), ALU.add, [rpz, rBSP], [rT2])
            tt("dve", CMt[:], T2[:, 0:256], UG[:, 0:256], ALU.mult, [rT2, rUG], [rCMt])
            dma("sp", cmd[i], CMt[:], [rCMt], [])

        if stop <= 5:
            return
        S.barrier()
        AR.off = mark_tmp
        T1 = AR.get([128, 512], F32); rT1 = Res()
        T2 = AR.get([128, 512], F32); rT2 = Res()
        AT = AR.get([128, 2, 4, 128], BF16); rAT = Res()
        PT = [AR.get([128, 512], BF16) for _ in range(3)]; rPT = [Res(), Res(), Res()]
        OTM = AR.get([128, 512], F32); rOTM = Res()
        VMAt2 = AR.get([128, (4 * EW)], BF16); VMAt = VMAt2.rearrange("p (h d) -> p h d", h=4); rVMAt = Res()
        OMSt = AR.get([128, 256], BF16); rOMSt = Res()
        CMt = AR.get([128, 256], BF16); rCMt = Res()
        RD = AR.get([128, 512], F32); rRD = Res()
        HS = AR.get([128, 256], F32); rHS = Res()
        rMIX = [[Res() for _ in range(18)] for _ in range(8)]
        rDEN = [Res(), Res()]; rSS = Res()

        def mixcols(i):
            return hcol(i * 128)

        qblocks = list(range(2, 18)) + ([] if last else [0, 1])
        its = []
        for qb in qblocks:
            ktiles = list(range(18)) if qb >= 2 else [0, 1]
            for j in range(2):
                for kk, kt in enumerate(ktiles):
                    its.append((qb, j, kk, kt, len(ktiles)))

        def a_score(n_):
            qb, j, kk, kt, nkt = its[n_]
            q0 = qb * 128
            nq = 0 if qb < 2 else 1 + (qb - 2) // 4
            nk = 0 if kt < 2 else 1 + (kt - 2) // 4
            pscore, rpscore = psrot()
            mm(pscore[:, 0:512], QK[64 * j:64 * j + 64, 4, kt * 128:(kt + 1) * 128],
               QK[64 * j:64 * j + 64, 0:4, q0:q0 + 128], True, True,
               [rQK[4][nk]] + [rQK[c][nq] for c in range(4)], [rpscore])
            pi_ = n_ % 3
            act(PT[pi_][:], pscore[:, 0:512], AF.Exp, [rpscore], [rPT[pi_]], scale=0.125)

        def a_pv(n_):
            qb, j, kk, kt, nkt = its[n_]
            po, rpo = PS[4 + j], rPS[4 + j]
            pi_ = n_ % 3
            mm(po[:, 0:512], VA[:, kt, j * 128:(j + 1) * 128], PT[pi_][:], kk == 0, kk == nkt - 1,
               [rVA[kt], rPT[pi_]], [rpo])
            if kk == nkt - 1:
                recip(RD[64:128, :], po[64:128, :], [rpo], [rRD])
                c0 = mixcols(qb)
                tt("dve", HB[64 * j:64 * j + 64, 0:4, c0:c0 + 128],
                   po[0:64, 0:512].rearrange("p (c t) -> p c t", c=4),
                   RD[64:128, :].rearrange("p (c t) -> p c t", c=4), ALU.mult, [rpo, rRD],
                   [rMIX[c][qb] for c in range(4)])

        for n_ in range(len(its)):
            a_score(n_)
            if n_ >= 2:
                a_pv(n_ - 2)
        a_pv(len(its) - 2)
        a_pv(len(its) - 1)

        if stop <= 6:
            return
        tiles2 = list(range(18)) if not last else list(range(2, 18))
        for i in tiles2:
            t0 = i * 128
            n = 0 if i < 2 else 1 + (i - 2) // 4
            if True:
                dma("sp", VMAt2, vmad[i], [], [rVMAt])
                dma("sp", OMSt[:], omsd[i], [], [rOMSt])
                dma("sp", CMt[:], cmd[i], [], [rCMt])
            pscs = [psrot(), psrot()]
            for h in (0, 2, 1, 3):
                ci, hh = h // 2, h % 2
                mm(pscs[hh][0][:, ci * 128:(ci + 1) * 128], QM[64 * hh:64 * hh + 64, 2 + ci, t0:t0 + 128],
                   QM[64 * hh:64 * hh + 64, ci, t0:t0 + 128], True, True, [rQM[2 + ci][n], rQM[ci][n]], [pscs[hh][1]])
            for d_ in range(2):
                msk = MASKF if d_ == 0 else MASKB
                for h in range(4):
                    ci, hh = h // 2, h % 2
                    stt("dve", AT[:, d_, h, :], pscs[hh][0][:, ci * 128:(ci + 1) * 128],
                        SCAL[:, i, d_ * 4 + h:d_ * 4 + h + 1], msk, ALU.mult, ALU.mult,
                        [pscs[hh][1], rSCAL[i], rCONST], [rAT])
            if stop <= 6.1:
                continue
            for d_ in range(2):
                pout, rpout = PS[6 + d_], rPS[6 + d_]
                for h in range(4):
                    ci, hh = h // 2, h % 2
                    mm(pout[:, h * EW:(h + 1) * EW], AT[:, d_, h, :], VMAt[:, h, :], True, False,
                       [rAT, rVMAt], [rpout])
                    mm(pout[:, h * EW:(h + 1) * EW], QM[64 * hh:64 * hh + 64, ci, t0:t0 + 128],
                       CBF[64 * hh:64 * hh + 64, d_ * 18 + i, ci * EW:(ci + 1) * EW], False, True,
                       [rQM[ci][n], rCBF[d_ * 18 + i]], [rpout])
            if stop <= 6.2:
                continue
            for d_ in range(2):
                pout, rpout = PS[6 + d_], rPS[6 + d_]
                pv = pout[:, 0:(4 * EW)].rearrange("p (h e) -> p h e", h=4)
                den = SM[:, 40 + 4 * d_:44 + 4 * d_]
                act(den, pv[:, :, 64], AF.Abs, [rpout], [rDEN[d_]])
                tt("dve", den, den, SCAL[:, i, 16 + 4 * d_:20 + 4 * d_], ALU.max, [rDEN[d_], rSCAL[i]], [rDEN[d_]])
                recip(den, den, [rDEN[d_]], [rDEN[d_]])
                dst = HS if d_ == 0 else T1[:, 0:256]
                tt("dve", dst.rearrange("p (h e) -> p h e", h=4), pv[:, :, 0:64],
                   den.unsqueeze(2).to_broadcast([128, 4, 64]), ALU.mult, [rpout, rDEN[d_]], [rHS if d_ == 0 else rT1])
            tt("dve", HS[:], HS[:], T1[:, 0:256], ALU.add, [rHS, rT1], [rHS])
            if stop <= 6.3:
                continue
            tt("dve", T2[:, 0:256], HS[:], HS[:], ALU.mult, [rHS], [rT2])
            S.op("dve", lambda e, o=SM[:, 48:52], a=T2[:, 0:256].rearrange("p (g d) -> p g d", g=4):
                 e.tensor_reduce(out=o, in_=a, axis=AX.X, op=ALU.add), [rT2], [rSS])
            act(SM[:, 48:52], SM[:, 48:52], AF.Sqrt, [rSS], [rSS], bias=EPS, scale=1.0 / 64)
            recip(SM[:, 52:56], SM[:, 48:52], [rSS], [rSS])
            tt("dve", HS[:].rearrange("p (h e) -> p h e", h=4), HS[:].rearrange("p (h e) -> p h e", h=4),
               SM[:, 52:56].unsqueeze(2).to_broadcast([128, 4, 64]), ALU.mult, [rHS, rSS], [rHS])
            tt("dve", HS[:], HS[:], GMV[:, 0, :], ALU.mult, [rHS, rGMV], [rHS])
            tt("dve", OTM[:, 0:256], HS[:], OMSt[:], ALU.mult, [rHS, rOMSt], [rOTM])
            cp("pool", OTM[:, 256:512], CMt[:], [rCMt], [rOTM])
            if stop <= 6.4:
                continue
            ptr, rptr = psrot()
            for c in range(4):
                S.op("pe", lambda e, o=ptr[:, c * 128:(c + 1) * 128], a=OTM[:, c * 128:(c + 1) * 128]:
                     e.transpose(out=o, in_=a, identity=IDENT), [rOTM, rCONST], [rptr])
            c0 = mixcols(i)
            cp("act", HB[:, 4:8, c0:c0 + 128], ptr[:, 0:512].rearrange("p (c t) -> p c t", c=4), [rptr],
               [rMIX[c][i] for c in range(4, 8)])

        if stop <= 7:
            return
        S.barrier()
        if dbg:
            dma("sp", mix_dbg[b], HB[:], [], [])
            S.barrier()
        AR.reset()
        WOUT = AR.get([128, 8, 1024], BF16); rWOUT = [Res(), Res()]
        SQ = AR.get([128, 8, 512], BF16)
        for hf in range(2):
            dma("pool", WOUT[:, :, hf * 512:(hf + 1) * 512], wout[l, :, :, hf * 512:(hf + 1) * 512], (), [rWOUT[hf]])
        tiles3 = list(range(5)) if not last else list(range(1, 5))
        for n in tiles3:
            t0, w = TT[n]
            c0 = hcol(t0)
            m = ctx_m if n == 0 else b
            xt, rxt = xt_next()
            dma("sp", xt[:, :, 0:w], xsrc[b, :, :, t0:t0 + w], [rXD[b][n]], [rxt])
            for mo in range(8):
                pa, rpa = psrot()
                for k in range(8):
                    mm(pa[:, 0:w], WOUT[:, k, mo * 128:(mo + 1) * 128], HB[:, k, c0:c0 + w], k == 0, k == 7,
                       [rWOUT[mo // 4], rHB[k][n]], [rpa])
                stt("dve", xt[:, mo, 0:w], pa[:, 0:w], MOD[:, l, 16 + mo, m:m + 1], xt[:, mo, 0:w],
                    ALU.mult, ALU.add, [rpa, rMOD, rxt], [rxt])
            dma("sp", xd[b, :, :, t0:t0 + w], xt[:, :, 0:w], [rxt], [rXD[b][n]])
            if dbg:
                dma("sp", xm_dbg[b, :, :, t0:t0 + w], xt[:, :, 0:w], [rxt], [])
            norm_tile(l, 1, m, n, xt, rxt, SQ)

        if stop <= 8:
            return
        S.barrier()
        AR.reset()
        WUP = AR.get([128, 8, 2816], BF16); rWUP = [Res() for _ in range(6)]
        WDN = AR.get([128, 11, 1024], BF16); rWDN = [Res(), Res()]
        U = AR.get([128, 11, 512], BF16); rU = [Res() for _ in range(11)]
        TG = [AR.get([128, 512], F32) for _ in range(2)]; rTG = [Res(), Res()]
        TV = [AR.get([128, 512], F32) for _ in range(2)]; rTV = [Res(), Res()]
        SG = [AR.get([128, 512], F32) for _ in range(2)]; rSG = [Res(), Res()]
        wins = []
        if not last:
            wins.append((0, 258, 0, 0))
        for i in range(4):
            wins.append((258 + 510 * i, 512, 256 + 510 * i, None))
        wins.append((258 + 2040, 10, 256 + 2040, None))
        rXW = [Res() for _ in range(len(wins))]
        gi = [0]
        for hf in range(2):
            for pc in range(6):
                wd = 512 if pc < 5 else 256
                dma("pool", WUP[:, :, pc * 512:pc * 512 + wd], wup[l, hf, :, :, pc * 512:pc * 512 + wd], (), [rWUP[pc]])
            for pc in range(2):
                dma("pool", WDN[:, :, pc * 512:(pc + 1) * 512], wdown[l, hf, :, :, pc * 512:(pc + 1) * 512], (), [rWDN[pc]])
            for wi, (c0, w, tok0, _) in enumerate(wins):
                wo = w - 2
                m = ctx_m if tok0 < 256 else b
                hbres = [rHB[k][nn] for k in range(8) for nn in range(5)]
                for g in range(11):
                    pg_, rpg_ = psrot()
                    pv_, rpv_ = psrot()
                    for k in range(8):
                        mm(pg_[:, 0:w], WUP[:, k, g * 128:(g + 1) * 128], HB[:, k, c0:c0 + w], k == 0, k == 7,
                           [rWUP[(g * 128) // 512]] + (hbres if k == 0 else []), [rpg_])
                    for k in range(8):
                        mm(pv_[:, 0:w], WUP[:, k, 1408 + g * 128:1408 + (g + 1) * 128], HB[:, k, c0:c0 + w], k == 0, k == 7,
                           [rWUP[(1408 + g * 128) // 512]], [rpv_])
                    x_ = gi[0] % 2
                    gi[0] += 1
                    cg = CONVP[:, hf, g, :]
                    cv_ = CONVP[:, hf, 11 + g, :]
                    act(TG[x_][:, 0:wo], pg_[:, 1:1 + wo], AF.Identity, [rpg_, rCONVP], [rTG[x_]],
                        bias=cg[:, 3:4], scale=cg[:, 1:2])
                    stt("dve", TG[x_][:, 0:wo], pg_[:, 0:wo], cg[:, 0:1], TG[x_][:, 0:wo], ALU.mult, ALU.add,
                        [rpg_, rTG[x_]], [rTG[x_]])
                    stt("dve", TG[x_][:, 0:wo], pg_[:, 2:2 + wo], cg[:, 2:3], TG[x_][:, 0:wo], ALU.mult, ALU.add,
                        [rpg_, rTG[x_]], [rTG[x_]])
                    act(TV[x_][:, 0:wo], pv_[:, 1:1 + wo], AF.Identity, [rpv_, rCONVP], [rTV[x_]],
                        bias=cv_[:, 3:4], scale=cv_[:, 1:2])
                    stt("dve", TV[x_][:, 0:wo], pv_[:, 0:wo], cv_[:, 0:1], TV[x_][:, 0:wo], ALU.mult, ALU.add,
                        [rpv_, rTV[x_]], [rTV[x_]])
                    stt("dve", TV[x_][:, 0:wo], pv_[:, 2:2 + wo], cv_[:, 2:3], TV[x_][:, 0:wo], ALU.mult, ALU.add,
                        [rpv_, rTV[x_]], [rTV[x_]])
                    act(SG[x_][:, 0:wo], TG[x_][:, 0:wo], AF.Silu, [rTG[x_]], [rSG[x_]])
                    tt("dve", U[:, g, 0:wo], SG[x_][:, 0:wo], TV[x_][:, 0:wo], ALU.mult, [rSG[x_], rTV[x_]], [rU[g]])
                xt, rxt = xt_next()
                dma("sp", xt[:, :, 0:wo], xd[b, :, :, tok0:tok0 + wo], [rXW[wi]], [rxt])
                for mo in range(8):
                    pa, rpa = psrot()
                    for kc in range(11):
                        mm(pa[:, 0:wo], WDN[:, kc, mo * 128:(mo + 1) * 128], U[:, kc, 0:wo], kc == 0, kc == 10,
                           [rWDN[mo // 4], rU[kc]], [rpa])
                    stt("dve", xt[:, mo, 0:wo], pa[:, 0:wo], MOD[:, l, 40 + mo, m:m + 1], xt[:, mo, 0:wo],
                        ALU.mult, ALU.add, [rpa, rMOD, rxt], [rxt])
                if last and hf == 1:
                    tk = dma("sp", yout[b, :, :, tok0 - 256:tok0 - 256 + wo], xt[:, :, 0:wo], [rxt], [rXW[wi]])
                    out_toks.append(tk)
                else:
                    dma("sp", xd[b, :, :, tok0:tok0 + wo], xt[:, :, 0:wo], [rxt], [rXW[wi]])
        S.barrier()

    out_toks = []
    for b in range(nb):
        for l in range(nlayers):
            if stop > 0:
                layer(b, l)
    S.barrier()
    S.emit(nc, st)
    st.close()
    return nc


def _rope_tables():
    n = S_LAT
    grid_w = 64
    nf = HD // 4
    t = np.arange(n)
    row = (t // grid_w).astype(np.float32)
    col = (t % grid_w).astype(np.float32)
    inv = (10000.0 ** (-np.arange(nf, dtype=np.float32) / nf)).astype(np.float32)
    ang = np.concatenate([row[:, None] * inv[None], col[:, None] * inv[None]], axis=-1)
    cos = np.cos(ang).astype(np.float32).reshape(n, 2, nf)
    sin = np.sin(ang).astype(np.float32).reshape(n, 2, nf)
    C = np.ones((HD, NT), np.float32)
    Sg = np.zeros((HD, NT), np.float32)
    for ax in range(2):
        for half in range(2):
            rows = slice(ax * 32 + half * 16, ax * 32 + half * 16 + 16)
            C[rows, T_CTX:] = cos[:, ax, :].T
            Sg[rows, T_CTX:] = (-1.0 if half == 0 else 1.0) * sin[:, ax, :].T
    tab = np.stack([np.concatenate([C, C], 0), np.concatenate([Sg, Sg], 0)], axis=1)
    return np.ascontiguousarray(tab)


def _consts():
    c = np.zeros((128, 8, 128), np.float32)
    r = np.arange(128)[:, None]
    s = np.arange(128)[None, :]
    c[:, 0, :] = (r == s)
    c[:, 1, :] = -1.0 * (r <= s)
    c[:, 2, :] = -1.0 * (r >= s)
    c[:, 3, :] = -1.0
    c[:, 4, :] = (r <= s)
    c[:, 5, :] = (r >= s)
    c[:, 6, :] = 1.0
    c[:, 7, :] = ((r // 64) == (s // 64))
    return c


def _swap_idx():
    d = np.arange(HD)
    ax, half, f = d // 32, (d // 16) % 2, d % 16
    return ax * 32 + (1 - half) * 16 + f


def _prep_shared(inp):
    f = np.float32
    w_ada = inp["w_ada"]; w_in = inp["w_in"]; b_in = inp["b_in"]
    sh = {}
    sh["wada"] = np.ascontiguousarray(w_ada.reshape(DEPTH, 8, 128, 12, 512).transpose(0, 3, 2, 1, 4))
    sh["bada"] = np.ascontiguousarray(inp["b_ada"].reshape(DEPTH, 48, 128).transpose(2, 0, 1))
    gn = np.stack([inp["g_norm1"], inp["g_norm2"]], axis=1)
    sh["gn12"] = np.ascontiguousarray(gn.reshape(DEPTH, 2, 8, 128).transpose(3, 0, 1, 2))
    sw = _swap_idx()
    QA, KA, VAo, QMo, KMo, VMo, OMo, GTo, UCo, VCo = 0, 512, 640, 768, 1024, 1280, 1536, 1792, 1808, 2064
    fm_cols = []
    for c in range(4):
        h0, h1 = c, 4 + c
        q = np.concatenate([QA + h0 * 64 + np.arange(64), QA + h1 * 64 + np.arange(64)])
        qs = np.concatenate([QA + h0 * 64 + sw, QA + h1 * 64 + sw])
        fm_cols += [q, qs]
    k = np.concatenate([KA + np.arange(64), KA + 64 + np.arange(64)])
    ks = np.concatenate([KA + sw, KA + 64 + sw])
    fm_cols += [k, ks]
    fm_cols += [QMo + np.arange(128), QMo + 128 + np.arange(128), KMo + np.arange(128), KMo + 128 + np.arange(128)]
    fm_idx = np.concatenate(fm_cols)
    wfm = w_in[:, :, fm_idx]
    sh["winfm"] = np.ascontiguousarray(wfm.reshape(DEPTH, 8, 128, 7, 256).transpose(0, 3, 2, 1, 4))
    sh["binfm"] = np.ascontiguousarray(b_in[:, fm_idx].reshape(DEPTH, 14, 128).transpose(2, 0, 1))
    tm_idx = np.concatenate([GTo + np.arange(16), VMo + np.arange(256), KMo + np.arange(256), VAo + np.arange(128),
                             OMo + np.arange(256), UCo + np.arange(256), VCo + np.arange(256)])
    wtm = w_in[:, :, tm_idx]
    sh["wintm"] = np.ascontiguousarray(wtm.reshape(DEPTH, 8, 128, NTM).transpose(0, 2, 1, 3))
    sh["bintm"] = np.ascontiguousarray(b_in[:, tm_idx][None])
    gq = inp["g_q"]; gk = inp["g_k"]
    g4 = np.stack([np.tile(gq, (1, 2)), np.tile(gq[:, sw], (1, 2)), np.tile(gk, (1, 2)), np.tile(gk[:, sw], (1, 2))], axis=2)
    sh["gqk"] = np.ascontiguousarray(g4.transpose(1, 0, 2))
    sh["rope"] = _rope_tables()
    sh["gmv"] = np.ascontiguousarray(np.stack([inp["g_mh"], inp["g_v"]], axis=1))
    sh["wsp"] = np.ascontiguousarray(inp["w_sp"].transpose(0, 3, 1, 2))
    sh["bsp"] = np.ascontiguousarray(inp["b_sp"].transpose(2, 0, 1))
    rows = []
    for c in range(4):
        rows += [c * 64 + np.arange(64), (4 + c) * 64 + np.arange(64)]
    rows += [512 + np.arange(512)]
    ridx = np.concatenate(rows)
    wo = inp["w_out"][:, ridx, :]
    sh["wout"] = np.ascontiguousarray(wo.reshape(DEPTH, 8, 128, 1024).transpose(0, 2, 1, 3))
    w_up = inp["w_up"]
    ucols = np.stack([np.concatenate([hf * 1408 + np.arange(1408), DFF + hf * 1408 + np.arange(1408)]) for hf in range(2)])
    wu = w_up[:, :, ucols]
    sh["wup"] = np.ascontiguousarray(wu.reshape(DEPTH, 8, 128, 2, 2816).transpose(0, 3, 2, 1, 4))
    cw = inp["conv_w"]; cb = inp["conv_b"]
    cpar = np.concatenate([cw, cb[:, None, :]], axis=1)
    cpar = cpar[:, :, ucols]
    sh["convp"] = np.ascontiguousarray(cpar.reshape(DEPTH, 4, 2, 22, 128).transpose(4, 0, 2, 3, 1))
    sh["wdown"] = np.ascontiguousarray(inp["w_down"].reshape(DEPTH, 2, 11, 128, 1024).transpose(0, 1, 3, 2, 4))
    sh["consts"] = _consts()
    return {k: np.ascontiguousarray(v, dtype=f) for k, v in sh.items()}


def _prep_core(inp, core):
    b0 = 2 * core
    xs = []
    for b in (b0, b0 + 1):
        xcat = np.concatenate([inp["ctx"][b], inp["x"][b]], axis=0)
        xs.append(xcat.T.reshape(8, 128, NT).transpose(1, 0, 2))
    xin = np.ascontiguousarray(np.stack(xs), dtype=np.float32)
    cv = np.stack([inp["c"][b0], inp["c"][b0 + 1], inp["c_ctx"]], axis=1)
    cvec = np.ascontiguousarray(cv.reshape(8, 128, 3).transpose(1, 0, 2), dtype=np.float32)
    return {"xin": xin, "cvec": cvec}


_NC_CACHE = {}


def kernel(**inputs):
    inp = {k: np.asarray(v) for k, v in inputs.items()}
    shared = _prep_shared(inp)
    if "nc" not in _NC_CACHE:
        _NC_CACHE["nc"] = build()
    nc = _NC_CACHE["nc"]
    in_maps = []
    for core in range(8):
        m = dict(shared)
        m.update(_prep_core(inp, core))
        in_maps.append(m)
    res = run_bass_kernel_spmd(nc, in_maps, core_ids=list(range(8)))
    out = np.empty((16, S_LAT, D), np.float32)
    for core in range(8):
        y = np.asarray(res.results[core]["yout"])
        for i in range(2):
            out[2 * core + i] = y[i].transpose(1, 0, 2).reshape(D, S_LAT).T
    return out
```
